# Optimizing a Trainium2 kernel written in Bass

```python
import jax, jax.numpy as jnp
from jax import lax
import numpy as np

D_MODEL = 1024
BATCH = 2
SEQ = 8192
DEPTH = 1
DEC_BATCH = 32
DEC_SEQ = 1
PAST_LEN = 16384
PAGE_SIZE = 128

HEAD_DIM = 128
ROT_DIM = HEAD_DIM // 4
ROPE_THETA = 500000.0
DIL_GROUPS = ((128, 1), (512, 4), (2048, 16))
HEADS_PER_GROUP = 4
N_Q_A = HEADS_PER_GROUP * len(DIL_GROUPS)
N_KV_A = HEADS_PER_GROUP
WIN_MAX = 2048
BAND_BLK = 128
CONV_CH = 512
CONV_W = 31
N_HEADS_M = 4
N_MEM = 256
N_BRANCH = 3
PEER_HEADS = 8
N_KEYS = 128
N_EXPERTS = N_KEYS * N_KEYS
PEER_TOPK = 16
PK_HALF = 128
PEER_BLK = 128
EPS = 1e-6

QA_W = N_Q_A * HEAD_DIM
KVA_W = N_KV_A * HEAD_DIM
CONV_IN_W = 2 * CONV_CH
QM_W = N_HEADS_M * HEAD_DIM
GATE_W = N_BRANCH * D_MODEL
IN_W = QA_W + 2 * KVA_W + CONV_IN_W + QM_W + GATE_W
SPLIT_IDX = (QA_W, QA_W + KVA_W, QA_W + 2 * KVA_W, QA_W + 2 * KVA_W + CONV_IN_W,
             QA_W + 2 * KVA_W + CONV_IN_W + QM_W)
A_OUT_W = N_KV_A * HEAD_DIM
M_OUT_W = N_HEADS_M * HEAD_DIM

kernel_name = 'hybrid_dilated_conformer_peer_step'


def rms_norm(x, g):
    xf = x.astype(jnp.float32)
    y = xf * lax.rsqrt(jnp.mean(xf * xf, axis=-1, keepdims=True) + EPS)
    return (y * g.astype(jnp.float32)).astype(x.dtype)


def rotary(x, pos):
    half = ROT_DIM // 2
    inv = jnp.float32(ROPE_THETA) ** (-jnp.arange(half, dtype=jnp.float32) / half)
    ang = pos.astype(jnp.float32)[:, None] * inv[None, :]
    cos = jnp.cos(ang)[:, None, :]
    sin = jnp.sin(ang)[:, None, :]
    xr = x[..., :ROT_DIM].astype(jnp.float32)
    x1, x2 = xr[..., :half], xr[..., half:]
    rot = jnp.concatenate([x1 * cos - x2 * sin, x2 * cos + x1 * sin], axis=-1).astype(x.dtype)
    return jnp.concatenate([rot, x[..., ROT_DIM:]], axis=-1)


def softmax_lse(s):
    m = jnp.max(s, axis=-1, keepdims=True)
    p = jnp.exp(s - m)
    den = jnp.sum(p, axis=-1, keepdims=True)
    return p / den, (m + jnp.log(den))[..., 0]


def front(x, pos, g_norm1, w_in, b_gate, qn_a, kn_a, qn_m):
    N, T, _ = x.shape
    h = rms_norm(x, g_norm1)
    p = h @ w_in
    qa, ka, va, u, qm, gl = jnp.split(p, SPLIT_IDX, axis=-1)
    qa = rotary(rms_norm(qa.reshape(N, T, N_Q_A, HEAD_DIM), qn_a), pos)
    ka = rotary(rms_norm(ka.reshape(N, T, N_KV_A, HEAD_DIM), kn_a), pos)
    va = va.reshape(N, T, N_KV_A, HEAD_DIM)
    qm = rms_norm(qm.reshape(N, T, N_HEADS_M, HEAD_DIM), qn_m)
    gates = jax.nn.sigmoid(gl + b_gate).reshape(N, T, N_BRANCH, D_MODEL)
    return qa, ka, va, u, qm, gates


def band_group(q, k, v, window, dil):
    B, S, H, Dh = q.shape
    L = S // dil
    Lp = -(-L // BAND_BLK) * BAND_BLK
    nb = Lp // BAND_BLK
    wsub = window // dil

    def to_blocks(t):
        t = t.reshape(B, L, dil, H, Dh).transpose(0, 2, 1, 3, 4)
        t = jnp.pad(t, ((0, 0), (0, 0), (0, Lp - L), (0, 0), (0, 0)))
        return t.reshape(B, dil, nb, BAND_BLK, H, Dh)

    def with_prev(t):
        prev = jnp.pad(t, ((0, 0), (0, 0), (1, 0), (0, 0), (0, 0), (0, 0)))[:, :, :-1]
        return jnp.concatenate([prev, t], axis=3)

    qb = to_blocks(q)
    kk = with_prev(to_blocks(k))
    vv = with_prev(to_blocks(v))
    s = jnp.einsum('brnqhd,brnkhd->brnhqk', qb, kk,
                   preferred_element_type=jnp.float32) * (HEAD_DIM ** -0.5)
    qi = jnp.arange(BAND_BLK)[:, None]
    ki = jnp.arange(2 * BAND_BLK)[None, :]
    delta = qi + BAND_BLK - ki
    band = (delta >= 0) & (delta <= wsub)
    nidx = jnp.arange(nb)[:, None, None]
    mask = band[None] & ((nidx > 0) | (ki >= BAND_BLK)[None])
    s = jnp.where(mask[None, None, :, None], s, -jnp.inf)
    p, lse = softmax_lse(s)
    o = jnp.einsum('brnhqk,brnkhd->brnhqd', p, vv.astype(jnp.float32))
    o = o.transpose(0, 1, 2, 4, 3, 5).reshape(B, dil, Lp, H, Dh)[:, :, :L]
    o = o.transpose(0, 2, 1, 3, 4).reshape(B, S, H, Dh)
    lse = lse.transpose(0, 1, 2, 4, 3).reshape(B, dil, Lp, H)[:, :, :L]
    lse = lse.transpose(0, 2, 1, 3).reshape(B, S, H)
    return o, lse


def gather_group(q, k_all, v_all, n_buf, window, dil):
    T = q.shape[1]
    offs = jnp.arange(window // dil + 1) * dil
    idx = n_buf + jnp.arange(T)[:, None] - offs[None, :]
    valid = idx >= 0
    idx = jnp.maximum(idx, 0)
    ks = k_all[:, idx]
    vs = v_all[:, idx]
    s = jnp.einsum('bthd,btkhd->bthk', q, ks,
                   preferred_element_type=jnp.float32) * (HEAD_DIM ** -0.5)
    s = jnp.where(valid[None, :, None, :], s, -jnp.inf)
    p, lse = softmax_lse(s)
    o = jnp.einsum('bthk,btkhd->bthd', p, vs.astype(jnp.float32))
    return o, lse


def combine_groups(outs, lses):
    o = jnp.stack(outs, axis=0)
    w = jax.nn.softmax(jnp.stack(lses, axis=0), axis=0)
    return jnp.sum(w[..., None] * o, axis=0)


def dilated_attn_prompt(qa, ka, va):
    outs, lses = [], []
    for g, (win, dil) in enumerate(DIL_GROUPS):
        qg = qa[:, :, g * HEADS_PER_GROUP:(g + 1) * HEADS_PER_GROUP]
        o, l = band_group(qg, ka, va, win, dil)
        outs.append(o)
        lses.append(l)
    return combine_groups(outs, lses)


def dilated_attn_sample(qa, ka, va, win_k, win_v):
    n_buf = win_k.shape[1]
    k_all = jnp.concatenate([win_k.astype(ka.dtype), ka], axis=1)
    v_all = jnp.concatenate([win_v.astype(va.dtype), va], axis=1)
    outs, lses = [], []
    for g, (win, dil) in enumerate(DIL_GROUPS):
        qg = qa[:, :, g * HEADS_PER_GROUP:(g + 1) * HEADS_PER_GROUP]
        o, l = gather_group(qg, k_all, v_all, n_buf, win, dil)
        outs.append(o)
        lses.append(l)
    return combine_groups(outs, lses)


def mem_kv(mem, g_mem, w_mem_kv, kn_m):
    N, Mn, _ = mem.shape
    kv = rms_norm(mem, g_mem) @ w_mem_kv
    mk = rms_norm(kv[..., :M_OUT_W].reshape(N, Mn, N_HEADS_M, HEAD_DIM), kn_m)
    mv = kv[..., M_OUT_W:].reshape(N, Mn, N_HEADS_M, HEAD_DIM)
    return mk, mv


def mem_attention(qm, mk, mv):
    s = jnp.einsum('bshd,bmhd->bhsm', qm, mk,
                   preferred_element_type=jnp.float32) * (HEAD_DIM ** -0.5)
    p = jax.nn.softmax(s, axis=-1)
    return jnp.einsum('bhsm,bmhd->bshd', p, mv.astype(jnp.float32))


def conformer_conv(u, left, w_dw, b_dw, ln_g, ln_b):
    glu = u[..., :CONV_CH] * jax.nn.sigmoid(u[..., CONV_CH:])
    seq = jnp.concatenate([left.astype(glu.dtype), glu], axis=1)
    y = lax.conv_general_dilated(seq, w_dw.astype(seq.dtype)[:, None, :], (1,), 'VALID',
                                 dimension_numbers=('NWC', 'WIO', 'NWC'),
                                 feature_group_count=CONV_CH)
    yf = y.astype(jnp.float32) + b_dw.astype(jnp.float32)
    mu = jnp.mean(yf, axis=-1, keepdims=True)
    var = jnp.mean(jnp.square(yf - mu), axis=-1, keepdims=True)
    yn = (yf - mu) * lax.rsqrt(var + EPS) * ln_g.astype(jnp.float32) + ln_b.astype(jnp.float32)
    out = (yn * jax.nn.sigmoid(yn)).astype(u.dtype)
    return out, seq[:, -(CONV_W - 1):]


def peer(x, w_pq, sub_keys, u_tab, v_tab):
    n_tok = x.shape[0]
    nblk = -(-n_tok // PEER_BLK)
    xp = jnp.pad(x, ((0, nblk * PEER_BLK - n_tok), (0, 0)))

    def block(xb):
        nb = xb.shape[0]
        q = (xb @ w_pq).reshape(nb, PEER_HEADS, 2, PK_HALF)
        s = jnp.einsum('bhcd,ckd->bhck', q, sub_keys, preferred_element_type=jnp.float32)
        s1, i1 = lax.top_k(s[:, :, 0], PEER_TOPK)
        s2, i2 = lax.top_k(s[:, :, 1], PEER_TOPK)
        cand = (s1[..., :, None] + s2[..., None, :]).reshape(nb, PEER_HEADS, PEER_TOPK * PEER_TOPK)
        sc, ci = lax.top_k(cand, PEER_TOPK)
        e = (jnp.take_along_axis(i1, ci // PEER_TOPK, axis=-1) * N_KEYS
             + jnp.take_along_axis(i2, ci % PEER_TOPK, axis=-1))
        g = jax.nn.softmax(sc, axis=-1)
        e = e.reshape(nb, PEER_HEADS * PEER_TOPK)
        g = g.reshape(nb, PEER_HEADS * PEER_TOPK)
        a = jnp.einsum('bd,bed->be', xb, u_tab[e], preferred_element_type=jnp.float32)
        hg = g * jax.nn.gelu(a)
        out = jnp.einsum('be,bed->bd', hg, v_tab[e].astype(jnp.float32))
        return out.astype(xb.dtype)

    ys = lax.map(block, xp.reshape(nblk, PEER_BLK, D_MODEL))
    return ys.reshape(nblk * PEER_BLK, D_MODEL)[:n_tok]


def back(x, a, b, m, gates, w_a_proj, w_b_proj, w_m_proj, w_o, g_norm2, w_pq, sub_keys, u_tab, v_tab):
    N, T, _ = x.shape
    dt = x.dtype
    ya = a.reshape(N, T, A_OUT_W).astype(dt) @ w_a_proj
    yb = b.astype(dt) @ w_b_proj
    ym = m.reshape(N, T, M_OUT_W).astype(dt) @ w_m_proj
    merged = gates[..., 0, :] * ya + gates[..., 1, :] * yb + gates[..., 2, :] * ym
    x1 = x + merged @ w_o
    h2 = rms_norm(x1, g_norm2).reshape(N * T, D_MODEL)
    return x1 + peer(h2, w_pq, sub_keys, u_tab, v_tab).reshape(N, T, D_MODEL)


def setup_inputs(seed: int = 0) -> dict:
    key = jax.random.key(seed)
    ks = iter(jax.random.split(key, 40))

    def nrm(shape, scale):
        return jax.random.normal(next(ks), shape, jnp.float32) * scale

    def gain(shape):
        return 1.0 + nrm(shape, 0.02)

    L = DEPTH
    D = D_MODEL
    win_s = min(WIN_MAX, PAST_LEN)
    return {
        'x_prompt': nrm((BATCH, SEQ, D), 1.0),
        'x_sample': nrm((DEC_BATCH, DEC_SEQ, D), 1.0),
        'mem_prompt': nrm((BATCH, N_MEM, D), 1.0),
        'cache_win_k': nrm((L, DEC_BATCH, win_s, N_KV_A, HEAD_DIM), 1.0),
        'cache_win_v': nrm((L, DEC_BATCH, win_s, N_KV_A, HEAD_DIM), 1.0),
        'cache_mem_k': nrm((L, DEC_BATCH, N_MEM, N_HEADS_M, HEAD_DIM), 1.0),
        'cache_mem_v': nrm((L, DEC_BATCH, N_MEM, N_HEADS_M, HEAD_DIM), 1.0),
        'state_conv': nrm((L, DEC_BATCH, CONV_W - 1, CONV_CH), 0.5),
        'g_norm1': gain((L, D)),
        'w_in': nrm((L, D, IN_W), D ** -0.5),
        'b_gate': nrm((L, GATE_W), 0.01),
        'qn_a': gain((L, HEAD_DIM)),
        'kn_a': gain((L, HEAD_DIM)),
        'qn_m': gain((L, HEAD_DIM)),
        'kn_m': gain((L, HEAD_DIM)),
        'g_mem': gain((L, D)),
        'w_mem_kv': nrm((L, D, 2 * M_OUT_W), D ** -0.5),
        'w_dw': nrm((L, CONV_W, CONV_CH), CONV_W ** -0.5),
        'b_dw': nrm((L, CONV_CH), 0.01),
        'ln_g': gain((L, CONV_CH)),
        'ln_b': nrm((L, CONV_CH), 0.01),
        'w_a_proj': nrm((L, A_OUT_W, D), A_OUT_W ** -0.5),
        'w_b_proj': nrm((L, CONV_CH, D), CONV_CH ** -0.5),
        'w_m_proj': nrm((L, M_OUT_W, D), M_OUT_W ** -0.5),
        'w_o': nrm((L, D, D), D ** -0.5),
        'g_norm2': gain((L, D)),
        'w_pq': nrm((L, D, PEER_HEADS * 2 * PK_HALF), D ** -0.5),
        'sub_keys': nrm((L, 2, N_KEYS, PK_HALF), PK_HALF ** -0.5),
        'u_tab': nrm((L, N_EXPERTS, D), D ** -0.5),
        'v_tab': nrm((L, N_EXPERTS, D), D ** -0.5),
    }


def reference(x_prompt, x_sample, mem_prompt, cache_win_k, cache_win_v, cache_mem_k, cache_mem_v,
              state_conv, g_norm1, w_in, b_gate, qn_a, kn_a, qn_m, kn_m, g_mem, w_mem_kv, w_dw, b_dw,
              ln_g, ln_b, w_a_proj, w_b_proj, w_m_proj, w_o, g_norm2, w_pq, sub_keys, u_tab, v_tab):
    S = x_prompt.shape[1]
    T = x_sample.shape[1]
    pos_p = jnp.arange(S)
    pos_s = PAST_LEN + jnp.arange(T)
    n_win_p = min(WIN_MAX, S)
    hp, hs = x_prompt, x_sample
    wk_p, wv_p, mk_p, mv_p, cv_p, wk_s, wv_s, cv_s = [], [], [], [], [], [], [], []
    for l in range(DEPTH):
        qa, ka, va, u, qm, gates = front(hp, pos_p, g_norm1[l], w_in[l], b_gate[l], qn_a[l], kn_a[l], qn_m[l])
        a = dilated_attn_prompt(qa, ka, va)
        mk, mv = mem_kv(mem_prompt, g_mem[l], w_mem_kv[l], kn_m[l])
        m = mem_attention(qm, mk, mv)
        left = jnp.zeros((hp.shape[0], CONV_W - 1, CONV_CH), u.dtype)
        b, conv_p = conformer_conv(u, left, w_dw[l], b_dw[l], ln_g[l], ln_b[l])
        hp = back(hp, a, b, m, gates, w_a_proj[l], w_b_proj[l], w_m_proj[l], w_o[l], g_norm2[l],
                  w_pq[l], sub_keys[l], u_tab[l], v_tab[l])
        wk_p.append(ka[:, S - n_win_p:])
        wv_p.append(va[:, S - n_win_p:])
        mk_p.append(mk)
        mv_p.append(mv)
        cv_p.append(conv_p)
        qa, ka, va, u, qm, gates = front(hs, pos_s, g_norm1[l], w_in[l], b_gate[l], qn_a[l], kn_a[l], qn_m[l])
        a = dilated_attn_sample(qa, ka, va, cache_win_k[l], cache_win_v[l])
        m = mem_attention(qm, cache_mem_k[l], cache_mem_v[l])
        b, conv_s = conformer_conv(u, state_conv[l], w_dw[l], b_dw[l], ln_g[l], ln_b[l])
        hs = back(hs, a, b, m, gates, w_a_proj[l], w_b_proj[l], w_m_proj[l], w_o[l], g_norm2[l],
                  w_pq[l], sub_keys[l], u_tab[l], v_tab[l])
        wk_s.append(ka)
        wv_s.append(va)
        cv_s.append(conv_s)
    return (hp, hs, jnp.stack(wk_p), jnp.stack(wv_p), jnp.stack(mk_p), jnp.stack(mv_p), jnp.stack(cv_p),
            jnp.stack(wk_s), jnp.stack(wv_s), jnp.stack(cv_s))
```

```python
import numpy as np
import concourse.bass as bass
import concourse.mybir as mybir
from concourse.bass_utils import run_bass_kernel_spmd

F32 = mybir.dt.float32
BF16 = mybir.dt.bfloat16
U32 = mybir.dt.uint32
AF = mybir.ActivationFunctionType
ALU = mybir.AluOpType
AX = mybir.AxisListType

EPS = 1e-6
NCORES = 8
CH = 2048
NS = 4
SCALE = 128 ** -0.5
DEBUG = {}
PIPE_BLOCKS = True


class Buf:
    __slots__ = ("w", "r", "excl")

    def __init__(self):
        self.w = {}
        self.r = []
        self.excl = False


class Prog:
    NDMA = 12

    def __init__(self, nc):
        self.nc = nc
        self.names = ("pe", "act", "dve", "pool", "sp")
        self.lists = {k: [] for k in self.names}
        self.cnt = {k: 0 for k in self.names}
        self.sems = {}
        self._stack = []
        for k in self.names:
            self.sems[k] = self._sem("c_" + k)
        self.dq = {}
        for q in ("sp", "act", "pool"):
            self.dq[q] = {"sems": [self._sem(f"d_{q}{i}") for i in range(self.NDMA)],
                          "uses": [0] * self.NDMA, "n": 0, "last": [None] * self.NDMA}
        self.known = {k: {} for k in self.names}

    def _sem(self, name):
        cm = self.nc.semaphore(name)
        s = cm.__enter__()
        self._stack.append(cm)
        return s

    def _collect(self, eng, reads, writes):
        deps = {}

        def add(ev):
            if ev is None:
                return
            s, v = ev
            if deps.get(s.name, (None, -1))[1] < v:
                deps[s.name] = (s, v)
        for b in reads:
            for e in b.w.values():
                add(e)
        for b in writes:
            for e in b.w.values():
                add(e)
            for e in b.r:
                add(e)
        out = []
        kn = self.known[eng]
        for name, (s, v) in deps.items():
            if eng == "pe" and name == "c_pe":
                continue
            if kn.get(name, -1) >= v:
                continue
            kn[name] = v
            out.append((s, v))
        return out

    def _commit(self, ev, reads, writes):
        for b in reads:
            b.r.append(ev)
            if len(b.r) > 64:
                b.r = b.r[-48:] if False else b.r
        for b in writes:
            b.w[ev[0].name] = ev
            b.r = []

    def op(self, eng, fn, reads=(), writes=()):
        ex = [b for b in reads if b.excl]
        if ex:
            writes = list(writes) + ex
            reads = [b for b in reads if not b.excl]
        waits = self._collect(eng, reads, writes)
        self.cnt[eng] += 1
        ev = (self.sems[eng], self.cnt[eng])
        self.lists[eng].append((waits, fn, self.sems[eng], 1))
        self._commit(ev, reads, writes)
        return ev

    def dma(self, q, fn, reads=(), writes=()):
        d = self.dq[q]
        i = d["n"] % self.NDMA
        d["n"] += 1
        waits = self._collect(q, reads, writes)
        prev = d["last"][i]
        if prev is not None:
            kn = self.known[q]
            if kn.get(prev[0].name, -1) < prev[1]:
                kn[prev[0].name] = prev[1]
                waits.append(prev)
        d["uses"][i] += 1
        ev = (d["sems"][i], 16 * d["uses"][i])
        d["last"][i] = ev
        self.lists[q].append((waits, fn, d["sems"][i], 16))
        self._commit(ev, reads, writes)
        return ev

    def finish(self):
        fin = []
        for q, d in self.dq.items():
            for ev in d["last"]:
                if ev is not None:
                    fin.append(ev)
        engs = {"pe": "tensor", "act": "scalar", "dve": "vector", "pool": "gpsimd", "sp": "sync"}
        with self.nc.Block() as block:
            def mk(name):
                def body(e):
                    for waits, fn, sem, amt in self.lists[name]:
                        for (s, v) in waits:
                            e.wait_ge(s, v)
                        ins = fn(e)
                        ins.then_inc(sem, amt)
                    if name == "sp":
                        for (s, v) in fin:
                            e.wait_ge(s, v)
                return body
            for name in ("sp", "act", "dve", "pool", "pe"):
                getattr(block, engs[name])(mk(name))
        for cm in reversed(self._stack):
            cm.__exit__(None, None, None)
        self._stack = []


class TN:
    def __init__(self, t):
        self.t = t
        self.b = Buf()


class Rot:
    def __init__(self, items):
        self.items = items
        self.i = 0

    def next(self):
        x = self.items[self.i % len(self.items)]
        self.i += 1
        return x


class _Stop(Exception):
    pass


def build_program(dbg=(), stop=None):
    nc = bass.Bass("TRN2", target_bir_lowering=False)
    P = Prog(nc)
    dram_in = {}
    dram_out = {}

    def phase_end(k):
        if stop == k:
            raise _Stop()

    def body():
        def DI(name, shape, dt=F32):
            dram_in[name] = TN(nc.dram_tensor(name, list(shape), dt, kind="ExternalInput"))
            return dram_in[name]

        def DO(name, shape, dt=F32):
            dram_out[name] = TN(nc.dram_tensor(name, list(shape), dt, kind="ExternalOutput"))
            return dram_out[name]

        SB_LO, SB_HI = 16512, 229344
        free_list = [[SB_LO, SB_HI]]
        grave = []
        live = {}
        _uid = [0]

        def SB(name, shape, dt=F32):
            nbytes = int(np.prod(shape[1:])) * (4 if dt in (F32, U32) else 2)
            nbytes = (nbytes + 31) // 32 * 32
            big = nbytes >= 8192
            for fr in (reversed(free_list) if big else free_list):
                if fr[1] - fr[0] >= nbytes:
                    if big:
                        fr[1] -= nbytes
                        off = fr[1]
                    else:
                        off = fr[0]
                        fr[0] += nbytes
                    break
            else:
                raise RuntimeError(f"SBUF arena full allocating {name} {shape}: free={free_list}")
            _uid[0] += 1
            tn = TN(nc.alloc_sbuf_tensor_at(f"{name}_{_uid[0]}", list(shape), dt, offset=off))
            for (gs, ge, gb) in grave:
                if gs < off + nbytes and off < ge:
                    tn.b.r.extend(gb.w.values())
                    tn.b.r.extend(gb.r)
            live[id(tn)] = (off, off + nbytes)
            return tn

        def FREE(*tns):
            for tn in tns:
                if isinstance(tn, Rot):
                    FREE(*tn.items)
                    continue
                a, b = live.pop(id(tn))
                grave.append((a, b, tn.b))
                free_list.append([a, b])
            free_list.sort()
            merged = []
            for fr in free_list:
                if fr[1] <= fr[0]:
                    continue
                if merged and merged[-1][1] == fr[0]:
                    merged[-1][1] = fr[1]
                else:
                    merged.append(fr)
            free_list[:] = merged

        xm = DI("xm", [CH, 1024]); xh = DI("xh", [CH, 1024]); xs = DI("xs", [NS, 1024]); memx = DI("memx", [256, 1024])
        wkc = DI("wkc", [NS, 2048, 512]); wvc = DI("wvc", [NS, 2048, 512])
        mkc = DI("mkc", [NS, 256, 512]); mvc = DI("mvc", [NS, 256, 512]); stc = DI("stc", [NS, 30, 512])
        w_in = DI("w_in", [1024, 7168]); w_mem = DI("w_mem", [1024, 1024])
        w_a = DI("w_a", [512, 1024]); w_b = DI("w_b", [512, 1024]); w_m = DI("w_m", [512, 1024])
        w_o = DI("w_o", [1024, 1024]); w_pq = DI("w_pq", [1024, 2048])
        u_tab = DI("u_tab", [16384, 1024]); v_tab = DI("v_tab", [16384, 1024])
        skT_d = DI("skT", [128, 2, 128])
        g1T_d = DI("g1T", [128, 8]); g2T_d = DI("g2T", [128, 8]); gmT_d = DI("gmT", [128, 8])
        bgT_d = DI("bgT", [128, 24]); bdwT_d = DI("bdwT", [128, 4]); lngT_d = DI("lngT", [128, 4]); lnbT_d = DI("lnbT", [128, 4])
        wdwT_d = DI("wdwT", [128, 4, 31])
        qna_d = DI("qna_bc", [128, 128]); kna_d = DI("kna_bc", [128, 128]); qnm_d = DI("qnm_bc", [128, 128]); knm_d = DI("knm_bc", [128, 128])
        ident_d = DI("ident", [128, 128]); iota_d = DI("iota", [128, 128])
        mprev_d = DI("mprev", [128, 128]); mcur_d = DI("mcur", [128, 128]); hb_d = DI("hb", [128, 1])
        csm_d = DI("csm", [CH, 64]); csh_d = DI("csh", [CH, 64]); css_d = DI("css", [NS, 64])

        y_o = DO("y", [CH, 1024]); ys_o = DO("ys", [NS, 1024])
        ko_o = DO("ko", [CH, 512]); vo_o = DO("vo", [CH, 512])
        mko_o = DO("mko", [256, 512]); mvo_o = DO("mvo", [256, 512]); cvo_o = DO("cvo", [30, 512])
        kso_o = DO("kso", [NS, 512]); vso_o = DO("vso", [NS, 512]); cso_o = DO("cso", [NS, 30, 512])
        x1s = TN(nc.dram_tensor("x1s", [CH + NS, 1024], F32))
        uTs = TN(nc.dram_tensor("uTs", [128, 128, 1024], BF16))
        vbs = TN(nc.dram_tensor("vbs", [128, 128, 1024], BF16))

        def dbg_out(name, tn, ap, shape, dt=F32):
            if name in dbg:
                o = DO("dbg_" + name, shape, dt)
                P.dma("sp", lambda e: e.dma_start(out=o.t.ap(), in_=ap), [tn.b], [o.b])

        PS = [TN(nc.alloc_psum_tensor(f"ps{i}", [128, 512], F32)) for i in range(6)]
        PSB = [TN(nc.alloc_psum_tensor(f"psb{i}", [128, 1024], BF16)) for i in range(2)]
        for _p in PS + PSB:
            _p.b.excl = True
        psr = Rot(PS[0:4])
        psbr = Rot(PSB)

        identf = SB("identf", [128, 128]); identb = SB("identb", [128, 128], BF16)
        iota = SB("iota", [128, 128]); onesb = SB("onesb", [128, 128], BF16); onesf = SB("onesf", [128, 128])
        maskA = SB("maskA", [128, 2, 128], BF16); maskH = SB("maskH", [128, 2, 128], BF16)
        hb = SB("hb", [128, 1]); mtmp = SB("mtmp", [128, 2, 128])
        g1T = SB("g1T", [128, 8]); g2T = SB("g2T", [128, 8]); gmT = SB("gmT", [128, 8])
        bgT = SB("bgT", [128, 24]); bdwT = SB("bdwT", [128, 4]); lngT = SB("lngT", [128, 4]); lnbT = SB("lnbT", [128, 4])
        wdwT = SB("wdwT", [128, 4, 31])
        qna = SB("qna", [128, 128]); kna = SB("kna", [128, 128]); qnm = SB("qnm", [128, 128]); knm = SB("knm", [128, 128])
        skT = SB("skT", [128, 2, 128], BF16)

        def ld(dst, src, q="sp"):
            P.dma(q, lambda e: e.dma_start(out=dst.t[:], in_=src.t.ap()), [src.b], [dst.b])
        for dst, src in ((identf, ident_d), (iota, iota_d), (hb, hb_d), (g1T, g1T_d), (g2T, g2T_d), (gmT, gmT_d),
                         (bgT, bgT_d), (bdwT, bdwT_d), (lngT, lngT_d), (lnbT, lnbT_d), (wdwT, wdwT_d),
                         (qna, qna_d), (kna, kna_d), (qnm, qnm_d), (knm, knm_d)):
            ld(dst, src)
        ld(skT, skT_d, "pool")
        P.dma("sp", lambda e: e.dma_start(out=mtmp.t[:, 0, :], in_=mprev_d.t.ap()), [], [mtmp.b])
        P.dma("sp", lambda e: e.dma_start(out=mtmp.t[:, 1, :], in_=mcur_d.t.ap()), [mtmp.b], [mtmp.b])
        P.op("dve", lambda e: e.tensor_copy(out=identb.t[:], in_=identf.t[:]), [identf.b], [identb.b])
        P.op("dve", lambda e: e.memset(onesb.t[:], 1.0), [], [onesb.b])
        P.op("dve", lambda e: e.memset(onesf.t[:], 1.0), [], [onesf.b])
        P.op("dve", lambda e: e.tensor_copy(out=maskA.t[:], in_=mtmp.t[:]), [mtmp.b], [maskA.b])
        P.op("dve", lambda e: e.tensor_copy(out=maskH.t[:, 1, :], in_=mtmp.t[:, 1, :]), [mtmp.b], [maskH.b])
        P.op("dve", lambda e: e.tensor_scalar(out=maskH.t[:, 0, :], in0=mtmp.t[:, 0, :], scalar1=hb.t[:, 0:1], scalar2=None,
                                              op0=ALU.mult), [mtmp.b, hb.b, maskH.b], [maskH.b])

        hTm = SB("hTm", [128, 8, CH], BF16)
        hTs = SB("hTs", [128, 8, NS], BF16)
        hTh = SB("hTh", [128, 8, CH], BF16)
        hTmem = SB("hTmem", [128, 8, 256], BF16)
        AT = SB("AT", [128, 4, CH], BF16); AsT = SB("AsT", [128, 4, NS], BF16)

        H = {}

        def alloc_xbufs():
            H["xt"] = Rot([SB(f"xt{i}", [128, 1024]) for i in range(2)])
            H["xb"] = Rot([SB(f"xb{i}", [128, 1024], BF16) for i in range(2)])
        alloc_xbufs()
        junk = SB("junk", [128, 1024], BF16)
        ss_r = Rot([SB(f"ss{i}", [128, 8]) for i in range(8)])

        def norm_T(src, n, gT, dst, c0):
            ss = ss_r.next(); xb = H["xb"].next()
            P.op("pool", lambda e: e.memset(ss.t[:n, 0:1], 0.0), [], [ss.b])
            P.op("act", lambda e: e.activation(out=junk.t[:n], in_=src.t[:n], func=AF.Square, accum_out=ss.t[:n, 0:1]),
                 [src.b, ss.b], [ss.b])
            P.op("act", lambda e: e.activation(out=ss.t[:n, 0:1], in_=ss.t[:n, 0:1], func=AF.Sqrt, bias=EPS, scale=1.0 / 1024),
                 [ss.b], [ss.b])
            P.op("dve", lambda e: e.reciprocal(out=ss.t[:n, 0:1], in_=ss.t[:n, 0:1]), [ss.b], [ss.b])
            P.op("dve", lambda e: e.tensor_scalar(out=xb.t[:n], in0=src.t[:n], scalar1=ss.t[:n, 0:1], scalar2=None, op0=ALU.mult),
                 [src.b, ss.b], [xb.b])
            ps = psbr.next()

            def tr(e):
                for kc in range(8):
                    ins = e.transpose(out=ps.t[:, kc * 128:kc * 128 + n], in_=xb.t[:n, kc * 128:(kc + 1) * 128], identity=identb.t[:n, :n])
                return ins
            P.op("pe", tr, [xb.b, identb.b], [ps.b])
            psv = ps.t[:].rearrange("p (k t) -> p k t", k=8)
            P.op("dve", lambda e: e.tensor_tensor(out=dst.t[:, :, c0:c0 + n], in0=psv[:, :, 0:n],
                                                  in1=gT.t[:, :].unsqueeze(2).to_broadcast([128, 8, n]), op=ALU.mult),
                 [ps.b, gT.b], [dst.b])

        def load_norm(src_d, r0, n, gT, dst, c0):
            xt = H["xt"].next()
            P.dma("sp", lambda e: e.dma_start(out=xt.t[:n], in_=src_d.t.ap()[r0:r0 + n, :]), [src_d.b], [xt.b])
            norm_T(xt, n, gT, dst, c0)

        for t in range(16):
            load_norm(xh, t * 128, 128, g1T, hTh, t * 128)
        for t in range(16):
            load_norm(xm, t * 128, 128, g1T, hTm, t * 128)
        load_norm(xs, 0, NS, g1T, hTs, 0)
        for t in range(2):
            load_norm(memx, t * 128, 128, gmT, hTmem, t * 128)
        dbg_out("hTm", hTm, hTm.t[:, 0, 0:256], [128, 256], BF16)
        FREE(H["xt"], H["xb"])

        phase_end(1)
        H["wb"] = Rot([SB(f"wb{i}", [128, 8, 512], BF16) for i in range(2)])

        def load_w(src, r0, nk, c0, ncols, dst=None, dcol=0):
            wb = dst if dst is not None else H["wb"].next()
            sap = src.t.ap()[r0:r0 + nk * 128, c0:c0 + ncols].rearrange("(kc p) n -> p kc n", p=128)
            P.dma("pool", lambda e: e.dma_start(out=wb.t[:, 0:nk, dcol:dcol + ncols], in_=sap), [src.b], [wb.b])
            return wb

        def mm_tm(ps, n, hT, cols, wb, wc0, wn, po=0):
            def f(e):
                for kc in range(8):
                    ins = e.matmul(ps.t[:n, po:po + wn], lhsT=hT.t[:, kc, cols], rhs=wb.t[:, kc, wc0:wc0 + wn],
                                   start=(kc == 0), stop=(kc == 7))
                return ins
            P.op("pe", f, [hT.b, wb.b], [ps.b])

        def mm_fm(ps, N, wb, wc0, hT, cols, nk=8, po=0):
            def f(e):
                for kc in range(nk):
                    ins = e.matmul(ps.t[:, po:po + N], lhsT=wb.t[:, kc, wc0:wc0 + 128], rhs=hT.t[:, kc, cols],
                                   start=(kc == 0), stop=(kc == nk - 1))
                return ins
            P.op("pe", f, [hT.b, wb.b], [ps.b])

        kf_r = Rot([SB(f"kf{i}", [128, 512]) for i in range(4)])
        kb_r = Rot([SB(f"kb{i}", [128, 512], BF16) for i in range(4)])
        cs_r = Rot([SB(f"cs{i}", [128, 64]) for i in range(4)])
        rA_r = Rot([SB(f"rA{i}", [128, 4, 32]) for i in range(4)]); rB_r = Rot([SB(f"rB{i}", [128, 4, 32]) for i in range(4)])
        vf_r = Rot([SB(f"vf{i}", [128, 512]) for i in range(2)])
        sg_r = Rot([SB(f"sg{i}", [128, 512]) for i in range(2)])

        def run_pipe(gens, depth=4, side=None):
            gens = list(gens)
            active = []
            gi = 0
            while gi < len(gens) or active:
                while gi < len(gens) and len(active) < depth:
                    active.append(gens[gi]); gi += 1
                if side is not None:
                    try:
                        next(side)
                    except StopIteration:
                        side = None
                for g_ in list(active):
                    try:
                        next(g_)
                    except StopIteration:
                        active.remove(g_)

        def qk_gen(hT, cols, wb, wc0, n, nh, gain, cs_src, cs_r0, out_d, out_r0, out_c0, dstT, c0):
            W = nh * 128
            ss = ss_r.next(); kf = kf_r.next(); kb = kb_r.next()
            ps = psr.next()
            if cs_src is not None:
                cs = cs_r.next(); rA = rA_r.next(); rB = rB_r.next()
                P.dma("sp", lambda e: e.dma_start(out=cs.t[:n], in_=cs_src.t.ap()[cs_r0:cs_r0 + n, :]), [cs_src.b], [cs.b])
            mm_tm(ps, n, hT, cols, wb, wc0, W)
            P.op("pool", lambda e: e.memset(ss.t[:n, 0:nh], 0.0), [], [ss.b])
            yield

            def sq(e):
                for h in range(nh):
                    ins = e.activation(out=junk.t[:n, h * 128:(h + 1) * 128], in_=ps.t[:n, h * 128:(h + 1) * 128], func=AF.Square,
                                       accum_out=ss.t[:n, h:h + 1])
                return ins
            P.op("act", sq, [ps.b, ss.b], [ss.b])
            P.op("act", lambda e: e.activation(out=ss.t[:n, 0:nh], in_=ss.t[:n, 0:nh], func=AF.Sqrt, bias=EPS, scale=1.0 / 128),
                 [ss.b], [ss.b])
            yield
            P.op("dve", lambda e: e.reciprocal(out=ss.t[:n, 0:nh], in_=ss.t[:n, 0:nh]), [ss.b], [ss.b])

            def sc_(e):
                for h in range(nh):
                    ins = e.scalar_tensor_tensor(out=kf.t[:n, h * 128:(h + 1) * 128], in0=ps.t[:n, h * 128:(h + 1) * 128],
                                                 scalar=ss.t[:n, h:h + 1], in1=gain.t[:n, :], op0=ALU.mult, op1=ALU.mult)
                return ins
            P.op("dve", sc_, [ps.b, ss.b, gain.b], [kf.b])
            yield
            if cs_src is not None:
                kv = kf.t[:n, 0:W].rearrange("p (h d) -> p h d", h=nh)

                def rot1(e):
                    e.tensor_tensor(out=rA.t[:n, 0:nh], in0=kv[:, :, 0:32], in1=cs.t[:n, 0:32].unsqueeze(1).to_broadcast([n, nh, 32]), op=ALU.mult)
                    return e.tensor_tensor(out=rB.t[:n, 0:nh], in0=kv[:, :, 0:32], in1=cs.t[:n, 32:64].unsqueeze(1).to_broadcast([n, nh, 32]), op=ALU.mult)
                P.op("dve", rot1, [kf.b, cs.b], [rA.b, rB.b])
                yield

                def rot2(e):
                    e.tensor_tensor(out=kv[:, :, 0:16], in0=rA.t[:n, 0:nh, 0:16], in1=rB.t[:n, 0:nh, 16:32], op=ALU.subtract)
                    return e.tensor_tensor(out=kv[:, :, 16:32], in0=rA.t[:n, 0:nh, 16:32], in1=rB.t[:n, 0:nh, 0:16], op=ALU.add)
                P.op("dve", rot2, [rA.b, rB.b], [kf.b])
                yield
            if out_d is not None:
                P.dma("sp", lambda e: e.dma_start(out=out_d.t.ap()[out_r0:out_r0 + n, out_c0:out_c0 + W], in_=kf.t[:n, 0:W]), [kf.b], [out_d.b])
            P.op("act", lambda e: e.copy(out=kb.t[:n, 0:W], in_=kf.t[:n, 0:W]), [kf.b], [kb.b])
            yield
            pt = psbr.next()

            def tr(e):
                for h in range(nh):
                    ins = e.transpose(out=pt.t[:, h * 128:h * 128 + n], in_=kb.t[:n, h * 128:(h + 1) * 128], identity=identb.t[:n, :n])
                return ins
            P.op("pe", tr, [kb.b, identb.b], [pt.b])
            ptv = pt.t[:, 0:W].rearrange("p (h t) -> p h t", h=nh)
            P.op("act", lambda e: e.copy(out=dstT.t[:, 0:nh, c0:c0 + n], in_=ptv[:, :, 0:n]), [pt.b], [dstT.b])

        phase_end(12)
        def glu_cols(wu, hT, cols, N, dst_ap_fn, tail_fn=None):
            for j in range(4):
                pg = psr.next()
                mm_fm(pg, N, wu, (4 + j) * 128, hT, cols)
                sg = sg_r.next()
                P.op("act", lambda e, pg=pg, sg=sg: e.activation(out=sg.t[:, 0:N], in_=pg.t[:, 0:N], func=AF.Sigmoid), [pg.b], [sg.b])
                pv = psr.next()
                mm_fm(pv, N, wu, j * 128, hT, cols)
                dst_tn, dst_ap = dst_ap_fn(j)
                P.op("dve", lambda e, pv=pv, sg=sg, dst_ap=dst_ap: e.tensor_tensor(out=dst_ap, in0=pv.t[:, 0:N], in1=sg.t[:, 0:N], op=ALU.mult),
                     [pv.b, sg.b], [dst_tn.b])
                if tail_fn is not None:
                    t_tn, t_ap = tail_fn(j)
                    P.op("dve", lambda e, pv=pv, sg=sg, t_ap=t_ap: e.tensor_tensor(out=t_ap, in0=pv.t[:, N - 30:N], in1=sg.t[:, N - 30:N], op=ALU.mult),
                         [pv.b, sg.b], [t_tn.b])

        def load_wu():
            wu = SB("wu", [128, 8, 1024], BF16)
            load_w(w_in, 0, 8, 2560, 512, wu, 0); load_w(w_in, 0, 8, 3072, 512, wu, 512)
            return wu
        gluH = SB("gluH", [128, 4, 128])
        wu = load_wu()
        glu_cols(wu, hTh, slice(CH - 128, CH), 128, lambda j: (gluH, gluH.t[:, j, :]))
        FREE(wu)

        phase_end(15)
        mkT = SB("mkT", [128, 4, 256], BF16); mvb = SB("mvb", [128, 2, 512], BF16)
        wmk = load_w(w_mem, 0, 8, 0, 512); wmv = load_w(w_mem, 0, 8, 512, 512)
        for t in range(2):
            run_pipe([qk_gen(hTmem, slice(t * 128, (t + 1) * 128), wmk, 0, 128, 4, knm, None, 0, mko_o, t * 128, 0, mkT, t * 128)])
            ps2 = psr.next()
            mm_tm(ps2, 128, hTmem, slice(t * 128, (t + 1) * 128), wmv, 0, 512)
            vf = vf_r.next()
            P.op("dve", lambda e, ps2=ps2, vf=vf: e.tensor_copy(out=vf.t[:], in_=ps2.t[:]), [ps2.b], [vf.b])
            P.op("act", lambda e, ps2=ps2, t=t: e.copy(out=mvb.t[:, t, :], in_=ps2.t[:]), [ps2.b], [mvb.b])
            P.dma("sp", lambda e, vf=vf, t=t: e.dma_start(out=mvo_o.t.ap()[t * 128:(t + 1) * 128, :], in_=vf.t[:]), [vf.b], [mvo_o.b])
        FREE(hTmem)
        FREE(H["wb"])
        H["wb"] = Rot([SB(f"wbs{i}", [128, 8, 128], BF16) for i in range(3)])

        phase_end(2)
        KT = SB("KT", [128, 1, 2 * CH], BF16); KTs = SB("KTs", [128, 1, NS], BF16)
        QT = SB("QT", [128, 1, CH], BF16); QTs = SB("QTs", [128, 3, NS], BF16); qs_tmp = SB("qs_tmp", [128, 1, NS], BF16)
        Vg = SB("Vg", [128, 32, 128], BF16)
        vrow = SB("vrow", [1, NS, 128], BF16)
        acc = SB("acc", [128, 2, CH])
        Pt_r = Rot([SB(f"Pt{i}", [128, 2, 128], BF16) for i in range(3)])
        kcs_all = [[SB(f"kcs{i}_{g}", [128, 128], BF16) for g in range(3)] for i in range(NS)]
        vcs_all = [[SB(f"vcs{i}_{g}", [128, 128], BF16) for g in range(3)] for i in range(NS)]
        kcT_r = Rot([SB(f"kcT{i}", [128, 128], BF16) for i in range(2)])
        wvb = SB("wvb", [128, 8, 128], BF16)
        pS = SB("pS", [128, 3, 2], BF16)
        psOD = PS[4]; psSm = PS[5]
        psOD_r = Rot([PS[4], PS[5]])
        DILS = (1, 4, 16)

        ub_r = Rot([SB(f"ub{i}", [128, 1024], BF16) for i in range(3)])
        ut_r = Rot([SB(f"ut{i}", [128, 1024], BF16) for i in range(4)])
        vb_r = Rot([SB(f"vb{i}", [128, 1024], BF16) for i in range(3)])

        def prep_tables():
            def loads(i):
                ub = ub_r.next(); vb = vb_r.next()
                P.dma("pool", lambda e, ub=ub, i=i: e.dma_start(out=ub.t[:], in_=u_tab.t.ap()[i * 128:(i + 1) * 128, :]), [u_tab.b], [ub.b])
                P.dma("pool", lambda e, vb=vb, i=i: e.dma_start(out=vb.t[:], in_=v_tab.t.ap()[i * 128:(i + 1) * 128, :]), [v_tab.b], [vb.b])
                return ub, vb
            nxt = loads(0)
            for i in range(128):
                ub, vb = nxt
                if i + 1 < 128:
                    nxt = loads(i + 1)
                pt = psbr.next()

                def tru(e, pt=pt, ub=ub):
                    for kc in range(8):
                        ins = e.transpose(out=pt.t[:, kc * 128:(kc + 1) * 128], in_=ub.t[:, kc * 128:(kc + 1) * 128], identity=identb.t[:])
                    return ins
                P.op("pe", tru, [ub.b, identb.b], [pt.b])
                ut = ut_r.next()
                if i % 2 == 0:
                    P.op("act", lambda e, pt=pt, ut=ut: e.copy(out=ut.t[:], in_=pt.t[:]), [pt.b], [ut.b])
                else:
                    P.op("dve", lambda e, pt=pt, ut=ut: e.tensor_copy(out=ut.t[:], in_=pt.t[:]), [pt.b], [ut.b])
                P.dma("sp", lambda e, ut=ut, i=i: e.dma_start(out=uTs.t.ap()[i], in_=ut.t[:]), [ut.b], [uTs.b])
                P.dma("sp", lambda e, vb=vb, i=i: e.dma_start(out=vbs.t.ap()[i], in_=vb.t[:]), [vb.b], [vbs.b])
                yield
        prep_gen = prep_tables()
        blk_ctr = [0]

        def prep_step(k):
            for _ in range(k):
                try:
                    next(prep_gen)
                except StopIteration:
                    return

        def prep_side(every, count):
            n_ = 0; k_ = 0
            while k_ < count:
                n_ += 1
                if n_ % every == 0:
                    prep_step(1); k_ += 1
                yield

        for hp in range(4):
            wk = load_w(w_in, 0, 8, 1536 + hp * 128, 128)
            for i_ in range(NS):
                for g_, d_ in enumerate(DILS):
                    rows_ = slice(2048 - 128 * d_, 2048, d_)
                    kcs_ = kcs_all[i_][g_]; vcs_ = vcs_all[i_][g_]
                    P.dma("pool", lambda e, kcs_=kcs_, rows_=rows_, hp=hp, i_=i_: e.dma_start(out=kcs_.t[:], in_=wkc.t.ap()[i_, rows_, hp * 128:(hp + 1) * 128]), [wkc.b], [kcs_.b])
                    P.dma("pool", lambda e, vcs_=vcs_, rows_=rows_, hp=hp, i_=i_: e.dma_start(out=vcs_.t[:], in_=wvc.t.ap()[i_, rows_, hp * 128:(hp + 1) * 128]), [wvc.b], [vcs_.b])
            kg = []
            for t in range(32):
                hT = hTh if t < 16 else hTm
                tt = t % 16
                kg.append(qk_gen(hT, slice(tt * 128, (tt + 1) * 128), wk, 0, 128, 1, kna, csh_d if t < 16 else csm_d, tt * 128,
                                 ko_o if t >= 16 else None, tt * 128, hp * 128, KT, t * 128))
            kg.append(qk_gen(hTs, slice(0, NS), wk, 0, NS, 1, kna, css_d, 0, kso_o, 0, hp * 128, KTs, 0))
            run_pipe(kg, side=prep_side(5, 10))
            wv = load_w(w_in, 0, 8, 2048 + hp * 128, 128, wvb, 0)
            for t in range(16):
                ps = psr.next()
                mm_tm(ps, 128, hTm, slice(t * 128, (t + 1) * 128), wv, 0, 128)
                vf = vf_r.next()
                P.op("act", lambda e, ps=ps, vf=vf: e.copy(out=vf.t[:, 0:128], in_=ps.t[:, 0:128]), [ps.b], [vf.b])
                P.dma("sp", lambda e, vf=vf, t=t, hp=hp: e.dma_start(out=vo_o.t.ap()[t * 128:(t + 1) * 128, hp * 128:(hp + 1) * 128], in_=vf.t[:, 0:128]),
                      [vf.b], [vo_o.b])
                if t % 4 == 3:
                    prep_step(1)
            ps = psr.next()
            mm_tm(ps, NS, hTs, slice(0, NS), wv, 0, 128)
            vf = vf_r.next()
            P.op("act", lambda e, ps=ps, vf=vf: e.copy(out=vf.t[:NS, 0:128], in_=ps.t[:NS, 0:128]), [ps.b], [vf.b])
            P.dma("sp", lambda e, vf=vf, hp=hp: e.dma_start(out=vso_o.t.ap()[:, hp * 128:(hp + 1) * 128], in_=vf.t[:NS, 0:128]), [vf.b], [vso_o.b])
            for i in range(NS):
                ps = psr.next()
                mm_tm(ps, 1, hTs, slice(i, i + 1), wv, 0, 128)
                P.op("act", lambda e, ps=ps, i=i: e.copy(out=vrow.t[0:1, i, :], in_=ps.t[0:1, 0:128]), [ps.b], [vrow.b])
            for g, d in enumerate(DILS):
                nbs = CH // (128 * d)

                def vblk(r, j, nbs=nbs):
                    return r * (nbs + 1) + j
                for r in range(d):
                    for j in range(nbs + 1):
                        ps = psr.next()
                        if j == 0:
                            hT, cols = hTh, slice(CH - 128 * d + r, CH, d)
                        else:
                            hT, cols = hTm, slice(128 * d * (j - 1) + r, 128 * d * j, d)
                        mm_tm(ps, 128, hT, cols, wv, 0, 128)
                        bi = vblk(r, j)
                        if bi % 2 == 0:
                            P.op("act", lambda e, ps=ps, bi=bi: e.copy(out=Vg.t[:, bi, :], in_=ps.t[:, 0:128]), [ps.b], [Vg.b])
                        else:
                            P.op("dve", lambda e, ps=ps, bi=bi: e.tensor_copy(out=Vg.t[:, bi, :], in_=ps.t[:, 0:128]), [ps.b], [Vg.b])
                wq = load_w(w_in, 0, 8, g * 512 + hp * 128, 128)
                qg = [qk_gen(hTm, slice(t * 128, (t + 1) * 128), wq, 0, 128, 1, qna, csm_d, t * 128, None, 0, 0, QT, t * 128) for t in range(16)]
                qg.append(qk_gen(hTs, slice(0, NS), wq, 0, NS, 1, qna, css_d, 0, None, 0, 0, qs_tmp, 0))
                run_pipe(qg, side=prep_side(8, 3))
                P.op("pool", lambda e, g=g: e.tensor_copy(out=QTs.t[:, g, :], in_=qs_tmp.t[:, 0, :]), [qs_tmp.b], [QTs.b])
                pend_b = None

                def issue_pv(Pt, bp, bc_, qcols, g=g):
                    pod = psOD_r.next()
                    odv = pod.t[:, 0:256].rearrange("p (o q) -> p o q", o=2)

                    def pv(e, Pt=Pt, odv=odv, bp=bp, bc_=bc_):
                        e.matmul(odv[:, 0, :], lhsT=Vg.t[:, bp, :], rhs=Pt.t[:, 0, :], start=True, stop=False)
                        e.matmul(odv[:, 0, :], lhsT=Vg.t[:, bc_, :], rhs=Pt.t[:, 1, :], start=False, stop=True)
                        e.matmul(odv[:, 1, :], lhsT=onesb.t[:], rhs=Pt.t[:, 0, :], start=True, stop=False)
                        return e.matmul(odv[:, 1, :], lhsT=onesb.t[:], rhs=Pt.t[:, 1, :], start=False, stop=True)
                    P.op("pe", pv, [Pt.b, Vg.b, onesb.b], [pod.b])
                    if g == 0:
                        P.op("dve", lambda e, odv=odv, qcols=qcols: e.tensor_copy(out=acc.t[:, :, qcols], in_=odv), [pod.b], [acc.b])
                    else:
                        P.op("dve", lambda e, odv=odv, qcols=qcols: e.tensor_tensor(out=acc.t[:, :, qcols], in0=acc.t[:, :, qcols], in1=odv, op=ALU.add),
                             [pod.b, acc.b], [acc.b])
                for r in range(d):
                    for nb in range(nbs):
                        qcols = slice(128 * d * nb + r, 128 * d * (nb + 1), d)
                        ccols = slice(CH + 128 * d * nb + r, CH + 128 * d * (nb + 1), d)
                        if nb == 0:
                            pcols = slice(CH - 128 * d + r, CH, d)
                        else:
                            pcols = slice(CH + 128 * d * (nb - 1) + r, CH + 128 * d * nb, d)
                        psS = psr.next()
                        psSv = psS.t[:, 0:256].rearrange("p (c q) -> p c q", c=2)

                        def smm(e, psSv=psSv, qcols=qcols, ccols=ccols, pcols=pcols):
                            e.matmul(psSv[:, 0, :], lhsT=KT.t[:, 0, pcols], rhs=QT.t[:, 0, qcols], start=True, stop=True)
                            return e.matmul(psSv[:, 1, :], lhsT=KT.t[:, 0, ccols], rhs=QT.t[:, 0, qcols], start=True, stop=True)
                        P.op("pe", smm, [KT.b, QT.b], [psS.b])
                        if pend_b is not None and PIPE_BLOCKS:
                            issue_pv(*pend_b)
                        Pt = Pt_r.next()
                        P.op("act", lambda e, Pt=Pt, psSv=psSv: e.activation(out=Pt.t[:], in_=psSv, func=AF.Exp, scale=SCALE), [psS.b], [Pt.b])
                        mk_ = maskH if nb == 0 else maskA
                        P.op("dve", lambda e, Pt=Pt, mk_=mk_: e.tensor_tensor(out=Pt.t[:], in0=Pt.t[:], in1=mk_.t[:], op=ALU.mult),
                             [Pt.b, mk_.b], [Pt.b])
                        pend_b = (Pt, vblk(r, nb), vblk(r, nb + 1), qcols)
                        blk_ctr[0] += 1
                        if blk_ctr[0] % 5 == 0:
                            prep_step(1)
                        if not PIPE_BLOCKS:
                            issue_pv(*pend_b)
                if PIPE_BLOCKS:
                    issue_pv(*pend_b)
            P.op("dve", lambda e: e.reciprocal(out=acc.t[:, 1], in_=acc.t[:, 1]), [acc.b], [acc.b])
            P.op("dve", lambda e, hp=hp: e.tensor_tensor(out=AT.t[:, hp, :], in0=acc.t[:, 0], in1=acc.t[:, 1], op=ALU.mult), [acc.b], [AT.b])
            for i in range(NS):
                odv = psOD.t[:, 0:2]
                vts = []
                for g, d in enumerate(DILS):
                    kcs = kcs_all[i][g]; vcs = vcs_all[i][g]
                    pt = psbr.next()
                    P.op("pe", lambda e, pt=pt, kcs=kcs: e.transpose(out=pt.t[:, 0:128], in_=kcs.t[:], identity=identb.t[:]), [kcs.b, identb.b], [pt.b])
                    kcT = kcT_r.next()
                    P.op("act", lambda e, pt=pt, kcT=kcT: e.copy(out=kcT.t[:], in_=pt.t[:, 0:128]), [pt.b], [kcT.b])

                    def ssm(e, kcT=kcT, g=g, i=i):
                        e.matmul(psSm.t[:, 2 * g:2 * g + 1], lhsT=kcT.t[:], rhs=QTs.t[:, g, i:i + 1], start=True, stop=True)
                        return e.matmul(psSm.t[0:1, 2 * g + 1:2 * g + 2], lhsT=KTs.t[:, 0, i:i + 1], rhs=QTs.t[:, g, i:i + 1], start=True, stop=True)
                    P.op("pe", ssm, [kcT.b, QTs.b, KTs.b], [psSm.b])
                    vts.append(vcs)

                def sexp(e):
                    e.activation(out=pS.t[:, :, 0], in_=psSm.t[:, 0:6:2], func=AF.Exp, scale=SCALE)
                    return e.activation(out=pS.t[0:1, :, 1], in_=psSm.t[0:1, 1:6:2], func=AF.Exp, scale=SCALE)
                P.op("act", sexp, [psSm.b], [pS.b])

                def spv(e, vts=vts, i=i, odv=odv):
                    for o in range(2):
                        for g in range(3):
                            lc = vts[g].t[:] if o == 0 else onesb.t[:]
                            ls = vrow.t[0:1, i, :] if o == 0 else onesb.t[0:1, :]
                            e.matmul(odv[:, o:o + 1], lhsT=lc, rhs=pS.t[:, g, 0:1], start=(g == 0), stop=False)
                            ins = e.matmul(odv[:, o:o + 1], lhsT=ls, rhs=pS.t[0:1, g, 1:2], start=False, stop=(g == 2))
                    return ins
                P.op("pe", spv, [pS.b, vrow.b, onesb.b] + [v.b for v in vts], [psOD.b])
                rs = ss_r.next()
                P.op("dve", lambda e, rs=rs, odv=odv: e.reciprocal(out=rs.t[:, 0:1], in_=odv[:, 1:2]), [psOD.b], [rs.b])
                P.op("dve", lambda e, rs=rs, odv=odv, hp=hp, i=i: e.tensor_tensor(out=AsT.t[:, hp, i:i + 1], in0=odv[:, 0:1], in1=rs.t[:, 0:1], op=ALU.mult),
                     [psOD.b, rs.b], [AsT.b])
        dbg_out("AT", AT, AT.t[:, 0, 0:256], [128, 256], BF16)
        prep_step(128)
        FREE(hTh, KT, KTs, QT, QTs, qs_tmp, Vg, vrow, acc, Pt_r, kcT_r, pS, wvb)
        for i_ in range(NS):
            FREE(*kcs_all[i_], *vcs_all[i_])
        FREE(ub_r, vb_r, ut_r)
        FREE(H["wb"])
        H["wb"] = Rot([SB(f"wb{i}", [128, 8, 512], BF16) for i in range(2)])

        phase_end(3)
        bT = SB("bT", [128, 4, CH], BF16); bsT = SB("bsT", [128, 4, NS], BF16)
        gluT = SB("gluT", [128, 4, 30 + CH], BF16)
        gluL = SB("gluL", [128, 4, 30])
        Dg = SB("Dg", [128, 4, 31, 128], BF16)
        gluS = SB("gluS", [128, 4, NS]); convS = SB("convS", [128, 4, NS])
        wu = load_wu()
        P.op("dve", lambda e: e.tensor_scalar(out=gluT.t[:, :, 0:30], in0=gluH.t[:, :, 98:128], scalar1=hb.t[:, 0:1], scalar2=None, op0=ALU.mult),
             [gluH.b, hb.b], [gluT.b])
        for st in range(4):
            glu_cols(wu, hTm, slice(st * 512, (st + 1) * 512), 512, lambda j, st=st: (gluT, gluT.t[:, j, 30 + st * 512:30 + (st + 1) * 512]),
                     (lambda j: (gluL, gluL.t[:, j, :])) if st == 3 else None)
        glu_cols(wu, hTs, slice(0, NS), NS, lambda j: (gluS, gluS.t[:, j, :]))
        gst = SB("gst", [NS, 512])
        psa_ = psr.next()
        mm_tm(psa_, NS, hTs, slice(0, NS), wu, 512, 512)
        sgs_ = sg_r.next()
        P.op("act", lambda e: e.activation(out=sgs_.t[:NS, :], in_=psa_.t[:NS, :], func=AF.Sigmoid), [psa_.b], [sgs_.b])
        psb_ = psr.next()
        mm_tm(psb_, NS, hTs, slice(0, NS), wu, 0, 512)
        P.op("dve", lambda e: e.tensor_tensor(out=gst.t[:], in0=psb_.t[:NS, :], in1=sgs_.t[:NS, :], op=ALU.mult), [psb_.b, sgs_.b], [gst.b])
        P.dma("sp", lambda e: e.dma_start(out=cso_o.t.ap()[:, 29, :], in_=gst.t[:]), [gst.b], [cso_o.b])
        FREE(wu, gluH)
        pc = psr.next()

        def trc(e):
            for j in range(4):
                ins = e.transpose(out=pc.t[0:30, j * 128:(j + 1) * 128], in_=gluL.t[:, j, :], identity=identf.t[:])
            return ins
        P.op("pe", trc, [gluL.b, identf.b], [pc.b])
        cvt = SB("cvt", [30, 512])
        P.op("act", lambda e: e.copy(out=cvt.t[:], in_=pc.t[0:30, :]), [pc.b], [cvt.b])
        P.dma("sp", lambda e: e.dma_start(out=cvo_o.t.ap(), in_=cvt.t[:]), [cvt.b], [cvo_o.b])
        st_tm = SB("st_tm", [30, NS, 512]); stT = SB("stT", [128, 4, NS, 31]); stw = SB("stw", [128, 4, NS, 31])
        P.dma("sp", lambda e: e.dma_start(out=st_tm.t[:], in_=stc.t.ap().rearrange("i t c -> t i c")), [stc.b], [st_tm.b])
        for i in range(NS):
            P.dma("sp", lambda e, i=i: e.dma_start(out=cso_o.t.ap()[i, 0:29, :], in_=st_tm.t[1:30, i, :]), [st_tm.b], [cso_o.b])
            pc = psr.next()

            def trs(e, pc=pc, i=i):
                for j in range(4):
                    ins = e.transpose(out=pc.t[:, j * 32:j * 32 + 30], in_=st_tm.t[0:30, i, j * 128:(j + 1) * 128], identity=identf.t[0:30, 0:30])
                return ins
            P.op("pe", trs, [st_tm.b, identf.b], [pc.b])
            pcv = pc.t[:, 0:128].rearrange("p (j t) -> p j t", j=4)
            P.op("act", lambda e, pcv=pcv, i=i: e.copy(out=stT.t[:, :, i, 0:30], in_=pcv[:, :, 0:30]), [pc.b], [stT.b])
        P.op("dve", lambda e: e.tensor_copy(out=stT.t[:, :, :, 30], in_=gluS.t[:]), [gluS.b, stT.b], [stT.b])
        P.op("dve", lambda e: e.tensor_tensor(out=stw.t[:], in0=stT.t[:], in1=wdwT.t[:].unsqueeze(2).to_broadcast([128, 4, NS, 31]), op=ALU.mult),
             [stT.b, wdwT.b], [stw.b])
        P.op("dve", lambda e: e.reduce_sum(out=convS.t[:], in_=stw.t[:], axis=AX.X), [stw.b], [convS.b])
        P.op("dve", lambda e: e.tensor_tensor(out=convS.t[:], in0=convS.t[:], in1=bdwT.t[:].unsqueeze(2).to_broadcast([128, 4, NS]), op=ALU.add),
             [convS.b, bdwT.b], [convS.b])
        sq_t = SB("sq_t", [128, 4, 512]); ln_a = SB("ln_a", [128, 512]); ln_b = SB("ln_b", [128, 512]); ln_c = SB("ln_c", [128, 512])
        convT = SB("convT", [128, 4, 512])

        def ln_swish(src, cols, N, dst, dcols):
            pm = psr.next(); pq = psr.next()
            P.op("act", lambda e: e.activation(out=sq_t.t[:, :, 0:N], in_=src.t[:, :, cols], func=AF.Square), [src.b], [sq_t.b])

            def st_(e):
                for j in range(4):
                    e.matmul(pm.t[:, 0:N], lhsT=onesf.t[:], rhs=src.t[:, j, cols], start=(j == 0), stop=(j == 3))
                for j in range(4):
                    ins = e.matmul(pq.t[:, 0:N], lhsT=onesf.t[:], rhs=sq_t.t[:, j, 0:N], start=(j == 0), stop=(j == 3))
                return ins
            P.op("pe", st_, [src.b, sq_t.b, onesf.b], [pm.b, pq.b])
            P.op("act", lambda e: e.activation(out=ln_a.t[:, 0:N], in_=pm.t[:, 0:N], func=AF.Copy, scale=1.0 / 512), [pm.b], [ln_a.b])
            P.op("dve", lambda e: e.tensor_tensor(out=ln_b.t[:, 0:N], in0=ln_a.t[:, 0:N], in1=ln_a.t[:, 0:N], op=ALU.mult), [ln_a.b], [ln_b.b])
            P.op("dve", lambda e: e.scalar_tensor_tensor(out=ln_b.t[:, 0:N], in0=pq.t[:, 0:N], scalar=1.0 / 512, in1=ln_b.t[:, 0:N], op0=ALU.mult, op1=ALU.subtract),
                 [pq.b, ln_b.b], [ln_b.b])
            P.op("act", lambda e: e.activation(out=ln_b.t[:, 0:N], in_=ln_b.t[:, 0:N], func=AF.Sqrt, bias=EPS, scale=1.0), [ln_b.b], [ln_b.b])
            P.op("dve", lambda e: e.reciprocal(out=ln_b.t[:, 0:N], in_=ln_b.t[:, 0:N]), [ln_b.b], [ln_b.b])
            for j in range(4):
                P.op("dve", lambda e, j=j: e.tensor_tensor(out=ln_c.t[:, 0:N], in0=src.t[:, j, cols], in1=ln_a.t[:, 0:N], op=ALU.subtract), [src.b, ln_a.b], [ln_c.b])
                P.op("dve", lambda e, j=j: e.tensor_tensor(out=ln_c.t[:, 0:N], in0=ln_c.t[:, 0:N], in1=ln_b.t[:, 0:N], op=ALU.mult), [ln_c.b, ln_b.b], [ln_c.b])
                P.op("dve", lambda e, j=j: e.tensor_scalar(out=ln_c.t[:, 0:N], in0=ln_c.t[:, 0:N], scalar1=lngT.t[:, j:j + 1], scalar2=lnbT.t[:, j:j + 1],
                                                           op0=ALU.mult, op1=ALU.add), [ln_c.b, lngT.b, lnbT.b], [ln_c.b])
                P.op("act", lambda e, j=j: e.activation(out=dst.t[:, j, dcols], in_=ln_c.t[:, 0:N], func=AF.Silu), [ln_c.b], [dst.b])
        for j in range(4):
            P.op("dve", lambda e, j=j: e.tensor_tensor(out=Dg.t[:, j], in0=identb.t[:].unsqueeze(1).to_broadcast([128, 31, 128]),
                                                       in1=wdwT.t[:, j, :].unsqueeze(2).to_broadcast([128, 31, 128]), op=ALU.mult),
                 [identb.b, wdwT.b], [Dg.b])
        for st in range(4):
            for j in range(4):
                base = st * 512
                pcv_ = psr.next()

                def cmm(e, pcv_=pcv_, j=j, base=base):
                    for tap in range(31):
                        ins = e.matmul(pcv_.t[:, :], lhsT=Dg.t[:, j, tap, :], rhs=gluT.t[:, j, base + tap:base + tap + 512], start=(tap == 0), stop=(tap == 30))
                    return ins
                P.op("pe", cmm, [Dg.b, gluT.b], [pcv_.b])
                P.op("act", lambda e, pcv_=pcv_, j=j: e.activation(out=convT.t[:, j, :], in_=pcv_.t[:, :], func=AF.Identity, bias=bdwT.t[:, j:j + 1], scale=1.0),
                     [pcv_.b, bdwT.b], [convT.b])
            ln_swish(convT, slice(0, 512), 512, bT, slice(st * 512, (st + 1) * 512))
        ln_swish(convS, slice(0, NS), NS, bsT, slice(0, NS))
        dbg_out("bT", bT, bT.t[:, 0, 0:256], [128, 256], BF16)
        FREE(gluT, gluL, Dg, gluS, convS, cvt, st_tm, stT, stw, gst, sq_t, ln_a, ln_b, ln_c, convT)

        phase_end(4)
        wab = SB("wab", [128, 4, 1024], BF16); wbb = SB("wbb", [128, 4, 1024], BF16); wmb = SB("wmb", [128, 4, 1024], BF16)
        for dst, src in ((wab, w_a), (wbb, w_b), (wmb, w_m)):
            load_w(src, 0, 4, 0, 512, dst, 0); load_w(src, 0, 4, 512, 512, dst, 512)
        MT = SB("MT", [128, 4, CH], BF16); MsT = SB("MsT", [128, 4, NS], BF16)
        qmT = SB("qmT", [128, 4, CH], BF16); qmTs = SB("qmTs", [128, 4, NS], BF16)
        wqm = load_w(w_in, 0, 8, 3584, 512)
        mg_ = [qk_gen(hTm, slice(t * 128, (t + 1) * 128), wqm, 0, 128, 4, qnm, None, 0, None, 0, 0, qmT, t * 128) for t in range(16)]
        mg_.append(qk_gen(hTs, slice(0, NS), wqm, 0, NS, 4, qnm, None, 0, None, 0, 0, qmTs, 0))
        run_pipe(mg_)
        Pm_r = Rot([SB(f"Pm{i}", [128, 2, 512], BF16) for i in range(2)])
        rd_r = Rot([SB(f"rd{i}", [128, 512]) for i in range(2)])

        def mem_attn(keyT_fn, val_fn, qT, cols, N, dst, h, dcols):
            Pm = Pm_r.next()
            pss = [psr.next(), psr.next()]
            for mb in range(2):
                P.op("pe", lambda e, mb=mb: e.matmul(pss[mb].t[:, 0:N], lhsT=keyT_fn(mb)[1], rhs=qT.t[:, h, cols], start=True, stop=True),
                     [keyT_fn(mb)[0].b, qT.b], [pss[mb].b])
                P.op("act", lambda e, mb=mb: e.activation(out=Pm.t[:, mb, 0:N], in_=pss[mb].t[:, 0:N], func=AF.Exp, scale=SCALE), [pss[mb].b], [Pm.b])
            po = psr.next(); pd = psr.next()

            def f(e):
                for mb in range(2):
                    e.matmul(po.t[:, 0:N], lhsT=val_fn(mb)[1], rhs=Pm.t[:, mb, 0:N], start=(mb == 0), stop=(mb == 1))
                for mb in range(2):
                    ins = e.matmul(pd.t[:, 0:N], lhsT=onesb.t[:], rhs=Pm.t[:, mb, 0:N], start=(mb == 0), stop=(mb == 1))
                return ins
            P.op("pe", f, [Pm.b, val_fn(0)[0].b, val_fn(1)[0].b, onesb.b], [po.b, pd.b])
            rd = rd_r.next()
            P.op("dve", lambda e: e.reciprocal(out=rd.t[:, 0:N], in_=pd.t[:, 0:N]), [pd.b], [rd.b])
            P.op("dve", lambda e: e.tensor_tensor(out=dst.t[:, h, dcols], in0=po.t[:, 0:N], in1=rd.t[:, 0:N], op=ALU.mult), [po.b, rd.b], [dst.b])
        for st in range(4):
            for h in range(4):
                cols = slice(st * 512, (st + 1) * 512)
                mem_attn(lambda mb, h=h: (mkT, mkT.t[:, h, mb * 128:(mb + 1) * 128]),
                         lambda mb, h=h: (mvb, mvb.t[:, mb, h * 128:(h + 1) * 128]), qmT, cols, 512, MT, h, cols)
        mks_r = Rot([SB(f"mks{i}", [128, 2, 512], BF16) for i in range(NS)])
        mvs_r = Rot([SB(f"mvs{i}", [128, 2, 512], BF16) for i in range(NS)])
        mksT_r = Rot([SB(f"mksT{i}", [128, 2, 128], BF16) for i in range(2)])
        for i in range(NS):
            mks = mks_r.next(); mvs = mvs_r.next()
            P.dma("pool", lambda e, mks=mks, i=i: e.dma_start(out=mks.t[:], in_=mkc.t.ap()[i].rearrange("(mb p) c -> p mb c", p=128)), [mkc.b], [mks.b])
            P.dma("pool", lambda e, mvs=mvs, i=i: e.dma_start(out=mvs.t[:], in_=mvc.t.ap()[i].rearrange("(mb p) c -> p mb c", p=128)), [mvc.b], [mvs.b])
            for h in range(4):
                pt = psbr.next()

                def trm(e, pt=pt, mks=mks, h=h):
                    for mb in range(2):
                        ins = e.transpose(out=pt.t[:, mb * 128:(mb + 1) * 128], in_=mks.t[:, mb, h * 128:(h + 1) * 128], identity=identb.t[:])
                    return ins
                P.op("pe", trm, [mks.b, identb.b], [pt.b])
                mksT = mksT_r.next()
                P.op("act", lambda e, pt=pt, mksT=mksT: e.copy(out=mksT.t[:], in_=pt.t[:, 0:256].rearrange("p (m k) -> p m k", m=2)), [pt.b], [mksT.b])
                mem_attn(lambda mb, mksT=mksT: (mksT, mksT.t[:, mb, :]),
                         lambda mb, mvs=mvs, h=h: (mvs, mvs.t[:, mb, h * 128:(h + 1) * 128]), qmTs, slice(i, i + 1), 1, MsT, h, slice(i, i + 1))
        dbg_out("MT", MT, MT.t[:, 0, 0:256], [128, 256], BF16)
        FREE(qmT, qmTs, Pm_r, rd_r, mks_r, mvs_r, mksT_r, mkT, mvb)

        phase_end(5)
        mergedT = SB("mergedT", [128, 8, CH], BF16); mergedS = SB("mergedS", [128, 8, NS], BF16)
        mg_r = Rot([SB(f"mg{i}", [128, 512]) for i in range(2)])
        for fc in range(8):
            wg = H["wb"].next()
            for br in range(3):
                load_w(w_in, 0, 8, 4096 + br * 1024 + fc * 128, 128, wg, br * 128)
            for (hT, cols, N, XS, dst) in [(hTm, slice(st * 512, (st + 1) * 512), 512, (AT, bT, MT), mergedT) for st in range(4)] + \
                                          [(hTs, slice(0, NS), NS, (AsT, bsT, MsT), mergedS)]:
                mg = mg_r.next()
                for br in range(3):
                    pg = psr.next()
                    mm_fm(pg, N, wg, br * 128, hT, cols)
                    sg = sg_r.next()
                    P.op("act", lambda e, pg=pg, sg=sg, br=br, N=N, fc=fc: e.activation(out=sg.t[:, 0:N], in_=pg.t[:, 0:N], func=AF.Sigmoid,
                                                                                       bias=bgT.t[:, br * 8 + fc:br * 8 + fc + 1], scale=1.0),
                         [pg.b, bgT.b], [sg.b])
                    py = psr.next()
                    mm_fm(py, N, (wab, wbb, wmb)[br], fc * 128, XS[br], cols, nk=4)
                    if br == 0:
                        P.op("dve", lambda e, py=py, sg=sg, mg=mg, N=N: e.tensor_tensor(out=mg.t[:, 0:N], in0=py.t[:, 0:N], in1=sg.t[:, 0:N], op=ALU.mult),
                             [py.b, sg.b], [mg.b])
                    else:
                        P.op("dve", lambda e, py=py, sg=sg, N=N: e.tensor_tensor(out=sg.t[:, 0:N], in0=py.t[:, 0:N], in1=sg.t[:, 0:N], op=ALU.mult),
                             [py.b, sg.b], [sg.b])
                        if br == 1:
                            P.op("dve", lambda e, sg=sg, mg=mg, N=N: e.tensor_tensor(out=mg.t[:, 0:N], in0=mg.t[:, 0:N], in1=sg.t[:, 0:N], op=ALU.add),
                                 [sg.b, mg.b], [mg.b])
                        else:
                            P.op("dve", lambda e, sg=sg, mg=mg, N=N, dst=dst, cols=cols, fc=fc: e.tensor_tensor(out=dst.t[:, fc, cols], in0=mg.t[:, 0:N], in1=sg.t[:, 0:N], op=ALU.add),
                                 [sg.b, mg.b], [dst.b])
        FREE(hTm, hTs, AT, AsT, bT, bsT, MT, MsT, wab, wbb, wmb, mg_r, H["wb"], sg_r, kf_r, kb_r, cs_r, rA_r, rB_r, vf_r)

        phase_end(6)
        alloc_xbufs()
        wob = SB("wob", [128, 8, 1024], BF16)
        load_w(w_o, 0, 8, 0, 512, wob, 0); load_w(w_o, 0, 8, 512, 512, wob, 512)
        wpq = SB("wpq", [128, 8, 2048], BF16)
        for c in range(4):
            load_w(w_pq, 0, 8, c * 512, 512, wpq, c * 512)
        x1_r = Rot([SB(f"x1_{i}", [128, 1024]) for i in range(2)])
        h2T = SB("h2T", [128, 8, CH + NS], BF16)
        for t in range(17):
            n = 128 if t < 16 else NS
            mT, cols, src_d, r0 = (mergedT, slice(t * 128, (t + 1) * 128), xm, t * 128) if t < 16 else (mergedS, slice(0, NS), xs, 0)
            xt = H["xt"].next()
            P.dma("sp", lambda e, xt=xt, src_d=src_d, r0=r0, n=n: e.dma_start(out=xt.t[:n], in_=src_d.t.ap()[r0:r0 + n, :]), [src_d.b], [xt.b])
            x1 = x1_r.next()
            for half in range(2):
                ps = psr.next()
                mm_tm(ps, n, mT, cols, wob, half * 512, 512)
                P.op("dve", lambda e, ps=ps, xt=xt, x1=x1, half=half, n=n: e.tensor_tensor(out=x1.t[:n, half * 512:(half + 1) * 512], in0=ps.t[:n, :],
                                                                                         in1=xt.t[:n, half * 512:(half + 1) * 512], op=ALU.add),
                     [ps.b, xt.b], [x1.b])
            P.dma("sp", lambda e, x1=x1, t=t, n=n: e.dma_start(out=x1s.t.ap()[t * 128:t * 128 + n, :], in_=x1.t[:n]), [x1.b], [x1s.b])
            if t < 16:
                norm_T(x1, 128, g2T, h2T, t * 128)
            else:
                norm_T(x1, NS, g2T, h2T, CH)
        dbg_out("h2T", h2T, h2T.t[:, 0, 0:256], [128, 256], BF16)
        FREE(mergedT, mergedS, wob)

        phase_end(7)

        phase_end(8)
        NT = CH + NS
        selTall = SB("selTall", [128, 3, NT])
        qpT = SB("qpT", [128, 16, 256], BF16)

        class RS:
            pass
        rsets = []
        for k_ in range(2):
            S = RS()
            S.sc = SB(f"sc{k_}", [128, 16, 128]); S.m16 = SB(f"m16{k_}", [128, 16, 16]); S.i16 = SB(f"i16{k_}", [128, 16, 16], U32)
            S.i16f = SB(f"i16f{k_}", [128, 16, 16]); S.cand = SB(f"cand{k_}", [128, 8, 16, 16]); S.cm = SB(f"cm{k_}", [128, 8, 16])
            S.ci = SB(f"ci{k_}", [128, 8, 16], U32); S.cia = SB(f"cia{k_}", [128, 8, 16], U32); S.cib = SB(f"cib{k_}", [128, 8, 16], U32)
            S.ciaf = SB(f"ciaf{k_}", [128, 8, 16]); S.cibf = SB(f"cibf{k_}", [128, 8, 16]); S.eq = SB(f"eq{k_}", [128, 8, 16, 16])
            S.sel = SB(f"sel{k_}", [128, 3, 128]); S.gsum = SB(f"gsum{k_}", [128, 8])
            rsets.append(S)

        def route_tile(S, t0, n, g0):
            sc, m16, i16, i16f, cand, cm, ci = S.sc, S.m16, S.i16, S.i16f, S.cand, S.cm, S.ci
            cia, cib, ciaf, cibf, eq, sel, gsum = S.cia, S.cib, S.ciaf, S.cibf, S.eq, S.sel, S.gsum
            for q4 in range(4):
                ps = psr.next()

                def smm(e, ps=ps, q4=q4):
                    for c in range(4):
                        c16 = q4 * 4 + c
                        ins = e.matmul(ps.t[:n, c * 128:(c + 1) * 128], lhsT=qpT.t[:, c16, t0:t0 + n], rhs=skT.t[:, c16 % 2, :], start=True, stop=True)
                    return ins
                P.op("pe", smm, [qpT.b, skT.b], [ps.b])
                P.op("act", lambda e, ps=ps, q4=q4: e.copy(out=sc.t[:n, q4 * 4:(q4 + 1) * 4, :], in_=ps.t[:n, :].rearrange("p (c k) -> p c k", c=4)),
                     [ps.b], [sc.b])
            yield

            def tk_a(e):
                for c16 in range(16):
                    ins = e.max(out=m16.t[:n, c16, 0:8], in_=sc.t[:n, c16, :])
                return ins
            P.op("dve", tk_a, [sc.b], [m16.b])
            yield

            def tk_b(e):
                for c16 in range(16):
                    ins = e.max_index(out=i16.t[:n, c16, 0:8], in_max=m16.t[:n, c16, 0:8], in_values=sc.t[:n, c16, :])
                return ins
            P.op("dve", tk_b, [sc.b, m16.b], [i16.b])
            yield

            def tk_b2(e):
                for c16 in range(16):
                    ins = e.match_replace(out=sc.t[:n, c16, :], in_to_replace=m16.t[:n, c16, 0:8], in_values=sc.t[:n, c16, :], imm_value=-1e30)
                return ins
            P.op("dve", tk_b2, [sc.b, m16.b], [sc.b])
            yield

            def tk_c(e):
                for c16 in range(16):
                    ins = e.max(out=m16.t[:n, c16, 8:16], in_=sc.t[:n, c16, :])
                return ins
            P.op("dve", tk_c, [sc.b, m16.b], [m16.b])
            yield

            def tk_d(e):
                for c16 in range(16):
                    ins = e.max_index(out=i16.t[:n, c16, 8:16], in_max=m16.t[:n, c16, 8:16], in_values=sc.t[:n, c16, :])
                return ins
            P.op("dve", tk_d, [sc.b, m16.b, i16.b], [i16.b])
            yield
            m16v = m16.t[:n].rearrange("p (h c) k -> p h c k", c=2)
            i16fv = i16f.t[:n].rearrange("p (h c) k -> p h c k", c=2)

            def cf_a(e):
                e.tensor_copy(out=i16f.t[:n], in_=i16.t[:n])
                return e.tensor_tensor(out=cand.t[:n], in0=m16v[:, :, 0, :].unsqueeze(3).to_broadcast([n, 8, 16, 16]),
                                       in1=m16v[:, :, 1, :].unsqueeze(2).to_broadcast([n, 8, 16, 16]), op=ALU.add)
            P.op("dve", cf_a, [m16.b, i16.b], [i16f.b, cand.b])
            yield
            cvs = [cand.t[:n, h].rearrange("p a b -> p (a b)") for h in range(8)]

            def cf_b(e):
                for h in range(8):
                    ins = e.max(out=cm.t[:n, h, 0:8], in_=cvs[h])
                return ins
            P.op("dve", cf_b, [cand.b], [cm.b])
            yield

            def cf_c(e):
                for h in range(8):
                    ins = e.max_index(out=ci.t[:n, h, 0:8], in_max=cm.t[:n, h, 0:8], in_values=cvs[h])
                return ins
            P.op("dve", cf_c, [cand.b, cm.b], [ci.b])
            yield

            def cf_c2(e):
                for h in range(8):
                    ins = e.match_replace(out=cvs[h], in_to_replace=cm.t[:n, h, 0:8], in_values=cvs[h], imm_value=-1e30)
                return ins
            P.op("dve", cf_c2, [cand.b, cm.b], [cand.b])
            yield

            def cf_d(e):
                for h in range(8):
                    ins = e.max(out=cm.t[:n, h, 8:16], in_=cvs[h])
                return ins
            P.op("dve", cf_d, [cand.b, cm.b], [cm.b])
            yield

            def cf_e(e):
                for h in range(8):
                    ins = e.max_index(out=ci.t[:n, h, 8:16], in_max=cm.t[:n, h, 8:16], in_values=cvs[h])
                return ins
            P.op("dve", cf_e, [cand.b, cm.b, ci.b], [ci.b])
            yield

            def cf_f(e):
                e.tensor_single_scalar(out=cia.t[:n], in_=ci.t[:n], scalar=4, op=ALU.logical_shift_right)
                return e.tensor_single_scalar(out=cib.t[:n], in_=ci.t[:n], scalar=15, op=ALU.bitwise_and)
            P.op("dve", cf_f, [ci.b], [cia.b, cib.b])
            yield

            def cf_g(e):
                e.tensor_copy(out=ciaf.t[:n], in_=cia.t[:n])
                return e.tensor_copy(out=cibf.t[:n], in_=cib.t[:n])
            P.op("dve", cf_g, [cia.b, cib.b], [ciaf.b, cibf.b])
            yield
            io16 = iota.t[:n, 0:16].unsqueeze(1).unsqueeze(1).to_broadcast([n, 8, 16, 16])
            for w, cf in ((0, ciaf), (1, cibf)):
                P.op("dve", lambda e, cf=cf: e.tensor_tensor(out=eq.t[:n], in0=cf.t[:n].unsqueeze(3).to_broadcast([n, 8, 16, 16]), in1=io16, op=ALU.is_equal),
                     [cf.b, iota.b], [eq.b])
                yield
                P.op("dve", lambda e, w=w: e.tensor_tensor(out=eq.t[:n], in0=eq.t[:n], in1=i16fv[:, :, w, :].unsqueeze(2).to_broadcast([n, 8, 16, 16]), op=ALU.mult),
                     [eq.b, i16f.b], [eq.b])
                yield
                P.op("dve", lambda e, w=w: e.reduce_sum(out=sel.t[:n, w, :].rearrange("p (h k) -> p h k", h=8), in_=eq.t[:n], axis=AX.X),
                     [eq.b], [sel.b])
                yield
            gv = sel.t[:n, 2, :].rearrange("p (h k) -> p h k", h=8)
            P.op("dve", lambda e: e.tensor_tensor(out=gv, in0=cm.t[:n], in1=cm.t[:n, :, 0:1].to_broadcast([n, 8, 16]), op=ALU.subtract),
                 [cm.b, sel.b], [sel.b])
            yield
            P.op("act", lambda e: e.activation(out=gv, in_=gv, func=AF.Exp), [sel.b], [sel.b])
            yield
            P.op("dve", lambda e: e.reduce_sum(out=gsum.t[:n], in_=gv, axis=AX.X), [sel.b], [gsum.b])
            yield
            P.op("dve", lambda e: e.reciprocal(out=gsum.t[:n], in_=gsum.t[:n]), [gsum.b], [gsum.b])
            yield
            P.op("dve", lambda e: e.tensor_tensor(out=gv, in0=gv, in1=gsum.t[:n].unsqueeze(2).to_broadcast([n, 8, 16]), op=ALU.mult),
                 [sel.b, gsum.b], [sel.b])
            yield
            pT = psr.next()

            def trsel(e):
                for w in range(3):
                    ins = e.transpose(out=pT.t[:, w * 128:w * 128 + n], in_=sel.t[:n, w, :], identity=identf.t[:n, :n])
                return ins
            P.op("pe", trsel, [sel.b, identf.b], [pT.b])
            P.op("act", lambda e: e.copy(out=selTall.t[:, :, g0 + t0:g0 + t0 + n],
                                         in_=pT.t[:, 0:384].rearrange("p (w t) -> p w t", w=3)[:, :, 0:n]), [pT.b], [selTall.b])

        def route_block(hT, c0, ntok, tiles, g0):
            cols = slice(c0, c0 + ntok)
            for c16 in range(16):
                ps = psr.next()
                mm_fm(ps, ntok, wpq, c16 * 128, hT, cols)
                if c16 % 2 == 0:
                    P.op("act", lambda e, ps=ps, c16=c16: e.copy(out=qpT.t[:, c16, 0:ntok], in_=ps.t[:, 0:ntok]), [ps.b], [qpT.b])
                else:
                    P.op("dve", lambda e, ps=ps, c16=c16: e.tensor_copy(out=qpT.t[:, c16, 0:ntok], in_=ps.t[:, 0:ntok]), [ps.b], [qpT.b])
            run_pipe([route_tile(rsets[k_], t0, n, g0) for k_, (t0, n) in enumerate(tiles)], depth=2)

        for sp_ in range(8):
            route_block(h2T, sp_ * 256, 256, [(0, 128), (128, 128)], sp_ * 256)
        route_block(h2T, CH, NS, [(0, NS)], CH)
        dbg_out("selT", selTall, selTall.t[:, :, 0:128], [128, 3, 128])
        FREE(wpq, qpT)
        for S in rsets:
            FREE(S.sc, S.m16, S.i16, S.i16f, S.cand, S.cm, S.ci, S.cia, S.cib, S.ciaf, S.cibf, S.eq, S.sel, S.gsum)

        phase_end(9)
        ut_r = Rot([SB(f"utx{i}", [128, 1024], BF16) for i in range(4)])
        TwH = [SB("TwA", [128, 256 + NS, 64], BF16), SB("TwB", [128, 256 + NS, 64], BF16)]
        selTb = SB("selTb", [128, 3, NT], BF16)
        P.op("dve", lambda e: e.tensor_copy(out=selTb.t[:], in_=selTall.t[:]), [selTall.b], [selTb.b])
        iotab = SB("iotab", [128, 128], BF16)
        P.op("dve", lambda e: e.tensor_copy(out=iotab.t[:], in_=iota.t[:]), [iota.b], [iotab.b])
        FREE(selTall)
        NOH = 8
        ohJ_r = Rot([SB(f"ohJ{i}", [128, 4, 128], BF16) for i in range(NOH)])
        ohI_r = Rot([SB(f"ohI{i}", [128, 4, 64], BF16) for i in range(NOH)])
        vt_r = Rot([SB(f"vt{i}", [128, 1024], BF16) for i in range(4)])
        ge_r = Rot([SB(f"ge{i}", [128, 256 + NS]) for i in range(3)])
        hg_r = Rot([SB(f"hg{i}", [128, 256 + NS], BF16) for i in range(3)])

        class FV:
            def __init__(self, tn):
                self.b = tn.b
                self.v = tn.t[:].bitcast(F32)

        def oap(o, n):
            return o.v[:n, :] if isinstance(o, FV) else o.t[:n, :]
        fvb = [FV(PSB[0]), FV(PSB[1])]
        outs_all = [(PS[0], PS[1]), (PS[2], PS[3]), (fvb[0], fvb[1])]
        psWf_r = Rot(fvb)
        psA_r = Rot([PS[4], PS[5]])
        BDELAY = 3

        def build_gen(blk, half):
            hT, c0, ntok, tiles, g0, dests = blk
            Tw = TwH[half]
            pendq = []

            def emit_pe(ohJ, ohI, tb, nb_):
                psW = psWf_r.next()
                pwv = psW.v

                def wmm(e):
                    for u in range(nb_):
                        ins = e.matmul(pwv[:, u * 64:(u + 1) * 64], lhsT=ohJ.t[:, u, :], rhs=ohI.t[:, u, :], start=True, stop=True)
                    return ins
                P.op("pe", wmm, [ohJ.b, ohI.b], [psW.b])
                P.op("act", lambda e: e.copy(out=Tw.t[:, tb:tb + nb_, :], in_=pwv[:, 0:nb_ * 64].rearrange("p (u i) -> p u i", u=nb_)), [psW.b], [Tw.b])
            for tb in range(0, ntok, 4):
                nb_ = min(4, ntok - tb)
                ohJ = ohJ_r.next(); ohI = ohI_r.next()
                tk0 = g0 + tb
                P.op("dve", lambda e, ohI=ohI, tk0=tk0, nb_=nb_: e.tensor_tensor(
                    out=ohI.t[:, 0:nb_, :], in0=iotab.t[:, half * 64:(half + 1) * 64].unsqueeze(1).to_broadcast([128, nb_, 64]),
                    in1=selTb.t[:, 0, tk0:tk0 + nb_].unsqueeze(2).to_broadcast([128, nb_, 64]), op=ALU.is_equal),
                    [iotab.b, selTb.b], [ohI.b])
                P.op("dve", lambda e, ohJ=ohJ, tk0=tk0, nb_=nb_: e.tensor_tensor(
                    out=ohJ.t[:, 0:nb_, :], in0=iotab.t[:].unsqueeze(1).to_broadcast([128, nb_, 128]),
                    in1=selTb.t[:, 1, tk0:tk0 + nb_].unsqueeze(2).to_broadcast([128, nb_, 128]), op=ALU.is_equal),
                    [iotab.b, selTb.b], [ohJ.b])
                P.op("dve", lambda e, ohI=ohI, tk0=tk0, nb_=nb_: e.tensor_tensor(
                    out=ohI.t[:, 0:nb_, :], in0=ohI.t[:, 0:nb_, :], in1=selTb.t[:, 2, tk0:tk0 + nb_].unsqueeze(2).to_broadcast([128, nb_, 64]), op=ALU.mult),
                    [ohI.b, selTb.b], [ohI.b])
                pendq.append((ohJ, ohI, tb, nb_))
                if len(pendq) > BDELAY:
                    emit_pe(*pendq.pop(0))
                yield
            while pendq:
                emit_pe(*pendq.pop(0))
            yield

        def loop_gen(blk):
            hT, c0, ntok, tiles, g0, dests = blk
            cols = slice(c0, c0 + ntok)
            outs = outs_all
            pend = None

            def issue_omm(i, hg, vt):
                def omm(e, hg=hg, vt=vt, i=i):
                    for ti, (t0, n) in enumerate(tiles):
                        for half in range(2):
                            ins = e.matmul(oap(outs[ti][half], n), lhsT=hg.t[:, t0:t0 + n], rhs=vt.t[:, half * 512:(half + 1) * 512],
                                           start=(i == 0), stop=(i == 127))
                    return ins
                P.op("pe", omm, [hg.b, vt.b], [outs[ti][half].b for ti in range(len(tiles)) for half in range(2)])
            for i in range(128):
                ut = ut_r.next(); vt = vt_r.next()
                P.dma("sp", lambda e, ut=ut, i=i: e.dma_start(out=ut.t[:], in_=uTs.t.ap()[i]), [uTs.b], [ut.b])
                P.dma("act" if i % 2 == 0 else "pool", lambda e, vt=vt, i=i: e.dma_start(out=vt.t[:], in_=vbs.t.ap()[i]), [vbs.b], [vt.b])
                psA = psA_r.next()

                def amm(e, ut=ut, psA=psA):
                    for kc in range(8):
                        ins = e.matmul(psA.t[:, 0:ntok], lhsT=ut.t[:, kc * 128:(kc + 1) * 128], rhs=hT.t[:, kc, cols], start=(kc == 0), stop=(kc == 7))
                    return ins
                P.op("pe", amm, [ut.b, hT.b], [psA.b])
                if pend is not None:
                    issue_omm(*pend)
                ge = ge_r.next(); hg = hg_r.next()
                Tw = TwH[i // 64]
                P.op("act", lambda e, ge=ge, psA=psA: e.activation(out=ge.t[:, 0:ntok], in_=psA.t[:, 0:ntok], func=AF.Gelu_apprx_tanh), [psA.b], [ge.b])
                P.op("dve", lambda e, ge=ge, hg=hg, i=i, Tw=Tw: e.tensor_tensor(out=hg.t[:, 0:ntok], in0=ge.t[:, 0:ntok], in1=Tw.t[:, 0:ntok, i % 64], op=ALU.mult),
                     [ge.b, Tw.b], [hg.b])
                pend = (i, hg, vt)
                yield
            issue_omm(*pend)
            for ti, (t0, n) in enumerate(tiles):
                xt = H["xt"].next(); x1 = x1_r.next()
                P.dma("sp", lambda e, xt=xt, t0=t0, n=n: e.dma_start(out=xt.t[:n], in_=x1s.t.ap()[g0 + t0:g0 + t0 + n, :]), [x1s.b], [xt.b])
                for half in range(2):
                    P.op("dve", lambda e, xt=xt, x1=x1, ti=ti, half=half, n=n: e.tensor_tensor(out=x1.t[:n, half * 512:(half + 1) * 512], in0=oap(outs[ti][half], n),
                                                                                             in1=xt.t[:n, half * 512:(half + 1) * 512], op=ALU.add),
                         [outs[ti][half].b, xt.b], [x1.b])
                out_d, orow = dests[ti]
                P.dma("sp", lambda e, x1=x1, n=n, out_d=out_d, orow=orow: e.dma_start(out=out_d.t.ap()[orow:orow + n, :], in_=x1.t[:n]), [x1.b], [out_d.b])
            yield

        def drain(g_):
            if g_ is None:
                return
            for _ in g_:
                pass

        def step(g_):
            if g_ is None:
                return
            try:
                next(g_)
            except StopIteration:
                pass
        blocks = [(h2T, sp_ * 256, 256, [(0, 128), (128, 128)], sp_ * 256, [(y_o, sp_ * 256), (y_o, sp_ * 256 + 128)]) for sp_ in range(7)]
        blocks.append((h2T, 7 * 256, 256 + NS, [(0, 128), (128, 128), (256, NS)], 7 * 256, [(y_o, 7 * 256), (y_o, 7 * 256 + 128), (ys_o, 0)]))
        drain(build_gen(blocks[0], 0))
        for bi, blk in enumerate(blocks):
            last = (bi == len(blocks) - 1)
            bB = build_gen(blk, 1)
            if last:
                drain(bB)
                bB = None
            lg = loop_gen(blk)
            bA = build_gen(blocks[bi + 1], 0) if not last else None
            for i in range(128):
                if i == 64:
                    drain(bB)
                next(lg)
                step(bB if i < 64 else bA)
            drain(lg)
            drain(bA)


    try:
        body()
    except _Stop:
        pass
    P.finish()
    return nc, list(dram_out.keys())


_CACHE = {}


def _consts():
    half = 16
    inv = (np.float32(500000.0) ** (-np.arange(half, dtype=np.float32) / np.float32(half))).astype(np.float32)

    def cs(pos):
        ang = (pos.astype(np.float32)[:, None] * inv[None, :]).astype(np.float32)
        c = np.cos(ang).astype(np.float32); s = np.sin(ang).astype(np.float32)
        return np.ascontiguousarray(np.concatenate([c, c, s, s], axis=1))
    ki = np.arange(128)[:, None]; qi = np.arange(128)[None, :]
    return dict(cs=cs, mprev=(ki >= qi).astype(np.float32), mcur=(ki <= qi).astype(np.float32),
                ident=np.eye(128, dtype=np.float32),
                iota=np.ascontiguousarray(np.broadcast_to(np.arange(128, dtype=np.float32), (128, 128))))


def _colT(v, n):
    return np.ascontiguousarray(np.asarray(v, np.float32).reshape(n, 128).T)


def make_in_maps(inp):
    C = _consts()
    f = lambda a: np.ascontiguousarray(np.asarray(a, dtype=np.float32))
    shared = dict(
        w_in=f(inp["w_in"][0]), w_mem=f(inp["w_mem_kv"][0]), w_a=f(inp["w_a_proj"][0]), w_b=f(inp["w_b_proj"][0]),
        w_m=f(inp["w_m_proj"][0]), w_o=f(inp["w_o"][0]), w_pq=f(inp["w_pq"][0]), u_tab=f(inp["u_tab"][0]), v_tab=f(inp["v_tab"][0]),
        skT=np.ascontiguousarray(np.asarray(inp["sub_keys"][0], np.float32).transpose(2, 0, 1)),
        g1T=_colT(inp["g_norm1"][0], 8), g2T=_colT(inp["g_norm2"][0], 8), gmT=_colT(inp["g_mem"][0], 8),
        bgT=_colT(inp["b_gate"][0], 24), bdwT=_colT(inp["b_dw"][0], 4), lngT=_colT(inp["ln_g"][0], 4), lnbT=_colT(inp["ln_b"][0], 4),
        wdwT=np.ascontiguousarray(np.asarray(inp["w_dw"][0], np.float32).T.reshape(4, 128, 31).transpose(1, 0, 2)),
        qna_bc=np.ascontiguousarray(np.broadcast_to(np.asarray(inp["qn_a"][0], np.float32), (128, 128))),
        kna_bc=np.ascontiguousarray(np.broadcast_to(np.asarray(inp["kn_a"][0], np.float32), (128, 128))),
        qnm_bc=np.ascontiguousarray(np.broadcast_to(np.asarray(inp["qn_m"][0], np.float32), (128, 128))),
        knm_bc=np.ascontiguousarray(np.broadcast_to(np.asarray(inp["kn_m"][0], np.float32), (128, 128))),
        ident=C["ident"], iota=C["iota"], mprev=C["mprev"], mcur=C["mcur"],
        css=C["cs"](np.full((NS,), 16384, dtype=np.int64)),
    )
    xp = np.asarray(inp["x_prompt"], np.float32)
    maps = []
    for c in range(NCORES):
        b, ch = c // 4, c % 4
        c0 = ch * CH
        m = dict(shared)
        m["xm"] = np.ascontiguousarray(xp[b, c0:c0 + CH])
        m["xh"] = np.ascontiguousarray(xp[b, c0 - CH:c0]) if ch > 0 else np.zeros((CH, 1024), np.float32)
        m["xs"] = np.ascontiguousarray(np.asarray(inp["x_sample"], np.float32)[c * NS:(c + 1) * NS, 0])
        m["memx"] = np.ascontiguousarray(np.asarray(inp["mem_prompt"], np.float32)[b])
        m["wkc"] = np.ascontiguousarray(np.asarray(inp["cache_win_k"], np.float32)[0, c * NS:(c + 1) * NS].reshape(NS, 2048, 512))
        m["wvc"] = np.ascontiguousarray(np.asarray(inp["cache_win_v"], np.float32)[0, c * NS:(c + 1) * NS].reshape(NS, 2048, 512))
        m["mkc"] = np.ascontiguousarray(np.asarray(inp["cache_mem_k"], np.float32)[0, c * NS:(c + 1) * NS].reshape(NS, 256, 512))
        m["mvc"] = np.ascontiguousarray(np.asarray(inp["cache_mem_v"], np.float32)[0, c * NS:(c + 1) * NS].reshape(NS, 256, 512))
        m["stc"] = np.ascontiguousarray(np.asarray(inp["state_conv"], np.float32)[0, c * NS:(c + 1) * NS])
        m["hb"] = np.full((128, 1), 1.0 if ch > 0 else 0.0, np.float32)
        m["csm"] = C["cs"](np.arange(c0, c0 + CH))
        m["csh"] = C["cs"](np.maximum(np.arange(c0 - CH, c0), 0))
        maps.append(m)
    return maps


def assemble(res):
    y = np.zeros((2, 8192, 1024), np.float32); ys = np.zeros((32, 1, 1024), np.float32)
    wk = np.zeros((1, 2, 2048, 4, 128), np.float32); wv = np.zeros_like(wk)
    mk = np.zeros((1, 2, 256, 4, 128), np.float32); mv = np.zeros_like(mk)
    cv = np.zeros((1, 2, 30, 512), np.float32)
    ks = np.zeros((1, 32, 1, 4, 128), np.float32); vs = np.zeros_like(ks); cs = np.zeros((1, 32, 30, 512), np.float32)
    for c in range(NCORES):
        r = res[c]
        b, ch = c // 4, c % 4
        y[b, ch * CH:(ch + 1) * CH] = r["y"]
        ys[c * NS:(c + 1) * NS, 0] = r["ys"]
        if ch == 3:
            wk[0, b] = r["ko"].reshape(2048, 4, 128); wv[0, b] = r["vo"].reshape(2048, 4, 128)
            cv[0, b] = r["cvo"]
        if ch == 0:
            mk[0, b] = r["mko"].reshape(256, 4, 128); mv[0, b] = r["mvo"].reshape(256, 4, 128)
        ks[0, c * NS:(c + 1) * NS, 0] = r["kso"].reshape(NS, 4, 128)
        vs[0, c * NS:(c + 1) * NS, 0] = r["vso"].reshape(NS, 4, 128)
        cs[0, c * NS:(c + 1) * NS] = r["cso"]
    return (y, ys, wk, wv, mk, mv, cv, ks, vs, cs)


def kernel(**inputs):
    if "nc" not in _CACHE:
        _CACHE["nc"] = build_program()[0]
    nc = _CACHE["nc"]
    maps = make_in_maps(inputs)
    res = run_bass_kernel_spmd(nc, maps, core_ids=list(range(NCORES)))
    return assemble(res.results)
```

```python
import numpy as np
import concourse.bass as bass
import concourse.mybir as mybir
from concourse.bass_utils import run_bass_kernel_spmd

F32 = mybir.dt.float32
BF16 = mybir.dt.bfloat16
U32 = mybir.dt.uint32
AF = mybir.ActivationFunctionType
ALU = mybir.AluOpType
AX = mybir.AxisListType

EPS = 1e-6
NCORES = 8
CH = 2048
NS = 4
SCALE = 128 ** -0.5
DEBUG = {}
PIPE_BLOCKS = True


class Buf:
    __slots__ = ("w", "r", "excl")

    def __init__(self):
        self.w = {}
        self.r = []
        self.excl = False


class Prog:
    NDMA = 12

    def __init__(self, nc):
        self.nc = nc
        self.names = ("pe", "act", "dve", "pool", "sp")
        self.lists = {k: [] for k in self.names}
        self.cnt = {k: 0 for k in self.names}
        self.sems = {}
        self._stack = []
        for k in self.names:
            self.sems[k] = self._sem("c_" + k)
        self.dq = {}
        for q in ("sp", "act", "pool"):
            self.dq[q] = {"sems": [self._sem(f"d_{q}{i}") for i in range(self.NDMA)],
                          "uses": [0] * self.NDMA, "n": 0, "last": [None] * self.NDMA}
        self.known = {k: {} for k in self.names}

    def _sem(self, name):
        cm = self.nc.semaphore(name)
        s = cm.__enter__()
        self._stack.append(cm)
        return s

    def _collect(self, eng, reads, writes):
        deps = {}

        def add(ev):
            if ev is None:
                return
            s, v = ev
            if deps.get(s.name, (None, -1))[1] < v:
                deps[s.name] = (s, v)
        for b in reads:
            for e in b.w.values():
                add(e)
        for b in writes:
            for e in b.w.values():
                add(e)
            for e in b.r:
                add(e)
        out = []
        kn = self.known[eng]
        for name, (s, v) in deps.items():
            if eng == "pe" and name == "c_pe":
                continue
            if kn.get(name, -1) >= v:
                continue
            kn[name] = v
            out.append((s, v))
        return out

    def _commit(self, ev, reads, writes):
        for b in reads:
            b.r.append(ev)
            if len(b.r) > 64:
                b.r = b.r[-48:] if False else b.r
        for b in writes:
            b.w[ev[0].name] = ev
            b.r = []

    def op(self, eng, fn, reads=(), writes=()):
        ex = [b for b in reads if b.excl]
        if ex:
            writes = list(writes) + ex
            reads = [b for b in reads if not b.excl]
        waits = self._collect(eng, reads, writes)
        self.cnt[eng] += 1
        ev = (self.sems[eng], self.cnt[eng])
        self.lists[eng].append((waits, fn, self.sems[eng], 1))
        self._commit(ev, reads, writes)
        return ev

    def dma(self, q, fn, reads=(), writes=()):
        d = self.dq[q]
        i = d["n"] % self.NDMA
        d["n"] += 1
        waits = self._collect(q, reads, writes)
        prev = d["last"][i]
        if prev is not None:
            kn = self.known[q]
            if kn.get(prev[0].name, -1) < prev[1]:
                kn[prev[0].name] = prev[1]
                waits.append(prev)
        d["uses"][i] += 1
        ev = (d["sems"][i], 16 * d["uses"][i])
        d["last"][i] = ev
        self.lists[q].append((waits, fn, d["sems"][i], 16))
        self._commit(ev, reads, writes)
        return ev

    def finish(self):
        fin = []
        for q, d in self.dq.items():
            for ev in d["last"]:
                if ev is not None:
                    fin.append(ev)
        engs = {"pe": "tensor", "act": "scalar", "dve": "vector", "pool": "gpsimd", "sp": "sync"}
        with self.nc.Block() as block:
            def mk(name):
                def body(e):
                    for waits, fn, sem, amt in self.lists[name]:
                        for (s, v) in waits:
                            e.wait_ge(s, v)
                        ins = fn(e)
                        ins.then_inc(sem, amt)
                    if name == "sp":
                        for (s, v) in fin:
                            e.wait_ge(s, v)
                return body
            for name in ("sp", "act", "dve", "pool", "pe"):
                getattr(block, engs[name])(mk(name))
        for cm in reversed(self._stack):
            cm.__exit__(None, None, None)
        self._stack = []


class TN:
    def __init__(self, t):
        self.t = t
        self.b = Buf()


class Rot:
    def __init__(self, items):
        self.items = items
        self.i = 0

    def next(self):
        x = self.items[self.i % len(self.items)]
        self.i += 1
        return x


class _Stop(Exception):
    pass


def build_program(dbg=(), stop=None):
    nc = bass.Bass("TRN2", target_bir_lowering=False)
    P = Prog(nc)
    dram_in = {}
    dram_out = {}

    def phase_end(k):
        if stop == k:
            raise _Stop()

    def body():
        def DI(name, shape, dt=F32):
            dram_in[name] = TN(nc.dram_tensor(name, list(shape), dt, kind="ExternalInput"))
            return dram_in[name]

        def DO(name, shape, dt=F32):
            dram_out[name] = TN(nc.dram_tensor(name, list(shape), dt, kind="ExternalOutput"))
            return dram_out[name]

        SB_LO, SB_HI = 16512, 229344
        free_list = [[SB_LO, SB_HI]]
        grave = []
        live = {}
        _uid = [0]

        def SB(name, shape, dt=F32):
            nbytes = int(np.prod(shape[1:])) * (4 if dt in (F32, U32) else 2)
            nbytes = (nbytes + 31) // 32 * 32
            big = nbytes >= 8192
            for fr in (reversed(free_list) if big else free_list):
                if fr[1] - fr[0] >= nbytes:
                    if big:
                        fr[1] -= nbytes
                        off = fr[1]
                    else:
                        off = fr[0]
                        fr[0] += nbytes
                    break
            else:
                raise RuntimeError(f"SBUF arena full allocating {name} {shape}: free={free_list}")
            _uid[0] += 1
            tn = TN(nc.alloc_sbuf_tensor_at(f"{name}_{_uid[0]}", list(shape), dt, offset=off))
            for (gs, ge, gb) in grave:
                if gs < off + nbytes and off < ge:
                    tn.b.r.extend(gb.w.values())
                    tn.b.r.extend(gb.r)
            live[id(tn)] = (off, off + nbytes)
            return tn

        def FREE(*tns):
            for tn in tns:
                if isinstance(tn, Rot):
                    FREE(*tn.items)
                    continue
                a, b = live.pop(id(tn))
                grave.append((a, b, tn.b))
                free_list.append([a, b])
            free_list.sort()
            merged = []
            for fr in free_list:
                if fr[1] <= fr[0]:
                    continue
                if merged and merged[-1][1] == fr[0]:
                    merged[-1][1] = fr[1]
                else:
                    merged.append(fr)
            free_list[:] = merged

        xm = DI("xm", [CH, 1024]); xh = DI("xh", [CH, 1024]); xs = DI("xs", [NS, 1024]); memx = DI("memx", [256, 1024])
        wkc = DI("wkc", [NS, 2048, 512]); wvc = DI("wvc", [NS, 2048, 512])
        mkc = DI("mkc", [NS, 256, 512]); mvc = DI("mvc", [NS, 256, 512]); stc = DI("stc", [NS, 30, 512])
        w_in = DI("w_in", [1024, 7168]); w_mem = DI("w_mem", [1024, 1024])
        w_a = DI("w_a", [512, 1024]); w_b = DI("w_b", [512, 1024]); w_m = DI("w_m", [512, 1024])
        w_o = DI("w_o", [1024, 1024]); w_pq = DI("w_pq", [1024, 2048])
        u_tab = DI("u_tab", [16384, 1024]); v_tab = DI("v_tab", [16384, 1024])
        skT_d = DI("skT", [128, 2, 128])
        g1T_d = DI("g1T", [128, 8]); g2T_d = DI("g2T", [128, 8]); gmT_d = DI("gmT", [128, 8])
        bgT_d = DI("bgT", [128, 24]); bdwT_d = DI("bdwT", [128, 4]); lngT_d = DI("lngT", [128, 4]); lnbT_d = DI("lnbT", [128, 4])
        wdwT_d = DI("wdwT", [128, 4, 31])
        qna_d = DI("qna_bc", [128, 128]); kna_d = DI("kna_bc", [128, 128]); qnm_d = DI("qnm_bc", [128, 128]); knm_d = DI("knm_bc", [128, 128])
        ident_d = DI("ident", [128, 128]); iota_d = DI("iota", [128, 128])
        mprev_d = DI("mprev", [128, 128]); mcur_d = DI("mcur", [128, 128]); hb_d = DI("hb", [128, 1])
        csm_d = DI("csm", [CH, 64]); csh_d = DI("csh", [CH, 64]); css_d = DI("css", [NS, 64])

        y_o = DO("y", [CH, 1024]); ys_o = DO("ys", [NS, 1024])
        ko_o = DO("ko", [CH, 512]); vo_o = DO("vo", [CH, 512])
        mko_o = DO("mko", [256, 512]); mvo_o = DO("mvo", [256, 512]); cvo_o = DO("cvo", [30, 512])
        kso_o = DO("kso", [NS, 512]); vso_o = DO("vso", [NS, 512]); cso_o = DO("cso", [NS, 30, 512])
        x1s = TN(nc.dram_tensor("x1s", [CH + NS, 1024], F32))
        uTs = TN(nc.dram_tensor("uTs", [128, 128, 1024], BF16))
        vbs = TN(nc.dram_tensor("vbs", [128, 128, 1024], BF16))

        def dbg_out(name, tn, ap, shape, dt=F32):
            if name in dbg:
                o = DO("dbg_" + name, shape, dt)
                P.dma("sp", lambda e: e.dma_start(out=o.t.ap(), in_=ap), [tn.b], [o.b])

        PS = [TN(nc.alloc_psum_tensor(f"ps{i}", [128, 512], F32)) for i in range(6)]
        PSB = [TN(nc.alloc_psum_tensor(f"psb{i}", [128, 1024], BF16)) for i in range(2)]
        for _p in PS + PSB:
            _p.b.excl = True
        psr = Rot(PS[0:4])
        psbr = Rot(PSB)

        identf = SB("identf", [128, 128]); identb = SB("identb", [128, 128], BF16)
        iota = SB("iota", [128, 128]); onesb = SB("onesb", [128, 128], BF16); onesf = SB("onesf", [128, 128])
        maskA = SB("maskA", [128, 2, 128], BF16); maskH = SB("maskH", [128, 2, 128], BF16)
        hb = SB("hb", [128, 1]); mtmp = SB("mtmp", [128, 2, 128])
        g1T = SB("g1T", [128, 8]); g2T = SB("g2T", [128, 8]); gmT = SB("gmT", [128, 8])
        bgT = SB("bgT", [128, 24]); bdwT = SB("bdwT", [128, 4]); lngT = SB("lngT", [128, 4]); lnbT = SB("lnbT", [128, 4])
        wdwT = SB("wdwT", [128, 4, 31])
        qna = SB("qna", [128, 128]); kna = SB("kna", [128, 128]); qnm = SB("qnm", [128, 128]); knm = SB("knm", [128, 128])
        skT = SB("skT", [128, 2, 128], BF16)

        def ld(dst, src, q="sp"):
            P.dma(q, lambda e: e.dma_start(out=dst.t[:], in_=src.t.ap()), [src.b], [dst.b])
        for dst, src in ((identf, ident_d), (iota, iota_d), (hb, hb_d), (g1T, g1T_d), (g2T, g2T_d), (gmT, gmT_d),
                         (bgT, bgT_d), (bdwT, bdwT_d), (lngT, lngT_d), (lnbT, lnbT_d), (wdwT, wdwT_d),
                         (qna, qna_d), (kna, kna_d), (qnm, qnm_d), (knm, knm_d)):
            ld(dst, src)
        ld(skT, skT_d, "pool")
        P.dma("sp", lambda e: e.dma_start(out=mtmp.t[:, 0, :], in_=mprev_d.t.ap()), [], [mtmp.b])
        P.dma("sp", lambda e: e.dma_start(out=mtmp.t[:, 1, :], in_=mcur_d.t.ap()), [mtmp.b], [mtmp.b])
        P.op("dve", lambda e: e.tensor_copy(out=identb.t[:], in_=identf.t[:]), [identf.b], [identb.b])
        P.op("dve", lambda e: e.memset(onesb.t[:], 1.0), [], [onesb.b])
        P.op("dve", lambda e: e.memset(onesf.t[:], 1.0), [], [onesf.b])
        P.op("dve", lambda e: e.tensor_copy(out=maskA.t[:], in_=mtmp.t[:]), [mtmp.b], [maskA.b])
        P.op("dve", lambda e: e.tensor_copy(out=maskH.t[:, 1, :], in_=mtmp.t[:, 1, :]), [mtmp.b], [maskH.b])
        P.op("dve", lambda e: e.tensor_scalar(out=maskH.t[:, 0, :], in0=mtmp.t[:, 0, :], scalar1=hb.t[:, 0:1], scalar2=None,
                                              op0=ALU.mult), [mtmp.b, hb.b, maskH.b], [maskH.b])

        hTm = SB("hTm", [128, 8, CH], BF16)
        hTs = SB("hTs", [128, 8, NS], BF16)
        hTh = SB("hTh", [128, 8, CH], BF16)
        hTmem = SB("hTmem", [128, 8, 256], BF16)
        AT = SB("AT", [128, 4, CH], BF16); AsT = SB("AsT", [128, 4, NS], BF16)

        H = {}

        def alloc_xbufs():
            H["xt"] = Rot([SB(f"xt{i}", [128, 1024]) for i in range(2)])
            H["xb"] = Rot([SB(f"xb{i}", [128, 1024], BF16) for i in range(2)])
        alloc_xbufs()
        junk = SB("junk", [128, 1024], BF16)
        ss_r = Rot([SB(f"ss{i}", [128, 8]) for i in range(8)])

        def norm_T(src, n, gT, dst, c0):
            ss = ss_r.next(); xb = H["xb"].next()
            P.op("pool", lambda e: e.memset(ss.t[:n, 0:1], 0.0), [], [ss.b])
            P.op("act", lambda e: e.activation(out=junk.t[:n], in_=src.t[:n], func=AF.Square, accum_out=ss.t[:n, 0:1]),
                 [src.b, ss.b], [ss.b])
            P.op("act", lambda e: e.activation(out=ss.t[:n, 0:1], in_=ss.t[:n, 0:1], func=AF.Sqrt, bias=EPS, scale=1.0 / 1024),
                 [ss.b], [ss.b])
            P.op("dve", lambda e: e.reciprocal(out=ss.t[:n, 0:1], in_=ss.t[:n, 0:1]), [ss.b], [ss.b])
            P.op("dve", lambda e: e.tensor_scalar(out=xb.t[:n], in0=src.t[:n], scalar1=ss.t[:n, 0:1], scalar2=None, op0=ALU.mult),
                 [src.b, ss.b], [xb.b])
            ps = psbr.next()

            def tr(e):
                for kc in range(8):
                    ins = e.transpose(out=ps.t[:, kc * 128:kc * 128 + n], in_=xb.t[:n, kc * 128:(kc + 1) * 128], identity=identb.t[:n, :n])
                return ins
            P.op("pe", tr, [xb.b, identb.b], [ps.b])
            psv = ps.t[:].rearrange("p (k t) -> p k t", k=8)
            P.op("dve", lambda e: e.tensor_tensor(out=dst.t[:, :, c0:c0 + n], in0=psv[:, :, 0:n],
                                                  in1=gT.t[:, :].unsqueeze(2).to_broadcast([128, 8, n]), op=ALU.mult),
                 [ps.b, gT.b], [dst.b])

        def load_norm(src_d, r0, n, gT, dst, c0):
            xt = H["xt"].next()
            P.dma("sp", lambda e: e.dma_start(out=xt.t[:n], in_=src_d.t.ap()[r0:r0 + n, :]), [src_d.b], [xt.b])
            norm_T(xt, n, gT, dst, c0)

        for t in range(16):
            load_norm(xh, t * 128, 128, g1T, hTh, t * 128)
        for t in range(16):
            load_norm(xm, t * 128, 128, g1T, hTm, t * 128)
        load_norm(xs, 0, NS, g1T, hTs, 0)
        for t in range(2):
            load_norm(memx, t * 128, 128, gmT, hTmem, t * 128)
        dbg_out("hTm", hTm, hTm.t[:, 0, 0:256], [128, 256], BF16)
        FREE(H["xt"], H["xb"])

        phase_end(1)
        H["wb"] = Rot([SB(f"wb{i}", [128, 8, 512], BF16) for i in range(2)])

        def load_w(src, r0, nk, c0, ncols, dst=None, dcol=0):
            wb = dst if dst is not None else H["wb"].next()
            sap = src.t.ap()[r0:r0 + nk * 128, c0:c0 + ncols].rearrange("(kc p) n -> p kc n", p=128)
            P.dma("pool", lambda e: e.dma_start(out=wb.t[:, 0:nk, dcol:dcol + ncols], in_=sap), [src.b], [wb.b])
            return wb

        def mm_tm(ps, n, hT, cols, wb, wc0, wn, po=0):
            def f(e):
                for kc in range(8):
                    ins = e.matmul(ps.t[:n, po:po + wn], lhsT=hT.t[:, kc, cols], rhs=wb.t[:, kc, wc0:wc0 + wn],
                                   start=(kc == 0), stop=(kc == 7))
                return ins
            P.op("pe", f, [hT.b, wb.b], [ps.b])

        def mm_fm(ps, N, wb, wc0, hT, cols, nk=8, po=0):
            def f(e):
                for kc in range(nk):
                    ins = e.matmul(ps.t[:, po:po + N], lhsT=wb.t[:, kc, wc0:wc0 + 128], rhs=hT.t[:, kc, cols],
                                   start=(kc == 0), stop=(kc == nk - 1))
                return ins
            P.op("pe", f, [hT.b, wb.b], [ps.b])

        kf_r = Rot([SB(f"kf{i}", [128, 512]) for i in range(4)])
        kb_r = Rot([SB(f"kb{i}", [128, 512], BF16) for i in range(4)])
        cs_r = Rot([SB(f"cs{i}", [128, 64]) for i in range(4)])
        rA_r = Rot([SB(f"rA{i}", [128, 4, 32]) for i in range(4)]); rB_r = Rot([SB(f"rB{i}", [128, 4, 32]) for i in range(4)])
        vf_r = Rot([SB(f"vf{i}", [128, 512]) for i in range(2)])
        sg_r = Rot([SB(f"sg{i}", [128, 512]) for i in range(2)])

        def run_pipe(gens, depth=4, side=None):
            gens = list(gens)
            active = []
            gi = 0
            while gi < len(gens) or active:
                while gi < len(gens) and len(active) < depth:
                    active.append(gens[gi]); gi += 1
                if side is not None:
                    try:
                        next(side)
                    except StopIteration:
                        side = None
                for g_ in list(active):
                    try:
                        next(g_)
                    except StopIteration:
                        active.remove(g_)

        def qk_gen(hT, cols, wb, wc0, n, nh, gain, cs_src, cs_r0, out_d, out_r0, out_c0, dstT, c0):
            W = nh * 128
            ss = ss_r.next(); kf = kf_r.next(); kb = kb_r.next()
            ps = psr.next()
            if cs_src is not None:
                cs = cs_r.next(); rA = rA_r.next(); rB = rB_r.next()
                P.dma("sp", lambda e: e.dma_start(out=cs.t[:n], in_=cs_src.t.ap()[cs_r0:cs_r0 + n, :]), [cs_src.b], [cs.b])
            mm_tm(ps, n, hT, cols, wb, wc0, W)
            P.op("pool", lambda e: e.memset(ss.t[:n, 0:nh], 0.0), [], [ss.b])
            yield

            def sq(e):
                for h in range(nh):
                    ins = e.activation(out=junk.t[:n, h * 128:(h + 1) * 128], in_=ps.t[:n, h * 128:(h + 1) * 128], func=AF.Square,
                                       accum_out=ss.t[:n, h:h + 1])
                return ins
            P.op("act", sq, [ps.b, ss.b], [ss.b])
            P.op("act", lambda e: e.activation(out=ss.t[:n, 0:nh], in_=ss.t[:n, 0:nh], func=AF.Sqrt, bias=EPS, scale=1.0 / 128),
                 [ss.b], [ss.b])
            yield
            P.op("dve", lambda e: e.reciprocal(out=ss.t[:n, 0:nh], in_=ss.t[:n, 0:nh]), [ss.b], [ss.b])

            def sc_(e):
                for h in range(nh):
                    ins = e.scalar_tensor_tensor(out=kf.t[:n, h * 128:(h + 1) * 128], in0=ps.t[:n, h * 128:(h + 1) * 128],
                                                 scalar=ss.t[:n, h:h + 1], in1=gain.t[:n, :], op0=ALU.mult, op1=ALU.mult)
                return ins
            P.op("dve", sc_, [ps.b, ss.b, gain.b], [kf.b])
            yield
            if cs_src is not None:
                kv = kf.t[:n, 0:W].rearrange("p (h d) -> p h d", h=nh)

                def rot1(e):
                    e.tensor_tensor(out=rA.t[:n, 0:nh], in0=kv[:, :, 0:32], in1=cs.t[:n, 0:32].unsqueeze(1).to_broadcast([n, nh, 32]), op=ALU.mult)
                    return e.tensor_tensor(out=rB.t[:n, 0:nh], in0=kv[:, :, 0:32], in1=cs.t[:n, 32:64].unsqueeze(1).to_broadcast([n, nh, 32]), op=ALU.mult)
                P.op("dve", rot1, [kf.b, cs.b], [rA.b, rB.b])
                yield

                def rot2(e):
                    e.tensor_tensor(out=kv[:, :, 0:16], in0=rA.t[:n, 0:nh, 0:16], in1=rB.t[:n, 0:nh, 16:32], op=ALU.subtract)
                    return e.tensor_tensor(out=kv[:, :, 16:32], in0=rA.t[:n, 0:nh, 16:32], in1=rB.t[:n, 0:nh, 0:16], op=ALU.add)
                P.op("dve", rot2, [rA.b, rB.b], [kf.b])
                yield
            if out_d is not None:
                P.dma("sp", lambda e: e.dma_start(out=out_d.t.ap()[out_r0:out_r0 + n, out_c0:out_c0 + W], in_=kf.t[:n, 0:W]), [kf.b], [out_d.b])
            P.op("act", lambda e: e.copy(out=kb.t[:n, 0:W], in_=kf.t[:n, 0:W]), [kf.b], [kb.b])
            yield
            pt = psbr.next()

            def tr(e):
                for h in range(nh):
                    ins = e.transpose(out=pt.t[:, h * 128:h * 128 + n], in_=kb.t[:n, h * 128:(h + 1) * 128], identity=identb.t[:n, :n])
                return ins
            P.op("pe", tr, [kb.b, identb.b], [pt.b])
            ptv = pt.t[:, 0:W].rearrange("p (h t) -> p h t", h=nh)
            P.op("act", lambda e: e.copy(out=dstT.t[:, 0:nh, c0:c0 + n], in_=ptv[:, :, 0:n]), [pt.b], [dstT.b])

        phase_end(12)
        def glu_cols(wu, hT, cols, N, dst_ap_fn, tail_fn=None):
            for j in range(4):
                pg = psr.next()
                mm_fm(pg, N, wu, (4 + j) * 128, hT, cols)
                sg = sg_r.next()
                P.op("act", lambda e, pg=pg, sg=sg: e.activation(out=sg.t[:, 0:N], in_=pg.t[:, 0:N], func=AF.Sigmoid), [pg.b], [sg.b])
                pv = psr.next()
                mm_fm(pv, N, wu, j * 128, hT, cols)
                dst_tn, dst_ap = dst_ap_fn(j)
                P.op("dve", lambda e, pv=pv, sg=sg, dst_ap=dst_ap: e.tensor_tensor(out=dst_ap, in0=pv.t[:, 0:N], in1=sg.t[:, 0:N], op=ALU.mult),
                     [pv.b, sg.b], [dst_tn.b])
                if tail_fn is not None:
                    t_tn, t_ap = tail_fn(j)
                    P.op("dve", lambda e, pv=pv, sg=sg, t_ap=t_ap: e.tensor_tensor(out=t_ap, in0=pv.t[:, N - 30:N], in1=sg.t[:, N - 30:N], op=ALU.mult),
                         [pv.b, sg.b], [t_tn.b])

        def load_wu():
            wu = SB("wu", [128, 8, 1024], BF16)
            load_w(w_in, 0, 8, 2560, 512, wu, 0); load_w(w_in, 0, 8, 3072, 512, wu, 512)
            return wu
        gluH = SB("gluH", [128, 4, 128])
        wu = load_wu()
        glu_cols(wu, hTh, slice(CH - 128, CH), 128, lambda j: (gluH, gluH.t[:, j, :]))
        FREE(wu)

        phase_end(15)
        mkT = SB("mkT", [128, 4, 256], BF16); mvb = SB("mvb", [128, 2, 512], BF16)
        wmk = load_w(w_mem, 0, 8, 0, 512); wmv = load_w(w_mem, 0, 8, 512, 512)
        for t in range(2):
            run_pipe([qk_gen(hTmem, slice(t * 128, (t + 1) * 128), wmk, 0, 128, 4, knm, None, 0, mko_o, t * 128, 0, mkT, t * 128)])
            ps2 = psr.next()
            mm_tm(ps2, 128, hTmem, slice(t * 128, (t + 1) * 128), wmv, 0, 512)
            vf = vf_r.next()
            P.op("dve", lambda e, ps2=ps2, vf=vf: e.tensor_copy(out=vf.t[:], in_=ps2.t[:]), [ps2.b], [vf.b])
            P.op("act", lambda e, ps2=ps2, t=t: e.copy(out=mvb.t[:, t, :], in_=ps2.t[:]), [ps2.b], [mvb.b])
            P.dma("sp", lambda e, vf=vf, t=t: e.dma_start(out=mvo_o.t.ap()[t * 128:(t + 1) * 128, :], in_=vf.t[:]), [vf.b], [mvo_o.b])
        FREE(hTmem)
        FREE(H["wb"])
        H["wb"] = Rot([SB(f"wbs{i}", [128, 8, 128], BF16) for i in range(3)])

        phase_end(2)
        KT = SB("KT", [128, 1, 2 * CH], BF16); KTs = SB("KTs", [128, 1, NS], BF16)
        QT = SB("QT", [128, 1, CH], BF16); QTs = SB("QTs", [128, 3, NS], BF16); qs_tmp = SB("qs_tmp", [128, 1, NS], BF16)
        Vg = SB("Vg", [128, 32, 128], BF16)
        vrow = SB("vrow", [1, NS, 128], BF16)
        acc = SB("acc", [128, 2, CH])
        Pt_r = Rot([SB(f"Pt{i}", [128, 2, 128], BF16) for i in range(3)])
        kcs_all = [[SB(f"kcs{i}_{g}", [128, 128], BF16) for g in range(3)] for i in range(NS)]
        vcs_all = [[SB(f"vcs{i}_{g}", [128, 128], BF16) for g in range(3)] for i in range(NS)]
        kcT_r = Rot([SB(f"kcT{i}", [128, 128], BF16) for i in range(2)])
        wvb = SB("wvb", [128, 8, 128], BF16)
        pS = SB("pS", [128, 3, 2], BF16)
        psOD = PS[4]; psSm = PS[5]
        psOD_r = Rot([PS[4], PS[5]])
        DILS = (1, 4, 16)

        ub_r = Rot([SB(f"ub{i}", [128, 1024], BF16) for i in range(3)])
        ut_r = Rot([SB(f"ut{i}", [128, 1024], BF16) for i in range(4)])
        vb_r = Rot([SB(f"vb{i}", [128, 1024], BF16) for i in range(3)])

        def prep_tables():
            def loads(i):
                ub = ub_r.next(); vb = vb_r.next()
                P.dma("pool", lambda e, ub=ub, i=i: e.dma_start(out=ub.t[:], in_=u_tab.t.ap()[i * 128:(i + 1) * 128, :]), [u_tab.b], [ub.b])
                P.dma("pool", lambda e, vb=vb, i=i: e.dma_start(out=vb.t[:], in_=v_tab.t.ap()[i * 128:(i + 1) * 128, :]), [v_tab.b], [vb.b])
                return ub, vb
            nxt = loads(0)
            for i in range(128):
                ub, vb = nxt
                if i + 1 < 128:
                    nxt = loads(i + 1)
                pt = psbr.next()

                def tru(e, pt=pt, ub=ub):
                    for kc in range(8):
                        ins = e.transpose(out=pt.t[:, kc * 128:(kc + 1) * 128], in_=ub.t[:, kc * 128:(kc + 1) * 128], identity=identb.t[:])
                    return ins
                P.op("pe", tru, [ub.b, identb.b], [pt.b])
                ut = ut_r.next()
                if i % 2 == 0:
                    P.op("act", lambda e, pt=pt, ut=ut: e.copy(out=ut.t[:], in_=pt.t[:]), [pt.b], [ut.b])
                else:
                    P.op("dve", lambda e, pt=pt, ut=ut: e.tensor_copy(out=ut.t[:], in_=pt.t[:]), [pt.b], [ut.b])
                P.dma("sp", lambda e, ut=ut, i=i: e.dma_start(out=uTs.t.ap()[i], in_=ut.t[:]), [ut.b], [uTs.b])
                P.dma("sp", lambda e, vb=vb, i=i: e.dma_start(out=vbs.t.ap()[i], in_=vb.t[:]), [vb.b], [vbs.b])
                yield
        prep_gen = prep_tables()
        blk_ctr = [0]

        def prep_step(k):
            for _ in range(k):
                try:
                    next(prep_gen)
                except StopIteration:
                    return

        def prep_side(every, count):
            n_ = 0; k_ = 0
            while k_ < count:
                n_ += 1
                if n_ % every == 0:
                    prep_step(1); k_ += 1
                yield

        for hp in range(4):
            wk = load_w(w_in, 0, 8, 1536 + hp * 128, 128)
            for i_ in range(NS):
                for g_, d_ in enumerate(DILS):
                    rows_ = slice(2048 - 128 * d_, 2048, d_)
                    kcs_ = kcs_all[i_][g_]; vcs_ = vcs_all[i_][g_]
                    P.dma("pool", lambda e, kcs_=kcs_, rows_=rows_, hp=hp, i_=i_: e.dma_start(out=kcs_.t[:], in_=wkc.t.ap()[i_, rows_, hp * 128:(hp + 1) * 128]), [wkc.b], [kcs_.b])
                    P.dma("pool", lambda e, vcs_=vcs_, rows_=rows_, hp=hp, i_=i_: e.dma_start(out=vcs_.t[:], in_=wvc.t.ap()[i_, rows_, hp * 128:(hp + 1) * 128]), [wvc.b], [vcs_.b])
            kg = []
            for t in range(32):
                hT = hTh if t < 16 else hTm
                tt = t % 16
                kg.append(qk_gen(hT, slice(tt * 128, (tt + 1) * 128), wk, 0, 128, 1, kna, csh_d if t < 16 else csm_d, tt * 128,
                                 ko_o if t >= 16 else None, tt * 128, hp * 128, KT, t * 128))
            kg.append(qk_gen(hTs, slice(0, NS), wk, 0, NS, 1, kna, css_d, 0, kso_o, 0, hp * 128, KTs, 0))
            run_pipe(kg, side=prep_side(5, 10))
            wv = load_w(w_in, 0, 8, 2048 + hp * 128, 128, wvb, 0)
            for t in range(16):
                ps = psr.next()
                mm_tm(ps, 128, hTm, slice(t * 128, (t + 1) * 128), wv, 0, 128)
                vf = vf_r.next()
                P.op("act", lambda e, ps=ps, vf=vf: e.copy(out=vf.t[:, 0:128], in_=ps.t[:, 0:128]), [ps.b], [vf.b])
                P.dma("sp", lambda e, vf=vf, t=t, hp=hp: e.dma_start(out=vo_o.t.ap()[t * 128:(t + 1) * 128, hp * 128:(hp + 1) * 128], in_=vf.t[:, 0:128]),
                      [vf.b], [vo_o.b])
                if t % 4 == 3:
                    prep_step(1)
            ps = psr.next()
            mm_tm(ps, NS, hTs, slice(0, NS), wv, 0, 128)
            vf = vf_r.next()
            P.op("act", lambda e, ps=ps, vf=vf: e.copy(out=vf.t[:NS, 0:128], in_=ps.t[:NS, 0:128]), [ps.b], [vf.b])
            P.dma("sp", lambda e, vf=vf, hp=hp: e.dma_start(out=vso_o.t.ap()[:, hp * 128:(hp + 1) * 128], in_=vf.t[:NS, 0:128]), [vf.b], [vso_o.b])
            for i in range(NS):
                ps = psr.next()
                mm_tm(ps, 1, hTs, slice(i, i + 1), wv, 0, 128)
                P.op("act", lambda e, ps=ps, i=i: e.copy(out=vrow.t[0:1, i, :], in_=ps.t[0:1, 0:128]), [ps.b], [vrow.b])
            for g, d in enumerate(DILS):
                nbs = CH // (128 * d)

                def vblk(r, j, nbs=nbs):
                    return r * (nbs + 1) + j
                for r in range(d):
                    for j in range(nbs + 1):
                        ps = psr.next()
                        if j == 0:
                            hT, cols = hTh, slice(CH - 128 * d + r, CH, d)
                        else:
                            hT, cols = hTm, slice(128 * d * (j - 1) + r, 128 * d * j, d)
                        mm_tm(ps, 128, hT, cols, wv, 0, 128)
                        bi = vblk(r, j)
                        if bi % 2 == 0:
                            P.op("act", lambda e, ps=ps, bi=bi: e.copy(out=Vg.t[:, bi, :], in_=ps.t[:, 0:128]), [ps.b], [Vg.b])
                        else:
                            P.op("dve", lambda e, ps=ps, bi=bi: e.tensor_copy(out=Vg.t[:, bi, :], in_=ps.t[:, 0:128]), [ps.b], [Vg.b])
                wq = load_w(w_in, 0, 8, g * 512 + hp * 128, 128)
                qg = [qk_gen(hTm, slice(t * 128, (t + 1) * 128), wq, 0, 128, 1, qna, csm_d, t * 128, None, 0, 0, QT, t * 128) for t in range(16)]
                qg.append(qk_gen(hTs, slice(0, NS), wq, 0, NS, 1, qna, css_d, 0, None, 0, 0, qs_tmp, 0))
                run_pipe(qg, side=prep_side(8, 3))
                P.op("pool", lambda e, g=g: e.tensor_copy(out=QTs.t[:, g, :], in_=qs_tmp.t[:, 0, :]), [qs_tmp.b], [QTs.b])
                pend_b = None

                def issue_pv(Pt, bp, bc_, qcols, g=g):
                    pod = psOD_r.next()
                    odv = pod.t[:, 0:256].rearrange("p (o q) -> p o q", o=2)

                    def pv(e, Pt=Pt, odv=odv, bp=bp, bc_=bc_):
                        e.matmul(odv[:, 0, :], lhsT=Vg.t[:, bp, :], rhs=Pt.t[:, 0, :], start=True, stop=False)
                        e.matmul(odv[:, 0, :], lhsT=Vg.t[:, bc_, :], rhs=Pt.t[:, 1, :], start=False, stop=True)
                        e.matmul(odv[:, 1, :], lhsT=onesb.t[:], rhs=Pt.t[:, 0, :], start=True, stop=False)
                        return e.matmul(odv[:, 1, :], lhsT=onesb.t[:], rhs=Pt.t[:, 1, :], start=False, stop=True)
                    P.op("pe", pv, [Pt.b, Vg.b, onesb.b], [pod.b])
                    if g == 0:
                        P.op("dve", lambda e, odv=odv, qcols=qcols: e.tensor_copy(out=acc.t[:, :, qcols], in_=odv), [pod.b], [acc.b])
                    else:
                        P.op("dve", lambda e, odv=odv, qcols=qcols: e.tensor_tensor(out=acc.t[:, :, qcols], in0=acc.t[:, :, qcols], in1=odv, op=ALU.add),
                             [pod.b, acc.b], [acc.b])
                for r in range(d):
                    for nb in range(nbs):
                        qcols = slice(128 * d * nb + r, 128 * d * (nb + 1), d)
                        ccols = slice(CH + 128 * d * nb + r, CH + 128 * d * (nb + 1), d)
                        if nb == 0:
                            pcols = slice(CH - 128 * d + r, CH, d)
                        else:
                            pcols = slice(CH + 128 * d * (nb - 1) + r, CH + 128 * d * nb, d)
                        psS = psr.next()
                        psSv = psS.t[:, 0:256].rearrange("p (c q) -> p c q", c=2)

                        def smm(e, psSv=psSv, qcols=qcols, ccols=ccols, pcols=pcols):
                            e.matmul(psSv[:, 0, :], lhsT=KT.t[:, 0, pcols], rhs=QT.t[:, 0, qcols], start=True, stop=True)
                            return e.matmul(psSv[:, 1, :], lhsT=KT.t[:, 0, ccols], rhs=QT.t[:, 0, qcols], start=True, stop=True)
                        P.op("pe", smm, [KT.b, QT.b], [psS.b])
                        if pend_b is not None and PIPE_BLOCKS:
                            issue_pv(*pend_b)
                        Pt = Pt_r.next()
                        P.op("act", lambda e, Pt=Pt, psSv=psSv: e.activation(out=Pt.t[:], in_=psSv, func=AF.Exp, scale=SCALE), [psS.b], [Pt.b])
                        mk_ = maskH if nb == 0 else maskA
                        P.op("dve", lambda e, Pt=Pt, mk_=mk_: e.tensor_tensor(out=Pt.t[:], in0=Pt.t[:], in1=mk_.t[:], op=ALU.mult),
                             [Pt.b, mk_.b], [Pt.b])
                        pend_b = (Pt, vblk(r, nb), vblk(r, nb + 1), qcols)
                        blk_ctr[0] += 1
                        if blk_ctr[0] % 5 == 0:
                            prep_step(1)
                        if not PIPE_BLOCKS:
                            issue_pv(*pend_b)
                if PIPE_BLOCKS:
                    issue_pv(*pend_b)
            P.op("dve", lambda e: e.reciprocal(out=acc.t[:, 1], in_=acc.t[:, 1]), [acc.b], [acc.b])
            P.op("dve", lambda e, hp=hp: e.tensor_tensor(out=AT.t[:, hp, :], in0=acc.t[:, 0], in1=acc.t[:, 1], op=ALU.mult), [acc.b], [AT.b])
            for i in range(NS):
                odv = psOD.t[:, 0:2]
                vts = []
                for g, d in enumerate(DILS):
                    kcs = kcs_all[i][g]; vcs = vcs_all[i][g]
                    pt = psbr.next()
                    P.op("pe", lambda e, pt=pt, kcs=kcs: e.transpose(out=pt.t[:, 0:128], in_=kcs.t[:], identity=identb.t[:]), [kcs.b, identb.b], [pt.b])
                    kcT = kcT_r.next()
                    P.op("act", lambda e, pt=pt, kcT=kcT: e.copy(out=kcT.t[:], in_=pt.t[:, 0:128]), [pt.b], [kcT.b])

                    def ssm(e, kcT=kcT, g=g, i=i):
                        e.matmul(psSm.t[:, 2 * g:2 * g + 1], lhsT=kcT.t[:], rhs=QTs.t[:, g, i:i + 1], start=True, stop=True)
                        return e.matmul(psSm.t[0:1, 2 * g + 1:2 * g + 2], lhsT=KTs.t[:, 0, i:i + 1], rhs=QTs.t[:, g, i:i + 1], start=True, stop=True)
                    P.op("pe", ssm, [kcT.b, QTs.b, KTs.b], [psSm.b])
                    vts.append(vcs)

                def sexp(e):
                    e.activation(out=pS.t[:, :, 0], in_=psSm.t[:, 0:6:2], func=AF.Exp, scale=SCALE)
                    return e.activation(out=pS.t[0:1, :, 1], in_=psSm.t[0:1, 1:6:2], func=AF.Exp, scale=SCALE)
                P.op("act", sexp, [psSm.b], [pS.b])

                def spv(e, vts=vts, i=i, odv=odv):
                    for o in range(2):
                        for g in range(3):
                            lc = vts[g].t[:] if o == 0 else onesb.t[:]
                            ls = vrow.t[0:1, i, :] if o == 0 else onesb.t[0:1, :]
                            e.matmul(odv[:, o:o + 1], lhsT=lc, rhs=pS.t[:, g, 0:1], start=(g == 0), stop=False)
                            ins = e.matmul(odv[:, o:o + 1], lhsT=ls, rhs=pS.t[0:1, g, 1:2], start=False, stop=(g == 2))
                    return ins
                P.op("pe", spv, [pS.b, vrow.b, onesb.b] + [v.b for v in vts], [psOD.b])
                rs = ss_r.next()
                P.op("dve", lambda e, rs=rs, odv=odv: e.reciprocal(out=rs.t[:, 0:1], in_=odv[:, 1:2]), [psOD.b], [rs.b])
                P.op("dve", lambda e, rs=rs, odv=odv, hp=hp, i=i: e.tensor_tensor(out=AsT.t[:, hp, i:i + 1], in0=odv[:, 0:1], in1=rs.t[:, 0:1], op=ALU.mult),
                     [psOD.b, rs.b], [AsT.b])
        dbg_out("AT", AT, AT.t[:, 0, 0:256], [128, 256], BF16)
        prep_step(128)
        FREE(hTh, KT, KTs, QT, QTs, qs_tmp, Vg, vrow, acc, Pt_r, kcT_r, pS, wvb)
        for i_ in range(NS):
            FREE(*kcs_all[i_], *vcs_all[i_])
        FREE(ub_r, vb_r, ut_r)
        FREE(H["wb"])
        H["wb"] = Rot([SB(f"wb{i}", [128, 8, 512], BF16) for i in range(2)])

        phase_end(3)
        bT = SB("bT", [128, 4, CH], BF16); bsT = SB("bsT", [128, 4, NS], BF16)
        gluT = SB("gluT", [128, 4, 30 + CH], BF16)
        gluL = SB("gluL", [128, 4, 30])
        Dg = SB("Dg", [128, 4, 31, 128], BF16)
        gluS = SB("gluS", [128, 4, NS]); convS = SB("convS", [128, 4, NS])
        wu = load_wu()
        P.op("dve", lambda e: e.tensor_scalar(out=gluT.t[:, :, 0:30], in0=gluH.t[:, :, 98:128], scalar1=hb.t[:, 0:1], scalar2=None, op0=ALU.mult),
             [gluH.b, hb.b], [gluT.b])
        for st in range(4):
            glu_cols(wu, hTm, slice(st * 512, (st + 1) * 512), 512, lambda j, st=st: (gluT, gluT.t[:, j, 30 + st * 512:30 + (st + 1) * 512]),
                     (lambda j: (gluL, gluL.t[:, j, :])) if st == 3 else None)
        glu_cols(wu, hTs, slice(0, NS), NS, lambda j: (gluS, gluS.t[:, j, :]))
        gst = SB("gst", [NS, 512])
        psa_ = psr.next()
        mm_tm(psa_, NS, hTs, slice(0, NS), wu, 512, 512)
        sgs_ = sg_r.next()
        P.op("act", lambda e: e.activation(out=sgs_.t[:NS, :], in_=psa_.t[:NS, :], func=AF.Sigmoid), [psa_.b], [sgs_.b])
        psb_ = psr.next()
        mm_tm(psb_, NS, hTs, slice(0, NS), wu, 0, 512)
        P.op("dve", lambda e: e.tensor_tensor(out=gst.t[:], in0=psb_.t[:NS, :], in1=sgs_.t[:NS, :], op=ALU.mult), [psb_.b, sgs_.b], [gst.b])
        P.dma("sp", lambda e: e.dma_start(out=cso_o.t.ap()[:, 29, :], in_=gst.t[:]), [gst.b], [cso_o.b])
        FREE(wu, gluH)
        pc = psr.next()

        def trc(e):
            for j in range(4):
                ins = e.transpose(out=pc.t[0:30, j * 128:(j + 1) * 128], in_=gluL.t[:, j, :], identity=identf.t[:])
            return ins
        P.op("pe", trc, [gluL.b, identf.b], [pc.b])
        cvt = SB("cvt", [30, 512])
        P.op("act", lambda e: e.copy(out=cvt.t[:], in_=pc.t[0:30, :]), [pc.b], [cvt.b])
        P.dma("sp", lambda e: e.dma_start(out=cvo_o.t.ap(), in_=cvt.t[:]), [cvt.b], [cvo_o.b])
        st_tm = SB("st_tm", [30, NS, 512]); stT = SB("stT", [128, 4, NS, 31]); stw = SB("stw", [128, 4, NS, 31])
        P.dma("sp", lambda e: e.dma_start(out=st_tm.t[:], in_=stc.t.ap().rearrange("i t c -> t i c")), [stc.b], [st_tm.b])
        for i in range(NS):
            P.dma("sp", lambda e, i=i: e.dma_start(out=cso_o.t.ap()[i, 0:29, :], in_=st_tm.t[1:30, i, :]), [st_tm.b], [cso_o.b])
            pc = psr.next()

            def trs(e, pc=pc, i=i):
                for j in range(4):
                    ins = e.transpose(out=pc.t[:, j * 32:j * 32 + 30], in_=st_tm.t[0:30, i, j * 128:(j + 1) * 128], identity=identf.t[0:30, 0:30])
                return ins
            P.op("pe", trs, [st_tm.b, identf.b], [pc.b])
            pcv = pc.t[:, 0:128].rearrange("p (j t) -> p j t", j=4)
            P.op("act", lambda e, pcv=pcv, i=i: e.copy(out=stT.t[:, :, i, 0:30], in_=pcv[:, :, 0:30]), [pc.b], [stT.b])
        P.op("dve", lambda e: e.tensor_copy(out=stT.t[:, :, :, 30], in_=gluS.t[:]), [gluS.b, stT.b], [stT.b])
        P.op("dve", lambda e: e.tensor_tensor(out=stw.t[:], in0=stT.t[:], in1=wdwT.t[:].unsqueeze(2).to_broadcast([128, 4, NS, 31]), op=ALU.mult),
             [stT.b, wdwT.b], [stw.b])
        P.op("dve", lambda e: e.reduce_sum(out=convS.t[:], in_=stw.t[:], axis=AX.X), [stw.b], [convS.b])
        P.op("dve", lambda e: e.tensor_tensor(out=convS.t[:], in0=convS.t[:], in1=bdwT.t[:].unsqueeze(2).to_broadcast([128, 4, NS]), op=ALU.add),
             [convS.b, bdwT.b], [convS.b])
        sq_t = SB("sq_t", [128, 4, 512]); ln_a = SB("ln_a", [128, 512]); ln_b = SB("ln_b", [128, 512]); ln_c = SB("ln_c", [128, 512])
        convT = SB("convT", [128, 4, 512])

        def ln_swish(src, cols, N, dst, dcols):
            pm = psr.next(); pq = psr.next()
            P.op("act", lambda e: e.activation(out=sq_t.t[:, :, 0:N], in_=src.t[:, :, cols], func=AF.Square), [src.b], [sq_t.b])

            def st_(e):
                for j in range(4):
                    e.matmul(pm.t[:, 0:N], lhsT=onesf.t[:], rhs=src.t[:, j, cols], start=(j == 0), stop=(j == 3))
                for j in range(4):
                    ins = e.matmul(pq.t[:, 0:N], lhsT=onesf.t[:], rhs=sq_t.t[:, j, 0:N], start=(j == 0), stop=(j == 3))
                return ins
            P.op("pe", st_, [src.b, sq_t.b, onesf.b], [pm.b, pq.b])
            P.op("act", lambda e: e.activation(out=ln_a.t[:, 0:N], in_=pm.t[:, 0:N], func=AF.Copy, scale=1.0 / 512), [pm.b], [ln_a.b])
            P.op("dve", lambda e: e.tensor_tensor(out=ln_b.t[:, 0:N], in0=ln_a.t[:, 0:N], in1=ln_a.t[:, 0:N], op=ALU.mult), [ln_a.b], [ln_b.b])
            P.op("dve", lambda e: e.scalar_tensor_tensor(out=ln_b.t[:, 0:N], in0=pq.t[:, 0:N], scalar=1.0 / 512, in1=ln_b.t[:, 0:N], op0=ALU.mult, op1=ALU.subtract),
                 [pq.b, ln_b.b], [ln_b.b])
            P.op("act", lambda e: e.activation(out=ln_b.t[:, 0:N], in_=ln_b.t[:, 0:N], func=AF.Sqrt, bias=EPS, scale=1.0), [ln_b.b], [ln_b.b])
            P.op("dve", lambda e: e.reciprocal(out=ln_b.t[:, 0:N], in_=ln_b.t[:, 0:N]), [ln_b.b], [ln_b.b])
            for j in range(4):
                P.op("dve", lambda e, j=j: e.tensor_tensor(out=ln_c.t[:, 0:N], in0=src.t[:, j, cols], in1=ln_a.t[:, 0:N], op=ALU.subtract), [src.b, ln_a.b], [ln_c.b])
                P.op("dve", lambda e, j=j: e.tensor_tensor(out=ln_c.t[:, 0:N], in0=ln_c.t[:, 0:N], in1=ln_b.t[:, 0:N], op=ALU.mult), [ln_c.b, ln_b.b], [ln_c.b])
                P.op("dve", lambda e, j=j: e.tensor_scalar(out=ln_c.t[:, 0:N], in0=ln_c.t[:, 0:N], scalar1=lngT.t[:, j:j + 1], scalar2=lnbT.t[:, j:j + 1],
                                                           op0=ALU.mult, op1=ALU.add), [ln_c.b, lngT.b, lnbT.b], [ln_c.b])
                P.op("act", lambda e, j=j: e.activation(out=dst.t[:, j, dcols], in_=ln_c.t[:, 0:N], func=AF.Silu), [ln_c.b], [dst.b])
        for j in range(4):
            P.op("dve", lambda e, j=j: e.tensor_tensor(out=Dg.t[:, j], in0=identb.t[:].unsqueeze(1).to_broadcast([128, 31, 128]),
                                                       in1=wdwT.t[:, j, :].unsqueeze(2).to_broadcast([128, 31, 128]), op=ALU.mult),
                 [identb.b, wdwT.b], [Dg.b])
        for st in range(4):
            for j in range(4):
                base = st * 512
                pcv_ = psr.next()

                def cmm(e, pcv_=pcv_, j=j, base=base):
                    for tap in range(31):
                        ins = e.matmul(pcv_.t[:, :], lhsT=Dg.t[:, j, tap, :], rhs=gluT.t[:, j, base + tap:base + tap + 512], start=(tap == 0), stop=(tap == 30))
                    return ins
                P.op("pe", cmm, [Dg.b, gluT.b], [pcv_.b])
                P.op("act", lambda e, pcv_=pcv_, j=j: e.activation(out=convT.t[:, j, :], in_=pcv_.t[:, :], func=AF.Identity, bias=bdwT.t[:, j:j + 1], scale=1.0),
                     [pcv_.b, bdwT.b], [convT.b])
            ln_swish(convT, slice(0, 512), 512, bT, slice(st * 512, (st + 1) * 512))
        ln_swish(convS, slice(0, NS), NS, bsT, slice(0, NS))
        dbg_out("bT", bT, bT.t[:, 0, 0:256], [128, 256], BF16)
        FREE(gluT, gluL, Dg, gluS, convS, cvt, st_tm, stT, stw, gst, sq_t, ln_a, ln_b, ln_c, convT)

        phase_end(4)
        wab = SB("wab", [128, 4, 1024], BF16); wbb = SB("wbb", [128, 4, 1024], BF16); wmb = SB("wmb", [128, 4, 1024], BF16)
        for dst, src in ((wab, w_a), (wbb, w_b), (wmb, w_m)):
            load_w(src, 0, 4, 0, 512, dst, 0); load_w(src, 0, 4, 512, 512, dst, 512)
        MT = SB("MT", [128, 4, CH], BF16); MsT = SB("MsT", [128, 4, NS], BF16)
        qmT = SB("qmT", [128, 4, CH], BF16); qmTs = SB("qmTs", [128, 4, NS], BF16)
        wqm = load_w(w_in, 0, 8, 3584, 512)
        mg_ = [qk_gen(hTm, slice(t * 128, (t + 1) * 128), wqm, 0, 128, 4, qnm, None, 0, None, 0, 0, qmT, t * 128) for t in range(16)]
        mg_.append(qk_gen(hTs, slice(0, NS), wqm, 0, NS, 4, qnm, None, 0, None, 0, 0, qmTs, 0))
        run_pipe(mg_)
        Pm_r = Rot([SB(f"Pm{i}", [128, 2, 512], BF16) for i in range(2)])
        rd_r = Rot([SB(f"rd{i}", [128, 512]) for i in range(2)])

        def mem_attn(keyT_fn, val_fn, qT, cols, N, dst, h, dcols):
            Pm = Pm_r.next()
            pss = [psr.next(), psr.next()]
            for mb in range(2):
                P.op("pe", lambda e, mb=mb: e.matmul(pss[mb].t[:, 0:N], lhsT=keyT_fn(mb)[1], rhs=qT.t[:, h, cols], start=True, stop=True),
                     [keyT_fn(mb)[0].b, qT.b], [pss[mb].b])
                P.op("act", lambda e, mb=mb: e.activation(out=Pm.t[:, mb, 0:N], in_=pss[mb].t[:, 0:N], func=AF.Exp, scale=SCALE), [pss[mb].b], [Pm.b])
            po = psr.next(); pd = psr.next()

            def f(e):
                for mb in range(2):
                    e.matmul(po.t[:, 0:N], lhsT=val_fn(mb)[1], rhs=Pm.t[:, mb, 0:N], start=(mb == 0), stop=(mb == 1))
                for mb in range(2):
                    ins = e.matmul(pd.t[:, 0:N], lhsT=onesb.t[:], rhs=Pm.t[:, mb, 0:N], start=(mb == 0), stop=(mb == 1))
                return ins
            P.op("pe", f, [Pm.b, val_fn(0)[0].b, val_fn(1)[0].b, onesb.b], [po.b, pd.b])
            rd = rd_r.next()
            P.op("dve", lambda e: e.reciprocal(out=rd.t[:, 0:N], in_=pd.t[:, 0:N]), [pd.b], [rd.b])
            P.op("dve", lambda e: e.tensor_tensor(out=dst.t[:, h, dcols], in0=po.t[:, 0:N], in1=rd.t[:, 0:N], op=ALU.mult), [po.b, rd.b], [dst.b])
        for st in range(4):
            for h in range(4):
                cols = slice(st * 512, (st + 1) * 512)
                mem_attn(lambda mb, h=h: (mkT, mkT.t[:, h, mb * 128:(mb + 1) * 128]),
                         lambda mb, h=h: (mvb, mvb.t[:, mb, h * 128:(h + 1) * 128]), qmT, cols, 512, MT, h, cols)
        mks_r = Rot([SB(f"mks{i}", [128, 2, 512], BF16) for i in range(NS)])
        mvs_r = Rot([SB(f"mvs{i}", [128, 2, 512], BF16) for i in range(NS)])
        mksT_r = Rot([SB(f"mksT{i}", [128, 2, 128], BF16) for i in range(2)])
        for i in range(NS):
            mks = mks_r.next(); mvs = mvs_r.next()
            P.dma("pool", lambda e, mks=mks, i=i: e.dma_start(out=mks.t[:], in_=mkc.t.ap()[i].rearrange("(mb p) c -> p mb c", p=128)), [mkc.b], [mks.b])
            P.dma("pool", lambda e, mvs=mvs, i=i: e.dma_start(out=mvs.t[:], in_=mvc.t.ap()[i].rearrange("(mb p) c -> p mb c", p=128)), [mvc.b], [mvs.b])
            for h in range(4):
                pt = psbr.next()

                def trm(e, pt=pt, mks=mks, h=h):
                    for mb in range(2):
                        ins = e.transpose(out=pt.t[:, mb * 128:(mb + 1) * 128], in_=mks.t[:, mb, h * 128:(h + 1) * 128], identity=identb.t[:])
                    return ins
                P.op("pe", trm, [mks.b, identb.b], [pt.b])
                mksT = mksT_r.next()
                P.op("act", lambda e, pt=pt, mksT=mksT: e.copy(out=mksT.t[:], in_=pt.t[:, 0:256].rearrange("p (m k) -> p m k", m=2)), [pt.b], [mksT.b])
                mem_attn(lambda mb, mksT=mksT: (mksT, mksT.t[:, mb, :]),
                         lambda mb, mvs=mvs, h=h: (mvs, mvs.t[:, mb, h * 128:(h + 1) * 128]), qmTs, slice(i, i + 1), 1, MsT, h, slice(i, i + 1))
        dbg_out("MT", MT, MT.t[:, 0, 0:256], [128, 256], BF16)
        FREE(qmT, qmTs, Pm_r, rd_r, mks_r, mvs_r, mksT_r, mkT, mvb)

        phase_end(5)
        mergedT = SB("mergedT", [128, 8, CH], BF16); mergedS = SB("mergedS", [128, 8, NS], BF16)
        mg_r = Rot([SB(f"mg{i}", [128, 512]) for i in range(2)])
        for fc in range(8):
            wg = H["wb"].next()
            for br in range(3):
                load_w(w_in, 0, 8, 4096 + br * 1024 + fc * 128, 128, wg, br * 128)
            for (hT, cols, N, XS, dst) in [(hTm, slice(st * 512, (st + 1) * 512), 512, (AT, bT, MT), mergedT) for st in range(4)] + \
                                          [(hTs, slice(0, NS), NS, (AsT, bsT, MsT), mergedS)]:
                mg = mg_r.next()
                for br in range(3):
                    pg = psr.next()
                    mm_fm(pg, N, wg, br * 128, hT, cols)
                    sg = sg_r.next()
                    P.op("act", lambda e, pg=pg, sg=sg, br=br, N=N, fc=fc: e.activation(out=sg.t[:, 0:N], in_=pg.t[:, 0:N], func=AF.Sigmoid,
                                                                                       bias=bgT.t[:, br * 8 + fc:br * 8 + fc + 1], scale=1.0),
                         [pg.b, bgT.b], [sg.b])
                    py = psr.next()
                    mm_fm(py, N, (wab, wbb, wmb)[br], fc * 128, XS[br], cols, nk=4)
                    if br == 0:
                        P.op("dve", lambda e, py=py, sg=sg, mg=mg, N=N: e.tensor_tensor(out=mg.t[:, 0:N], in0=py.t[:, 0:N], in1=sg.t[:, 0:N], op=ALU.mult),
                             [py.b, sg.b], [mg.b])
                    else:
                        P.op("dve", lambda e, py=py, sg=sg, N=N: e.tensor_tensor(out=sg.t[:, 0:N], in0=py.t[:, 0:N], in1=sg.t[:, 0:N], op=ALU.mult),
                             [py.b, sg.b], [sg.b])
                        if br == 1:
                            P.op("dve", lambda e, sg=sg, mg=mg, N=N: e.tensor_tensor(out=mg.t[:, 0:N], in0=mg.t[:, 0:N], in1=sg.t[:, 0:N], op=ALU.add),
                                 [sg.b, mg.b], [mg.b])
                        else:
                            P.op("dve", lambda e, sg=sg, mg=mg, N=N, dst=dst, cols=cols, fc=fc: e.tensor_tensor(out=dst.t[:, fc, cols], in0=mg.t[:, 0:N], in1=sg.t[:, 0:N], op=ALU.add),
                                 [sg.b, mg.b], [dst.b])
        FREE(hTm, hTs, AT, AsT, bT, bsT, MT, MsT, wab, wbb, wmb, mg_r, H["wb"], sg_r, kf_r, kb_r, cs_r, rA_r, rB_r, vf_r)

        phase_end(6)
        alloc_xbufs()
        wob = SB("wob", [128, 8, 1024], BF16)
        load_w(w_o, 0, 8, 0, 512, wob, 0); load_w(w_o, 0, 8, 512, 512, wob, 512)
        wpq = SB("wpq", [128, 8, 2048], BF16)
        for c in range(4):
            load_w(w_pq, 0, 8, c * 512, 512, wpq, c * 512)
        x1_r = Rot([SB(f"x1_{i}", [128, 1024]) for i in range(2)])
        h2T = SB("h2T", [128, 8, CH + NS], BF16)
        for t in range(17):
            n = 128 if t < 16 else NS
            mT, cols, src_d, r0 = (mergedT, slice(t * 128, (t + 1) * 128), xm, t * 128) if t < 16 else (mergedS, slice(0, NS), xs, 0)
            xt = H["xt"].next()
            P.dma("sp", lambda e, xt=xt, src_d=src_d, r0=r0, n=n: e.dma_start(out=xt.t[:n], in_=src_d.t.ap()[r0:r0 + n, :]), [src_d.b], [xt.b])
            x1 = x1_r.next()
            for half in range(2):
                ps = psr.next()
                mm_tm(ps, n, mT, cols, wob, half * 512, 512)
                P.op("dve", lambda e, ps=ps, xt=xt, x1=x1, half=half, n=n: e.tensor_tensor(out=x1.t[:n, half * 512:(half + 1) * 512], in0=ps.t[:n, :],
                                                                                         in1=xt.t[:n, half * 512:(half + 1) * 512], op=ALU.add),
                     [ps.b, xt.b], [x1.b])
            P.dma("sp", lambda e, x1=x1, t=t, n=n: e.dma_start(out=x1s.t.ap()[t * 128:t * 128 + n, :], in_=x1.t[:n]), [x1.b], [x1s.b])
            if t < 16:
                norm_T(x1, 128, g2T, h2T, t * 128)
            else:
                norm_T(x1, NS, g2T, h2T, CH)
        dbg_out("h2T", h2T, h2T.t[:, 0, 0:256], [128, 256], BF16)
        FREE(mergedT, mergedS, wob)

        phase_end(7)

        phase_end(8)
        NT = CH + NS
        selTall = SB("selTall", [128, 3, NT])
        qpT = SB("qpT", [128, 16, 256], BF16)

        class RS:
            pass
        rsets = []
        for k_ in range(2):
            S = RS()
            S.sc = SB(f"sc{k_}", [128, 16, 128]); S.m16 = SB(f"m16{k_}", [128, 16, 16]); S.i16 = SB(f"i16{k_}", [128, 16, 16], U32)
            S.i16f = SB(f"i16f{k_}", [128, 16, 16]); S.cand = SB(f"cand{k_}", [128, 8, 16, 16]); S.cm = SB(f"cm{k_}", [128, 8, 16])
            S.ci = SB(f"ci{k_}", [128, 8, 16], U32); S.cia = SB(f"cia{k_}", [128, 8, 16], U32); S.cib = SB(f"cib{k_}", [128, 8, 16], U32)
            S.ciaf = SB(f"ciaf{k_}", [128, 8, 16]); S.cibf = SB(f"cibf{k_}", [128, 8, 16]); S.eq = SB(f"eq{k_}", [128, 8, 16, 16])
            S.sel = SB(f"sel{k_}", [128, 3, 128]); S.gsum = SB(f"gsum{k_}", [128, 8])
            rsets.append(S)

        def route_tile(S, t0, n, g0):
            sc, m16, i16, i16f, cand, cm, ci = S.sc, S.m16, S.i16, S.i16f, S.cand, S.cm, S.ci
            cia, cib, ciaf, cibf, eq, sel, gsum = S.cia, S.cib, S.ciaf, S.cibf, S.eq, S.sel, S.gsum
            for q4 in range(4):
                ps = psr.next()

                def smm(e, ps=ps, q4=q4):
                    for c in range(4):
                        c16 = q4 * 4 + c
                        ins = e.matmul(ps.t[:n, c * 128:(c + 1) * 128], lhsT=qpT.t[:, c16, t0:t0 + n], rhs=skT.t[:, c16 % 2, :], start=True, stop=True)
                    return ins
                P.op("pe", smm, [qpT.b, skT.b], [ps.b])
                P.op("act", lambda e, ps=ps, q4=q4: e.copy(out=sc.t[:n, q4 * 4:(q4 + 1) * 4, :], in_=ps.t[:n, :].rearrange("p (c k) -> p c k", c=4)),
                     [ps.b], [sc.b])
            yield

            def tk_a(e):
                for c16 in range(16):
                    ins = e.max(out=m16.t[:n, c16, 0:8], in_=sc.t[:n, c16, :])
                return ins
            P.op("dve", tk_a, [sc.b], [m16.b])
            yield

            def tk_b(e):
                for c16 in range(16):
                    ins = e.max_index(out=i16.t[:n, c16, 0:8], in_max=m16.t[:n, c16, 0:8], in_values=sc.t[:n, c16, :])
                return ins
            P.op("dve", tk_b, [sc.b, m16.b], [i16.b])
            yield

            def tk_b2(e):
                for c16 in range(16):
                    ins = e.match_replace(out=sc.t[:n, c16, :], in_to_replace=m16.t[:n, c16, 0:8], in_values=sc.t[:n, c16, :], imm_value=-1e30)
                return ins
            P.op("dve", tk_b2, [sc.b, m16.b], [sc.b])
            yield

            def tk_c(e):
                for c16 in range(16):
                    ins = e.max(out=m16.t[:n, c16, 8:16], in_=sc.t[:n, c16, :])
                return ins
            P.op("dve", tk_c, [sc.b, m16.b], [m16.b])
            yield

            def tk_d(e):
                for c16 in range(16):
                    ins = e.max_index(out=i16.t[:n, c16, 8:16], in_max=m16.t[:n, c16, 8:16], in_values=sc.t[:n, c16, :])
                return ins
            P.op("dve", tk_d, [sc.b, m16.b, i16.b], [i16.b])
            yield
            m16v = m16.t[:n].rearrange("p (h c) k -> p h c k", c=2)
            i16fv = i16f.t[:n].rearrange("p (h c) k -> p h c k", c=2)

            def cf_a(e):
                e.tensor_copy(out=i16f.t[:n], in_=i16.t[:n])
                return e.tensor_tensor(out=cand.t[:n], in0=m16v[:, :, 0, :].unsqueeze(3).to_broadcast([n, 8, 16, 16]),
                                       in1=m16v[:, :, 1, :].unsqueeze(2).to_broadcast([n, 8, 16, 16]), op=ALU.add)
            P.op("dve", cf_a, [m16.b, i16.b], [i16f.b, cand.b])
            yield
            cvs = [cand.t[:n, h].rearrange("p a b -> p (a b)") for h in range(8)]

            def cf_b(e):
                for h in range(8):
                    ins = e.max(out=cm.t[:n, h, 0:8], in_=cvs[h])
                return ins
            P.op("dve", cf_b, [cand.b], [cm.b])
            yield

            def cf_c(e):
                for h in range(8):
                    ins = e.max_index(out=ci.t[:n, h, 0:8], in_max=cm.t[:n, h, 0:8], in_values=cvs[h])
                return ins
            P.op("dve", cf_c, [cand.b, cm.b], [ci.b])
            yield

            def cf_c2(e):
                for h in range(8):
                    ins = e.match_replace(out=cvs[h], in_to_replace=cm.t[:n, h, 0:8], in_values=cvs[h], imm_value=-1e30)
                return ins
            P.op("dve", cf_c2, [cand.b, cm.b], [cand.b])
            yield

            def cf_d(e):
                for h in range(8):
                    ins = e.max(out=cm.t[:n, h, 8:16], in_=cvs[h])
                return ins
            P.op("dve", cf_d, [cand.b, cm.b], [cm.b])
            yield

            def cf_e(e):
                for h in range(8):
                    ins = e.max_index(out=ci.t[:n, h, 8:16], in_max=cm.t[:n, h, 8:16], in_values=cvs[h])
                return ins
            P.op("dve", cf_e, [cand.b, cm.b, ci.b], [ci.b])
            yield

            def cf_f(e):
                e.tensor_single_scalar(out=cia.t[:n], in_=ci.t[:n], scalar=4, op=ALU.logical_shift_right)
                return e.tensor_single_scalar(out=cib.t[:n], in_=ci.t[:n], scalar=15, op=ALU.bitwise_and)
            P.op("dve", cf_f, [ci.b], [cia.b, cib.b])
            yield

            def cf_g(e):
                e.tensor_copy(out=ciaf.t[:n], in_=cia.t[:n])
                return e.tensor_copy(out=cibf.t[:n], in_=cib.t[:n])
            P.op("dve", cf_g, [cia.b, cib.b], [ciaf.b, cibf.b])
            yield
            io16 = iota.t[:n, 0:16].unsqueeze(1).unsqueeze(1).to_broadcast([n, 8, 16, 16])
            for w, cf in ((0, ciaf), (1, cibf)):
                P.op("dve", lambda e, cf=cf: e.tensor_tensor(out=eq.t[:n], in0=cf.t[:n].unsqueeze(3).to_broadcast([n, 8, 16, 16]), in1=io16, op=ALU.is_equal),
                     [cf.b, iota.b], [eq.b])
                yield
                P.op("dve", lambda e, w=w: e.tensor_tensor(out=eq.t[:n], in0=eq.t[:n], in1=i16fv[:, :, w, :].unsqueeze(2).to_broadcast([n, 8, 16, 16]), op=ALU.mult),
                     [eq.b, i16f.b], [eq.b])
                yield
                P.op("dve", lambda e, w=w: e.reduce_sum(out=sel.t[:n, w, :].rearrange("p (h k) -> p h k", h=8), in_=eq.t[:n], axis=AX.X),
                     [eq.b], [sel.b])
                yield
            gv = sel.t[:n, 2, :].rearrange("p (h k) -> p h k", h=8)
            P.op("dve", lambda e: e.tensor_tensor(out=gv, in0=cm.t[:n], in1=cm.t[:n, :, 0:1].to_broadcast([n, 8, 16]), op=ALU.subtract),
                 [cm.b, sel.b], [sel.b])
            yield
            P.op("act", lambda e: e.activation(out=gv, in_=gv, func=AF.Exp), [sel.b], [sel.b])
            yield
            P.op("dve", lambda e: e.reduce_sum(out=gsum.t[:n], in_=gv, axis=AX.X), [sel.b], [gsum.b])
            yield
            P.op("dve", lambda e: e.reciprocal(out=gsum.t[:n], in_=gsum.t[:n]), [gsum.b], [gsum.b])
            yield
            P.op("dve", lambda e: e.tensor_tensor(out=gv, in0=gv, in1=gsum.t[:n].unsqueeze(2).to_broadcast([n, 8, 16]), op=ALU.mult),
                 [sel.b, gsum.b], [sel.b])
            yield
            pT = psr.next()

            def trsel(e):
                for w in range(3):
                    ins = e.transpose(out=pT.t[:, w * 128:w * 128 + n], in_=sel.t[:n, w, :], identity=identf.t[:n, :n])
                return ins
            P.op("pe", trsel, [sel.b, identf.b], [pT.b])
            P.op("act", lambda e: e.copy(out=selTall.t[:, :, g0 + t0:g0 + t0 + n],
                                         in_=pT.t[:, 0:384].rearrange("p (w t) -> p w t", w=3)[:, :, 0:n]), [pT.b], [selTall.b])

        def route_block(hT, c0, ntok, tiles, g0):
            cols = slice(c0, c0 + ntok)
            for c16 in range(16):
                ps = psr.next()
                mm_fm(ps, ntok, wpq, c16 * 128, hT, cols)
                if c16 % 2 == 0:
                    P.op("act", lambda e, ps=ps, c16=c16: e.copy(out=qpT.t[:, c16, 0:ntok], in_=ps.t[:, 0:ntok]), [ps.b], [qpT.b])
                else:
                    P.op("dve", lambda e, ps=ps, c16=c16: e.tensor_copy(out=qpT.t[:, c16, 0:ntok], in_=ps.t[:, 0:ntok]), [ps.b], [qpT.b])
            run_pipe([route_tile(rsets[k_], t0, n, g0) for k_, (t0, n) in enumerate(tiles)], depth=2)

        for sp_ in range(8):
            route_block(h2T, sp_ * 256, 256, [(0, 128), (128, 128)], sp_ * 256)
        route_block(h2T, CH, NS, [(0, NS)], CH)
        dbg_out("selT", selTall, selTall.t[:, :, 0:128], [128, 3, 128])
        FREE(wpq, qpT)
        for S in rsets:
            FREE(S.sc, S.m16, S.i16, S.i16f, S.cand, S.cm, S.ci, S.cia, S.cib, S.ciaf, S.cibf, S.eq, S.sel, S.gsum)

        phase_end(9)
        ut_r = Rot([SB(f"utx{i}", [128, 1024], BF16) for i in range(4)])
        TwH = [SB("TwA", [128, 256 + NS, 64], BF16), SB("TwB", [128, 256 + NS, 64], BF16)]
        selTb = SB("selTb", [128, 3, NT], BF16)
        P.op("dve", lambda e: e.tensor_copy(out=selTb.t[:], in_=selTall.t[:]), [selTall.b], [selTb.b])
        iotab = SB("iotab", [128, 128], BF16)
        P.op("dve", lambda e: e.tensor_copy(out=iotab.t[:], in_=iota.t[:]), [iota.b], [iotab.b])
        FREE(selTall)
        NOH = 8
        ohJ_r = Rot([SB(f"ohJ{i}", [128, 4, 128], BF16) for i in range(NOH)])
        ohI_r = Rot([SB(f"ohI{i}", [128, 4, 64], BF16) for i in range(NOH)])
        vt_r = Rot([SB(f"vt{i}", [128, 1024], BF16) for i in range(4)])
        ge_r = Rot([SB(f"ge{i}", [128, 256 + NS]) for i in range(3)])
        hg_r = Rot([SB(f"hg{i}", [128, 256 + NS], BF16) for i in range(3)])

        class FV:
            def __init__(self, tn):
                self.b = tn.b
                self.v = tn.t[:].bitcast(F32)

        def oap(o, n):
            return o.v[:n, :] if isinstance(o, FV) else o.t[:n, :]
        fvb = [FV(PSB[0]), FV(PSB[1])]
        outs_all = [(PS[0], PS[1]), (PS[2], PS[3]), (fvb[0], fvb[1])]
        psWf_r = Rot(fvb)
        psA_r = Rot([PS[4], PS[5]])
        BDELAY = 3

        def build_gen(blk, half):
            hT, c0, ntok, tiles, g0, dests = blk
            Tw = TwH[half]
            pendq = []

            def emit_pe(ohJ, ohI, tb, nb_):
                psW = psWf_r.next()
                pwv = psW.v

                def wmm(e):
                    for u in range(nb_):
                        ins = e.matmul(pwv[:, u * 64:(u + 1) * 64], lhsT=ohJ.t[:, u, :], rhs=ohI.t[:, u, :], start=True, stop=True)
                    return ins
                P.op("pe", wmm, [ohJ.b, ohI.b], [psW.b])
                P.op("act", lambda e: e.copy(out=Tw.t[:, tb:tb + nb_, :], in_=pwv[:, 0:nb_ * 64].rearrange("p (u i) -> p u i", u=nb_)), [psW.b], [Tw.b])
            for tb in range(0, ntok, 4):
                nb_ = min(4, ntok - tb)
                ohJ = ohJ_r.next(); ohI = ohI_r.next()
                tk0 = g0 + tb
                P.op("dve", lambda e, ohI=ohI, tk0=tk0, nb_=nb_: e.tensor_tensor(
                    out=ohI.t[:, 0:nb_, :], in0=iotab.t[:, half * 64:(half + 1) * 64].unsqueeze(1).to_broadcast([128, nb_, 64]),
                    in1=selTb.t[:, 0, tk0:tk0 + nb_].unsqueeze(2).to_broadcast([128, nb_, 64]), op=ALU.is_equal),
                    [iotab.b, selTb.b], [ohI.b])
                P.op("dve", lambda e, ohJ=ohJ, tk0=tk0, nb_=nb_: e.tensor_tensor(
                    out=ohJ.t[:, 0:nb_, :], in0=iotab.t[:].unsqueeze(1).to_broadcast([128, nb_, 128]),
                    in1=selTb.t[:, 1, tk0:tk0 + nb_].unsqueeze(2).to_broadcast([128, nb_, 128]), op=ALU.is_equal),
                    [iotab.b, selTb.b], [ohJ.b])
                P.op("dve", lambda e, ohI=ohI, tk0=tk0, nb_=nb_: e.tensor_tensor(
                    out=ohI.t[:, 0:nb_, :], in0=ohI.t[:, 0:nb_, :], in1=selTb.t[:, 2, tk0:tk0 + nb_].unsqueeze(2).to_broadcast([128, nb_, 64]), op=ALU.mult),
                    [ohI.b, selTb.b], [ohI.b])
                pendq.append((ohJ, ohI, tb, nb_))
                if len(pendq) > BDELAY:
                    emit_pe(*pendq.pop(0))
                yield
            while pendq:
                emit_pe(*pendq.pop(0))
            yield

        def loop_gen(blk):
            hT, c0, ntok, tiles, g0, dests = blk
            cols = slice(c0, c0 + ntok)
            outs = outs_all
            pend = None

            def issue_omm(i, hg, vt):
                def omm(e, hg=hg, vt=vt, i=i):
                    for ti, (t0, n) in enumerate(tiles):
                        for half in range(2):
                            ins = e.matmul(oap(outs[ti][half], n), lhsT=hg.t[:, t0:t0 + n], rhs=vt.t[:, half * 512:(half + 1) * 512],
                                           start=(i == 0), stop=(i == 127))
                    return ins
                P.op("pe", omm, [hg.b, vt.b], [outs[ti][half].b for ti in range(len(tiles)) for half in range(2)])
            for i in range(128):
                ut = ut_r.next(); vt = vt_r.next()
                P.dma("sp", lambda e, ut=ut, i=i: e.dma_start(out=ut.t[:], in_=uTs.t.ap()[i]), [uTs.b], [ut.b])
                P.dma("sp", lambda e, vt=vt, i=i: e.dma_start(out=vt.t[:], in_=vbs.t.ap()[i]), [vbs.b], [vt.b])
                psA = psA_r.next()

                def amm(e, ut=ut, psA=psA):
                    for kc in range(8):
                        ins = e.matmul(psA.t[:, 0:ntok], lhsT=ut.t[:, kc * 128:(kc + 1) * 128], rhs=hT.t[:, kc, cols], start=(kc == 0), stop=(kc == 7))
                    return ins
                P.op("pe", amm, [ut.b, hT.b], [psA.b])
                if pend is not None:
                    issue_omm(*pend)
                ge = ge_r.next(); hg = hg_r.next()
                Tw = TwH[i // 64]
                P.op("act", lambda e, ge=ge, psA=psA: e.activation(out=ge.t[:, 0:ntok], in_=psA.t[:, 0:ntok], func=AF.Gelu_apprx_tanh), [psA.b], [ge.b])
                P.op("dve", lambda e, ge=ge, hg=hg, i=i, Tw=Tw: e.tensor_tensor(out=hg.t[:, 0:ntok], in0=ge.t[:, 0:ntok], in1=Tw.t[:, 0:ntok, i % 64], op=ALU.mult),
                     [ge.b, Tw.b], [hg.b])
                pend = (i, hg, vt)
                yield
            issue_omm(*pend)
            for ti, (t0, n) in enumerate(tiles):
                xt = H["xt"].next(); x1 = x1_r.next()
                P.dma("sp", lambda e, xt=xt, t0=t0, n=n: e.dma_start(out=xt.t[:n], in_=x1s.t.ap()[g0 + t0:g0 + t0 + n, :]), [x1s.b], [xt.b])
                for half in range(2):
                    P.op("dve", lambda e, xt=xt, x1=x1, ti=ti, half=half, n=n: e.tensor_tensor(out=x1.t[:n, half * 512:(half + 1) * 512], in0=oap(outs[ti][half], n),
                                                                                             in1=xt.t[:n, half * 512:(half + 1) * 512], op=ALU.add),
                         [outs[ti][half].b, xt.b], [x1.b])
                out_d, orow = dests[ti]
                P.dma("sp", lambda e, x1=x1, n=n, out_d=out_d, orow=orow: e.dma_start(out=out_d.t.ap()[orow:orow + n, :], in_=x1.t[:n]), [x1.b], [out_d.b])
            yield

        def drain(g_):
            if g_ is None:
                return
            for _ in g_:
                pass

        def step(g_):
            if g_ is None:
                return
            try:
                next(g_)
            except StopIteration:
                pass
        blocks = [(h2T, sp_ * 256, 256, [(0, 128), (128, 128)], sp_ * 256, [(y_o, sp_ * 256), (y_o, sp_ * 256 + 128)]) for sp_ in range(7)]
        blocks.append((h2T, 7 * 256, 256 + NS, [(0, 128), (128, 128), (256, NS)], 7 * 256, [(y_o, 7 * 256), (y_o, 7 * 256 + 128), (ys_o, 0)]))
        drain(build_gen(blocks[0], 0))
        for bi, blk in enumerate(blocks):
            last = (bi == len(blocks) - 1)
            bB = build_gen(blk, 1)
            if last:
                drain(bB)
                bB = None
            lg = loop_gen(blk)
            bA = build_gen(blocks[bi + 1], 0) if not last else None
            for i in range(128):
                if i == 64:
                    drain(bB)
                next(lg)
                step(bB if i < 64 else bA)
            drain(lg)
            drain(bA)


    try:
        body()
    except _Stop:
        pass
    P.finish()
    return nc, list(dram_out.keys())


_CACHE = {}


def _consts():
    half = 16
    inv = (np.float32(500000.0) ** (-np.arange(half, dtype=np.float32) / np.float32(half))).astype(np.float32)

    def cs(pos):
        ang = (pos.astype(np.float32)[:, None] * inv[None, :]).astype(np.float32)
        c = np.cos(ang).astype(np.float32); s = np.sin(ang).astype(np.float32)
        return np.ascontiguousarray(np.concatenate([c, c, s, s], axis=1))
    ki = np.arange(128)[:, None]; qi = np.arange(128)[None, :]
    return dict(cs=cs, mprev=(ki >= qi).astype(np.float32), mcur=(ki <= qi).astype(np.float32),
                ident=np.eye(128, dtype=np.float32),
                iota=np.ascontiguousarray(np.broadcast_to(np.arange(128, dtype=np.float32), (128, 128))))


def _colT(v, n):
    return np.ascontiguousarray(np.asarray(v, np.float32).reshape(n, 128).T)


def make_in_maps(inp):
    C = _consts()
    f = lambda a: np.ascontiguousarray(np.asarray(a, dtype=np.float32))
    shared = dict(
        w_in=f(inp["w_in"][0]), w_mem=f(inp["w_mem_kv"][0]), w_a=f(inp["w_a_proj"][0]), w_b=f(inp["w_b_proj"][0]),
        w_m=f(inp["w_m_proj"][0]), w_o=f(inp["w_o"][0]), w_pq=f(inp["w_pq"][0]), u_tab=f(inp["u_tab"][0]), v_tab=f(inp["v_tab"][0]),
        skT=np.ascontiguousarray(np.asarray(inp["sub_keys"][0], np.float32).transpose(2, 0, 1)),
        g1T=_colT(inp["g_norm1"][0], 8), g2T=_colT(inp["g_norm2"][0], 8), gmT=_colT(inp["g_mem"][0], 8),
        bgT=_colT(inp["b_gate"][0], 24), bdwT=_colT(inp["b_dw"][0], 4), lngT=_colT(inp["ln_g"][0], 4), lnbT=_colT(inp["ln_b"][0], 4),
        wdwT=np.ascontiguousarray(np.asarray(inp["w_dw"][0], np.float32).T.reshape(4, 128, 31).transpose(1, 0, 2)),
        qna_bc=np.ascontiguousarray(np.broadcast_to(np.asarray(inp["qn_a"][0], np.float32), (128, 128))),
        kna_bc=np.ascontiguousarray(np.broadcast_to(np.asarray(inp["kn_a"][0], np.float32), (128, 128))),
        qnm_bc=np.ascontiguousarray(np.broadcast_to(np.asarray(inp["qn_m"][0], np.float32), (128, 128))),
        knm_bc=np.ascontiguousarray(np.broadcast_to(np.asarray(inp["kn_m"][0], np.float32), (128, 128))),
        ident=C["ident"], iota=C["iota"], mprev=C["mprev"], mcur=C["mcur"],
        css=C["cs"](np.full((NS,), 16384, dtype=np.int64)),
    )
    xp = np.asarray(inp["x_prompt"], np.float32)
    maps = []
    for c in range(NCORES):
        b, ch = c // 4, c % 4
        c0 = ch * CH
        m = dict(shared)
        m["xm"] = np.ascontiguousarray(xp[b, c0:c0 + CH])
        m["xh"] = np.ascontiguousarray(xp[b, c0 - CH:c0]) if ch > 0 else np.zeros((CH, 1024), np.float32)
        m["xs"] = np.ascontiguousarray(np.asarray(inp["x_sample"], np.float32)[c * NS:(c + 1) * NS, 0])
        m["memx"] = np.ascontiguousarray(np.asarray(inp["mem_prompt"], np.float32)[b])
        m["wkc"] = np.ascontiguousarray(np.asarray(inp["cache_win_k"], np.float32)[0, c * NS:(c + 1) * NS].reshape(NS, 2048, 512))
        m["wvc"] = np.ascontiguousarray(np.asarray(inp["cache_win_v"], np.float32)[0, c * NS:(c + 1) * NS].reshape(NS, 2048, 512))
        m["mkc"] = np.ascontiguousarray(np.asarray(inp["cache_mem_k"], np.float32)[0, c * NS:(c + 1) * NS].reshape(NS, 256, 512))
        m["mvc"] = np.ascontiguousarray(np.asarray(inp["cache_mem_v"], np.float32)[0, c * NS:(c + 1) * NS].reshape(NS, 256, 512))
        m["stc"] = np.ascontiguousarray(np.asarray(inp["state_conv"], np.float32)[0, c * NS:(c + 1) * NS])
        m["hb"] = np.full((128, 1), 1.0 if ch > 0 else 0.0, np.float32)
        m["csm"] = C["cs"](np.arange(c0, c0 + CH))
        m["csh"] = C["cs"](np.maximum(np.arange(c0 - CH, c0), 0))
        maps.append(m)
    return maps


def assemble(res):
    y = np.zeros((2, 8192, 1024), np.float32); ys = np.zeros((32, 1, 1024), np.float32)
    wk = np.zeros((1, 2, 2048, 4, 128), np.float32); wv = np.zeros_like(wk)
    mk = np.zeros((1, 2, 256, 4, 128), np.float32); mv = np.zeros_like(mk)
    cv = np.zeros((1, 2, 30, 512), np.float32)
    ks = np.zeros((1, 32, 1, 4, 128), np.float32); vs = np.zeros_like(ks); cs = np.zeros((1, 32, 30, 512), np.float32)
    for c in range(NCORES):
        r = res[c]
        b, ch = c // 4, c % 4
        y[b, ch * CH:(ch + 1) * CH] = r["y"]
        ys[c * NS:(c + 1) * NS, 0] = r["ys"]
        if ch == 3:
            wk[0, b] = r["ko"].reshape(2048, 4, 128); wv[0, b] = r["vo"].reshape(2048, 4, 128)
            cv[0, b] = r["cvo"]
        if ch == 0:
            mk[0, b] = r["mko"].reshape(256, 4, 128); mv[0, b] = r["mvo"].reshape(256, 4, 128)
        ks[0, c * NS:(c + 1) * NS, 0] = r["kso"].reshape(NS, 4, 128)
        vs[0, c * NS:(c + 1) * NS, 0] = r["vso"].reshape(NS, 4, 128)
        cs[0, c * NS:(c + 1) * NS] = r["cso"]
    return (y, ys, wk, wv, mk, mv, cv, ks, vs, cs)


def kernel(**inputs):
    if "nc" not in _CACHE:
        _CACHE["nc"] = build_program()[0]
    nc = _CACHE["nc"]
    maps = make_in_maps(inputs)
    res = run_bass_kernel_spmd(nc, maps, core_ids=list(range(NCORES)))
    return assemble(res.results)
```

```python
import numpy as np
import concourse.bass as bass
import concourse.mybir as mybir
from concourse.bass_utils import run_bass_kernel_spmd

F32 = mybir.dt.float32
BF16 = mybir.dt.bfloat16
U32 = mybir.dt.uint32
AF = mybir.ActivationFunctionType
ALU = mybir.AluOpType
AX = mybir.AxisListType

EPS = 1e-6
NCORES = 8
CH = 2048
NS = 4
SCALE = 128 ** -0.5
DEBUG = {}
PIPE_BLOCKS = True


class Buf:
    __slots__ = ("w", "r", "excl")

    def __init__(self):
        self.w = {}
        self.r = []
        self.excl = False


class Prog:
    NDMA = 12

    def __init__(self, nc):
        self.nc = nc
        self.names = ("pe", "act", "dve", "pool", "sp")
        self.lists = {k: [] for k in self.names}
        self.cnt = {k: 0 for k in self.names}
        self.sems = {}
        self._stack = []
        for k in self.names:
            self.sems[k] = self._sem("c_" + k)
        self.dq = {}
        for q in ("sp", "act", "pool"):
            self.dq[q] = {"sems": [self._sem(f"d_{q}{i}") for i in range(self.NDMA)],
                          "uses": [0] * self.NDMA, "n": 0, "last": [None] * self.NDMA}
        self.known = {k: {} for k in self.names}

    def _sem(self, name):
        cm = self.nc.semaphore(name)
        s = cm.__enter__()
        self._stack.append(cm)
        return s

    def _collect(self, eng, reads, writes):
        deps = {}

        def add(ev):
            if ev is None:
                return
            s, v = ev
            if deps.get(s.name, (None, -1))[1] < v:
                deps[s.name] = (s, v)
        for b in reads:
            for e in b.w.values():
                add(e)
        for b in writes:
            for e in b.w.values():
                add(e)
            for e in b.r:
                add(e)
        out = []
        kn = self.known[eng]
        for name, (s, v) in deps.items():
            if eng == "pe" and name == "c_pe":
                continue
            if kn.get(name, -1) >= v:
                continue
            kn[name] = v
            out.append((s, v))
        return out

    def _commit(self, ev, reads, writes):
        for b in reads:
            b.r.append(ev)
            if len(b.r) > 64:
                b.r = b.r[-48:] if False else b.r
        for b in writes:
            b.w[ev[0].name] = ev
            b.r = []

    def op(self, eng, fn, reads=(), writes=()):
        ex = [b for b in reads if b.excl]
        if ex:
            writes = list(writes) + ex
            reads = [b for b in reads if not b.excl]
        waits = self._collect(eng, reads, writes)
        self.cnt[eng] += 1
        ev = (self.sems[eng], self.cnt[eng])
        self.lists[eng].append((waits, fn, self.sems[eng], 1))
        self._commit(ev, reads, writes)
        return ev

    def dma(self, q, fn, reads=(), writes=()):
        d = self.dq[q]
        i = d["n"] % self.NDMA
        d["n"] += 1
        waits = self._collect(q, reads, writes)
        prev = d["last"][i]
        if prev is not None:
            kn = self.known[q]
            if kn.get(prev[0].name, -1) < prev[1]:
                kn[prev[0].name] = prev[1]
                waits.append(prev)
        d["uses"][i] += 1
        ev = (d["sems"][i], 16 * d["uses"][i])
        d["last"][i] = ev
        self.lists[q].append((waits, fn, d["sems"][i], 16))
        self._commit(ev, reads, writes)
        return ev

    def finish(self):
        fin = []
        for q, d in self.dq.items():
            for ev in d["last"]:
                if ev is not None:
                    fin.append(ev)
        engs = {"pe": "tensor", "act": "scalar", "dve": "vector", "pool": "gpsimd", "sp": "sync"}
        with self.nc.Block() as block:
            def mk(name):
                def body(e):
                    for waits, fn, sem, amt in self.lists[name]:
                        for (s, v) in waits:
                            e.wait_ge(s, v)
                        ins = fn(e)
                        ins.then_inc(sem, amt)
                    if name == "sp":
                        for (s, v) in fin:
                            e.wait_ge(s, v)
                return body
            for name in ("sp", "act", "dve", "pool", "pe"):
                getattr(block, engs[name])(mk(name))
        for cm in reversed(self._stack):
            cm.__exit__(None, None, None)
        self._stack = []


class TN:
    def __init__(self, t):
        self.t = t
        self.b = Buf()


class Rot:
    def __init__(self, items):
        self.items = items
        self.i = 0

    def next(self):
        x = self.items[self.i % len(self.items)]
        self.i += 1
        return x


class _Stop(Exception):
    pass


def build_program(dbg=(), stop=None):
    nc = bass.Bass("TRN2", target_bir_lowering=False)
    P = Prog(nc)
    dram_in = {}
    dram_out = {}

    def phase_end(k):
        if stop == k:
            raise _Stop()

    def body():
        def DI(name, shape, dt=F32):
            dram_in[name] = TN(nc.dram_tensor(name, list(shape), dt, kind="ExternalInput"))
            return dram_in[name]

        def DO(name, shape, dt=F32):
            dram_out[name] = TN(nc.dram_tensor(name, list(shape), dt, kind="ExternalOutput"))
            return dram_out[name]

        SB_LO, SB_HI = 16512, 229344
        free_list = [[SB_LO, SB_HI]]
        grave = []
        live = {}
        _uid = [0]

        def SB(name, shape, dt=F32):
            nbytes = int(np.prod(shape[1:])) * (4 if dt in (F32, U32) else 2)
            nbytes = (nbytes + 31) // 32 * 32
            big = nbytes >= 8192
            for fr in (reversed(free_list) if big else free_list):
                if fr[1] - fr[0] >= nbytes:
                    if big:
                        fr[1] -= nbytes
                        off = fr[1]
                    else:
                        off = fr[0]
                        fr[0] += nbytes
                    break
            else:
                raise RuntimeError(f"SBUF arena full allocating {name} {shape}: free={free_list}")
            _uid[0] += 1
            tn = TN(nc.alloc_sbuf_tensor_at(f"{name}_{_uid[0]}", list(shape), dt, offset=off))
            for (gs, ge, gb) in grave:
                if gs < off + nbytes and off < ge:
                    tn.b.r.extend(gb.w.values())
                    tn.b.r.extend(gb.r)
            live[id(tn)] = (off, off + nbytes)
            return tn

        def FREE(*tns):
            for tn in tns:
                if isinstance(tn, Rot):
                    FREE(*tn.items)
                    continue
                a, b = live.pop(id(tn))
                grave.append((a, b, tn.b))
                free_list.append([a, b])
            free_list.sort()
            merged = []
            for fr in free_list:
                if fr[1] <= fr[0]:
                    continue
                if merged and merged[-1][1] == fr[0]:
                    merged[-1][1] = fr[1]
                else:
                    merged.append(fr)
            free_list[:] = merged

        xm = DI("xm", [CH, 1024]); xh = DI("xh", [CH, 1024]); xs = DI("xs", [NS, 1024]); memx = DI("memx", [256, 1024])
        wkc = DI("wkc", [NS, 2048, 512]); wvc = DI("wvc", [NS, 2048, 512])
        mkc = DI("mkc", [NS, 256, 512]); mvc = DI("mvc", [NS, 256, 512]); stc = DI("stc", [NS, 30, 512])
        w_in = DI("w_in", [1024, 7168]); w_mem = DI("w_mem", [1024, 1024])
        w_a = DI("w_a", [512, 1024]); w_b = DI("w_b", [512, 1024]); w_m = DI("w_m", [512, 1024])
        w_o = DI("w_o", [1024, 1024]); w_pq = DI("w_pq", [1024, 2048])
        u_tab = DI("u_tab", [16384, 1024]); v_tab = DI("v_tab", [16384, 1024])
        skT_d = DI("skT", [128, 2, 128])
        g1T_d = DI("g1T", [128, 8]); g2T_d = DI("g2T", [128, 8]); gmT_d = DI("gmT", [128, 8])
        bgT_d = DI("bgT", [128, 24]); bdwT_d = DI("bdwT", [128, 4]); lngT_d = DI("lngT", [128, 4]); lnbT_d = DI("lnbT", [128, 4])
        wdwT_d = DI("wdwT", [128, 4, 31])
        qna_d = DI("qna_bc", [128, 128]); kna_d = DI("kna_bc", [128, 128]); qnm_d = DI("qnm_bc", [128, 128]); knm_d = DI("knm_bc", [128, 128])
        ident_d = DI("ident", [128, 128]); iota_d = DI("iota", [128, 128])
        mprev_d = DI("mprev", [128, 128]); mcur_d = DI("mcur", [128, 128]); hb_d = DI("hb", [128, 1])
        csm_d = DI("csm", [CH, 64]); csh_d = DI("csh", [CH, 64]); css_d = DI("css", [NS, 64])

        y_o = DO("y", [CH, 1024]); ys_o = DO("ys", [NS, 1024])
        ko_o = DO("ko", [CH, 512]); vo_o = DO("vo", [CH, 512])
        mko_o = DO("mko", [256, 512]); mvo_o = DO("mvo", [256, 512]); cvo_o = DO("cvo", [30, 512])
        kso_o = DO("kso", [NS, 512]); vso_o = DO("vso", [NS, 512]); cso_o = DO("cso", [NS, 30, 512])
        x1s = TN(nc.dram_tensor("x1s", [CH + NS, 1024], F32))
        uTs = TN(nc.dram_tensor("uTs", [128, 128, 1024], BF16))
        vbs = TN(nc.dram_tensor("vbs", [128, 128, 1024], BF16))

        def dbg_out(name, tn, ap, shape, dt=F32):
            if name in dbg:
                o = DO("dbg_" + name, shape, dt)
                P.dma("sp", lambda e: e.dma_start(out=o.t.ap(), in_=ap), [tn.b], [o.b])

        PS = [TN(nc.alloc_psum_tensor(f"ps{i}", [128, 512], F32)) for i in range(6)]
        PSB = [TN(nc.alloc_psum_tensor(f"psb{i}", [128, 1024], BF16)) for i in range(2)]
        for _p in PS + PSB:
            _p.b.excl = True
        psr = Rot(PS[0:4])
        psbr = Rot(PSB)

        identf = SB("identf", [128, 128]); identb = SB("identb", [128, 128], BF16)
        iota = SB("iota", [128, 128]); onesb = SB("onesb", [128, 128], BF16); onesf = SB("onesf", [128, 128])
        maskA = SB("maskA", [128, 2, 128], BF16); maskH = SB("maskH", [128, 2, 128], BF16)
        hb = SB("hb", [128, 1]); mtmp = SB("mtmp", [128, 2, 128])
        g1T = SB("g1T", [128, 8]); g2T = SB("g2T", [128, 8]); gmT = SB("gmT", [128, 8])
        bgT = SB("bgT", [128, 24]); bdwT = SB("bdwT", [128, 4]); lngT = SB("lngT", [128, 4]); lnbT = SB("lnbT", [128, 4])
        wdwT = SB("wdwT", [128, 4, 31])
        qna = SB("qna", [128, 128]); kna = SB("kna", [128, 128]); qnm = SB("qnm", [128, 128]); knm = SB("knm", [128, 128])
        skT = SB("skT", [128, 2, 128], BF16)

        def ld(dst, src, q="sp"):
            P.dma(q, lambda e: e.dma_start(out=dst.t[:], in_=src.t.ap()), [src.b], [dst.b])
        for dst, src in ((identf, ident_d), (iota, iota_d), (hb, hb_d), (g1T, g1T_d), (g2T, g2T_d), (gmT, gmT_d),
                         (bgT, bgT_d), (bdwT, bdwT_d), (lngT, lngT_d), (lnbT, lnbT_d), (wdwT, wdwT_d),
                         (qna, qna_d), (kna, kna_d), (qnm, qnm_d), (knm, knm_d)):
            ld(dst, src)
        ld(skT, skT_d, "pool")
        P.dma("sp", lambda e: e.dma_start(out=mtmp.t[:, 0, :], in_=mprev_d.t.ap()), [], [mtmp.b])
        P.dma("sp", lambda e: e.dma_start(out=mtmp.t[:, 1, :], in_=mcur_d.t.ap()), [mtmp.b], [mtmp.b])
        P.op("dve", lambda e: e.tensor_copy(out=identb.t[:], in_=identf.t[:]), [identf.b], [identb.b])
        P.op("dve", lambda e: e.memset(onesb.t[:], 1.0), [], [onesb.b])
        P.op("dve", lambda e: e.memset(onesf.t[:], 1.0), [], [onesf.b])
        P.op("dve", lambda e: e.tensor_copy(out=maskA.t[:], in_=mtmp.t[:]), [mtmp.b], [maskA.b])
        P.op("dve", lambda e: e.tensor_copy(out=maskH.t[:, 1, :], in_=mtmp.t[:, 1, :]), [mtmp.b], [maskH.b])
        P.op("dve", lambda e: e.tensor_scalar(out=maskH.t[:, 0, :], in0=mtmp.t[:, 0, :], scalar1=hb.t[:, 0:1], scalar2=None,
                                              op0=ALU.mult), [mtmp.b, hb.b, maskH.b], [maskH.b])

        hTm = SB("hTm", [128, 8, CH], BF16)
        hTs = SB("hTs", [128, 8, NS], BF16)
        hTh = SB("hTh", [128, 8, CH], BF16)
        hTmem = SB("hTmem", [128, 8, 256], BF16)
        AT = SB("AT", [128, 4, CH], BF16); AsT = SB("AsT", [128, 4, NS], BF16)

        H = {}

        def alloc_xbufs():
            H["xt"] = Rot([SB(f"xt{i}", [128, 1024]) for i in range(2)])
            H["xb"] = Rot([SB(f"xb{i}", [128, 1024], BF16) for i in range(2)])
        alloc_xbufs()
        junk = SB("junk", [128, 1024], BF16)
        ss_r = Rot([SB(f"ss{i}", [128, 8]) for i in range(8)])

        def norm_T(src, n, gT, dst, c0):
            ss = ss_r.next(); xb = H["xb"].next()
            P.op("pool", lambda e: e.memset(ss.t[:n, 0:1], 0.0), [], [ss.b])
            P.op("act", lambda e: e.activation(out=junk.t[:n], in_=src.t[:n], func=AF.Square, accum_out=ss.t[:n, 0:1]),
                 [src.b, ss.b], [ss.b])
            P.op("act", lambda e: e.activation(out=ss.t[:n, 0:1], in_=ss.t[:n, 0:1], func=AF.Sqrt, bias=EPS, scale=1.0 / 1024),
                 [ss.b], [ss.b])
            P.op("dve", lambda e: e.reciprocal(out=ss.t[:n, 0:1], in_=ss.t[:n, 0:1]), [ss.b], [ss.b])
            P.op("dve", lambda e: e.tensor_scalar(out=xb.t[:n], in0=src.t[:n], scalar1=ss.t[:n, 0:1], scalar2=None, op0=ALU.mult),
                 [src.b, ss.b], [xb.b])
            ps = psbr.next()

            def tr(e):
                for kc in range(8):
                    ins = e.transpose(out=ps.t[:, kc * 128:kc * 128 + n], in_=xb.t[:n, kc * 128:(kc + 1) * 128], identity=identb.t[:n, :n])
                return ins
            P.op("pe", tr, [xb.b, identb.b], [ps.b])
            psv = ps.t[:].rearrange("p (k t) -> p k t", k=8)
            P.op("dve", lambda e: e.tensor_tensor(out=dst.t[:, :, c0:c0 + n], in0=psv[:, :, 0:n],
                                                  in1=gT.t[:, :].unsqueeze(2).to_broadcast([128, 8, n]), op=ALU.mult),
                 [ps.b, gT.b], [dst.b])

        def load_norm(src_d, r0, n, gT, dst, c0):
            xt = H["xt"].next()
            P.dma("sp", lambda e: e.dma_start(out=xt.t[:n], in_=src_d.t.ap()[r0:r0 + n, :]), [src_d.b], [xt.b])
            norm_T(xt, n, gT, dst, c0)

        for t in range(16):
            load_norm(xh, t * 128, 128, g1T, hTh, t * 128)
        for t in range(16):
            load_norm(xm, t * 128, 128, g1T, hTm, t * 128)
        load_norm(xs, 0, NS, g1T, hTs, 0)
        for t in range(2):
            load_norm(memx, t * 128, 128, gmT, hTmem, t * 128)
        dbg_out("hTm", hTm, hTm.t[:, 0, 0:256], [128, 256], BF16)
        FREE(H["xt"], H["xb"])

        phase_end(1)
        H["wb"] = Rot([SB(f"wb{i}", [128, 8, 512], BF16) for i in range(2)])

        def load_w(src, r0, nk, c0, ncols, dst=None, dcol=0):
            wb = dst if dst is not None else H["wb"].next()
            sap = src.t.ap()[r0:r0 + nk * 128, c0:c0 + ncols].rearrange("(kc p) n -> p kc n", p=128)
            P.dma("pool", lambda e: e.dma_start(out=wb.t[:, 0:nk, dcol:dcol + ncols], in_=sap), [src.b], [wb.b])
            return wb

        def mm_tm(ps, n, hT, cols, wb, wc0, wn, po=0):
            def f(e):
                for kc in range(8):
                    ins = e.matmul(ps.t[:n, po:po + wn], lhsT=hT.t[:, kc, cols], rhs=wb.t[:, kc, wc0:wc0 + wn],
                                   start=(kc == 0), stop=(kc == 7))
                return ins
            P.op("pe", f, [hT.b, wb.b], [ps.b])

        def mm_fm(ps, N, wb, wc0, hT, cols, nk=8, po=0):
            def f(e):
                for kc in range(nk):
                    ins = e.matmul(ps.t[:, po:po + N], lhsT=wb.t[:, kc, wc0:wc0 + 128], rhs=hT.t[:, kc, cols],
                                   start=(kc == 0), stop=(kc == nk - 1))
                return ins
            P.op("pe", f, [hT.b, wb.b], [ps.b])

        kf_r = Rot([SB(f"kf{i}", [128, 512]) for i in range(4)])
        kb_r = Rot([SB(f"kb{i}", [128, 512], BF16) for i in range(4)])
        cs_r = Rot([SB(f"cs{i}", [128, 64]) for i in range(4)])
        rA_r = Rot([SB(f"rA{i}", [128, 4, 32]) for i in range(4)]); rB_r = Rot([SB(f"rB{i}", [128, 4, 32]) for i in range(4)])
        vf_r = Rot([SB(f"vf{i}", [128, 512]) for i in range(2)])
        sg_r = Rot([SB(f"sg{i}", [128, 512]) for i in range(2)])

        def run_pipe(gens, depth=4, side=None):
            gens = list(gens)
            active = []
            gi = 0
            while gi < len(gens) or active:
                while gi < len(gens) and len(active) < depth:
                    active.append(gens[gi]); gi += 1
                if side is not None:
                    try:
                        next(side)
                    except StopIteration:
                        side = None
                for g_ in list(active):
                    try:
                        next(g_)
                    except StopIteration:
                        active.remove(g_)

        def qk_gen(hT, cols, wb, wc0, n, nh, gain, cs_src, cs_r0, out_d, out_r0, out_c0, dstT, c0):
            W = nh * 128
            ss = ss_r.next(); kf = kf_r.next(); kb = kb_r.next()
            ps = psr.next()
            if cs_src is not None:
                cs = cs_r.next(); rA = rA_r.next(); rB = rB_r.next()
                P.dma("sp", lambda e: e.dma_start(out=cs.t[:n], in_=cs_src.t.ap()[cs_r0:cs_r0 + n, :]), [cs_src.b], [cs.b])
            mm_tm(ps, n, hT, cols, wb, wc0, W)
            P.op("pool", lambda e: e.memset(ss.t[:n, 0:nh], 0.0), [], [ss.b])
            yield

            def sq(e):
                for h in range(nh):
                    ins = e.activation(out=junk.t[:n, h * 128:(h + 1) * 128], in_=ps.t[:n, h * 128:(h + 1) * 128], func=AF.Square,
                                       accum_out=ss.t[:n, h:h + 1])
                return ins
            P.op("act", sq, [ps.b, ss.b], [ss.b])
            P.op("act", lambda e: e.activation(out=ss.t[:n, 0:nh], in_=ss.t[:n, 0:nh], func=AF.Sqrt, bias=EPS, scale=1.0 / 128),
                 [ss.b], [ss.b])
            yield
            P.op("dve", lambda e: e.reciprocal(out=ss.t[:n, 0:nh], in_=ss.t[:n, 0:nh]), [ss.b], [ss.b])

            def sc_(e):
                for h in range(nh):
                    ins = e.scalar_tensor_tensor(out=kf.t[:n, h * 128:(h + 1) * 128], in0=ps.t[:n, h * 128:(h + 1) * 128],
                                                 scalar=ss.t[:n, h:h + 1], in1=gain.t[:n, :], op0=ALU.mult, op1=ALU.mult)
                return ins
            P.op("dve", sc_, [ps.b, ss.b, gain.b], [kf.b])
            yield
            if cs_src is not None:
                kv = kf.t[:n, 0:W].rearrange("p (h d) -> p h d", h=nh)

                def rot1(e):
                    e.tensor_tensor(out=rA.t[:n, 0:nh], in0=kv[:, :, 0:32], in1=cs.t[:n, 0:32].unsqueeze(1).to_broadcast([n, nh, 32]), op=ALU.mult)
                    return e.tensor_tensor(out=rB.t[:n, 0:nh], in0=kv[:, :, 0:32], in1=cs.t[:n, 32:64].unsqueeze(1).to_broadcast([n, nh, 32]), op=ALU.mult)
                P.op("dve", rot1, [kf.b, cs.b], [rA.b, rB.b])
                yield

                def rot2(e):
                    e.tensor_tensor(out=kv[:, :, 0:16], in0=rA.t[:n, 0:nh, 0:16], in1=rB.t[:n, 0:nh, 16:32], op=ALU.subtract)
                    return e.tensor_tensor(out=kv[:, :, 16:32], in0=rA.t[:n, 0:nh, 16:32], in1=rB.t[:n, 0:nh, 0:16], op=ALU.add)
                P.op("dve", rot2, [rA.b, rB.b], [kf.b])
                yield
            if out_d is not None:
                P.dma("sp", lambda e: e.dma_start(out=out_d.t.ap()[out_r0:out_r0 + n, out_c0:out_c0 + W], in_=kf.t[:n, 0:W]), [kf.b], [out_d.b])
            P.op("act", lambda e: e.copy(out=kb.t[:n, 0:W], in_=kf.t[:n, 0:W]), [kf.b], [kb.b])
            yield
            pt = psbr.next()

            def tr(e):
                for h in range(nh):
                    ins = e.transpose(out=pt.t[:, h * 128:h * 128 + n], in_=kb.t[:n, h * 128:(h + 1) * 128], identity=identb.t[:n, :n])
                return ins
            P.op("pe", tr, [kb.b, identb.b], [pt.b])
            ptv = pt.t[:, 0:W].rearrange("p (h t) -> p h t", h=nh)
            P.op("act", lambda e: e.copy(out=dstT.t[:, 0:nh, c0:c0 + n], in_=ptv[:, :, 0:n]), [pt.b], [dstT.b])

        phase_end(12)
        def glu_cols(wu, hT, cols, N, dst_ap_fn, tail_fn=None):
            for j in range(4):
                pg = psr.next()
                mm_fm(pg, N, wu, (4 + j) * 128, hT, cols)
                sg = sg_r.next()
                P.op("act", lambda e, pg=pg, sg=sg: e.activation(out=sg.t[:, 0:N], in_=pg.t[:, 0:N], func=AF.Sigmoid), [pg.b], [sg.b])
                pv = psr.next()
                mm_fm(pv, N, wu, j * 128, hT, cols)
                dst_tn, dst_ap = dst_ap_fn(j)
                P.op("dve", lambda e, pv=pv, sg=sg, dst_ap=dst_ap: e.tensor_tensor(out=dst_ap, in0=pv.t[:, 0:N], in1=sg.t[:, 0:N], op=ALU.mult),
                     [pv.b, sg.b], [dst_tn.b])
                if tail_fn is not None:
                    t_tn, t_ap = tail_fn(j)
                    P.op("dve", lambda e, pv=pv, sg=sg, t_ap=t_ap: e.tensor_tensor(out=t_ap, in0=pv.t[:, N - 30:N], in1=sg.t[:, N - 30:N], op=ALU.mult),
                         [pv.b, sg.b], [t_tn.b])

        def load_wu():
            wu = SB("wu", [128, 8, 1024], BF16)
            load_w(w_in, 0, 8, 2560, 512, wu, 0); load_w(w_in, 0, 8, 3072, 512, wu, 512)
            return wu
        gluH = SB("gluH", [128, 4, 128])
        wu = load_wu()
        glu_cols(wu, hTh, slice(CH - 128, CH), 128, lambda j: (gluH, gluH.t[:, j, :]))
        FREE(wu)

        phase_end(15)
        mkT = SB("mkT", [128, 4, 256], BF16); mvb = SB("mvb", [128, 2, 512], BF16)
        wmk = load_w(w_mem, 0, 8, 0, 512); wmv = load_w(w_mem, 0, 8, 512, 512)
        for t in range(2):
            run_pipe([qk_gen(hTmem, slice(t * 128, (t + 1) * 128), wmk, 0, 128, 4, knm, None, 0, mko_o, t * 128, 0, mkT, t * 128)])
            ps2 = psr.next()
            mm_tm(ps2, 128, hTmem, slice(t * 128, (t + 1) * 128), wmv, 0, 512)
            vf = vf_r.next()
            P.op("dve", lambda e, ps2=ps2, vf=vf: e.tensor_copy(out=vf.t[:], in_=ps2.t[:]), [ps2.b], [vf.b])
            P.op("act", lambda e, ps2=ps2, t=t: e.copy(out=mvb.t[:, t, :], in_=ps2.t[:]), [ps2.b], [mvb.b])
            P.dma("sp", lambda e, vf=vf, t=t: e.dma_start(out=mvo_o.t.ap()[t * 128:(t + 1) * 128, :], in_=vf.t[:]), [vf.b], [mvo_o.b])
        FREE(hTmem)
        FREE(H["wb"])
        H["wb"] = Rot([SB(f"wbs{i}", [128, 8, 128], BF16) for i in range(3)])

        phase_end(2)
        KT = SB("KT", [128, 1, 2 * CH], BF16); KTs = SB("KTs", [128, 1, NS], BF16)
        QT = SB("QT", [128, 1, CH], BF16); QTs = SB("QTs", [128, 3, NS], BF16); qs_tmp = SB("qs_tmp", [128, 1, NS], BF16)
        Vg = SB("Vg", [128, 32, 128], BF16)
        vrow = SB("vrow", [1, NS, 128], BF16)
        acc = SB("acc", [128, 2, CH])
        Pt_r = Rot([SB(f"Pt{i}", [128, 2, 128], BF16) for i in range(3)])
        kcs_all = [[SB(f"kcs{i}_{g}", [128, 128], BF16) for g in range(3)] for i in range(NS)]
        vcs_all = [[SB(f"vcs{i}_{g}", [128, 128], BF16) for g in range(3)] for i in range(NS)]
        kcT_r = Rot([SB(f"kcT{i}", [128, 128], BF16) for i in range(2)])
        wvb = SB("wvb", [128, 8, 128], BF16)
        pS = SB("pS", [128, 3, 2], BF16)
        psOD = PS[4]; psSm = PS[5]
        psOD_r = Rot([PS[4], PS[5]])
        DILS = (1, 4, 16)

        ub_r = Rot([SB(f"ub{i}", [128, 1024], BF16) for i in range(3)])
        ut_r = Rot([SB(f"ut{i}", [128, 1024], BF16) for i in range(4)])
        vb_r = Rot([SB(f"vb{i}", [128, 1024], BF16) for i in range(3)])

        def prep_tables():
            def loads(i):
                ub = ub_r.next(); vb = vb_r.next()
                P.dma("pool", lambda e, ub=ub, i=i: e.dma_start(out=ub.t[:], in_=u_tab.t.ap()[i * 128:(i + 1) * 128, :]), [u_tab.b], [ub.b])
                P.dma("pool", lambda e, vb=vb, i=i: e.dma_start(out=vb.t[:], in_=v_tab.t.ap()[i * 128:(i + 1) * 128, :]), [v_tab.b], [vb.b])
                return ub, vb
            nxt = loads(0)
            for i in range(128):
                ub, vb = nxt
                if i + 1 < 128:
                    nxt = loads(i + 1)
                pt = psbr.next()

                def tru(e, pt=pt, ub=ub):
                    for kc in range(8):
                        ins = e.transpose(out=pt.t[:, kc * 128:(kc + 1) * 128], in_=ub.t[:, kc * 128:(kc + 1) * 128], identity=identb.t[:])
                    return ins
                P.op("pe", tru, [ub.b, identb.b], [pt.b])
                ut = ut_r.next()
                if i % 2 == 0:
                    P.op("act", lambda e, pt=pt, ut=ut: e.copy(out=ut.t[:], in_=pt.t[:]), [pt.b], [ut.b])
                else:
                    P.op("dve", lambda e, pt=pt, ut=ut: e.tensor_copy(out=ut.t[:], in_=pt.t[:]), [pt.b], [ut.b])
                P.dma("sp", lambda e, ut=ut, i=i: e.dma_start(out=uTs.t.ap()[i], in_=ut.t[:]), [ut.b], [uTs.b])
                P.dma("sp", lambda e, vb=vb, i=i: e.dma_start(out=vbs.t.ap()[i], in_=vb.t[:]), [vb.b], [vbs.b])
                yield
        prep_gen = prep_tables()
        blk_ctr = [0]

        def prep_step(k):
            for _ in range(k):
                try:
                    next(prep_gen)
                except StopIteration:
                    return

        def prep_side(every, count):
            n_ = 0; k_ = 0
            while k_ < count:
                n_ += 1
                if n_ % every == 0:
                    prep_step(1); k_ += 1
                yield

        for hp in range(4):
            wk = load_w(w_in, 0, 8, 1536 + hp * 128, 128)
            for i_ in range(NS):
                for g_, d_ in enumerate(DILS):
                    rows_ = slice(2048 - 128 * d_, 2048, d_)
                    kcs_ = kcs_all[i_][g_]; vcs_ = vcs_all[i_][g_]
                    P.dma("pool", lambda e, kcs_=kcs_, rows_=rows_, hp=hp, i_=i_: e.dma_start(out=kcs_.t[:], in_=wkc.t.ap()[i_, rows_, hp * 128:(hp + 1) * 128]), [wkc.b], [kcs_.b])
                    P.dma("pool", lambda e, vcs_=vcs_, rows_=rows_, hp=hp, i_=i_: e.dma_start(out=vcs_.t[:], in_=wvc.t.ap()[i_, rows_, hp * 128:(hp + 1) * 128]), [wvc.b], [vcs_.b])
            kg = []
            for t in range(32):
                hT = hTh if t < 16 else hTm
                tt = t % 16
                kg.append(qk_gen(hT, slice(tt * 128, (tt + 1) * 128), wk, 0, 128, 1, kna, csh_d if t < 16 else csm_d, tt * 128,
                                 ko_o if t >= 16 else None, tt * 128, hp * 128, KT, t * 128))
            kg.append(qk_gen(hTs, slice(0, NS), wk, 0, NS, 1, kna, css_d, 0, kso_o, 0, hp * 128, KTs, 0))
            run_pipe(kg, side=prep_side(5, 10))
            wv = load_w(w_in, 0, 8, 2048 + hp * 128, 128, wvb, 0)
            for t in range(16):
                ps = psr.next()
                mm_tm(ps, 128, hTm, slice(t * 128, (t + 1) * 128), wv, 0, 128)
                vf = vf_r.next()
                P.op("act", lambda e, ps=ps, vf=vf: e.copy(out=vf.t[:, 0:128], in_=ps.t[:, 0:128]), [ps.b], [vf.b])
                P.dma("sp", lambda e, vf=vf, t=t, hp=hp: e.dma_start(out=vo_o.t.ap()[t * 128:(t + 1) * 128, hp * 128:(hp + 1) * 128], in_=vf.t[:, 0:128]),
                      [vf.b], [vo_o.b])
                if t % 4 == 3:
                    prep_step(1)
            ps = psr.next()
            mm_tm(ps, NS, hTs, slice(0, NS), wv, 0, 128)
            vf = vf_r.next()
            P.op("act", lambda e, ps=ps, vf=vf: e.copy(out=vf.t[:NS, 0:128], in_=ps.t[:NS, 0:128]), [ps.b], [vf.b])
            P.dma("sp", lambda e, vf=vf, hp=hp: e.dma_start(out=vso_o.t.ap()[:, hp * 128:(hp + 1) * 128], in_=vf.t[:NS, 0:128]), [vf.b], [vso_o.b])
            for i in range(NS):
                ps = psr.next()
                mm_tm(ps, 1, hTs, slice(i, i + 1), wv, 0, 128)
                P.op("act", lambda e, ps=ps, i=i: e.copy(out=vrow.t[0:1, i, :], in_=ps.t[0:1, 0:128]), [ps.b], [vrow.b])
            for g, d in enumerate(DILS):
                nbs = CH // (128 * d)

                def vblk(r, j, nbs=nbs):
                    return r * (nbs + 1) + j
                for r in range(d):
                    for j in range(nbs + 1):
                        ps = psr.next()
                        if j == 0:
                            hT, cols = hTh, slice(CH - 128 * d + r, CH, d)
                        else:
                            hT, cols = hTm, slice(128 * d * (j - 1) + r, 128 * d * j, d)
                        mm_tm(ps, 128, hT, cols, wv, 0, 128)
                        bi = vblk(r, j)
                        if bi % 2 == 0:
                            P.op("act", lambda e, ps=ps, bi=bi: e.copy(out=Vg.t[:, bi, :], in_=ps.t[:, 0:128]), [ps.b], [Vg.b])
                        else:
                            P.op("dve", lambda e, ps=ps, bi=bi: e.tensor_copy(out=Vg.t[:, bi, :], in_=ps.t[:, 0:128]), [ps.b], [Vg.b])
                wq = load_w(w_in, 0, 8, g * 512 + hp * 128, 128)
                qg = [qk_gen(hTm, slice(t * 128, (t + 1) * 128), wq, 0, 128, 1, qna, csm_d, t * 128, None, 0, 0, QT, t * 128) for t in range(16)]
                qg.append(qk_gen(hTs, slice(0, NS), wq, 0, NS, 1, qna, css_d, 0, None, 0, 0, qs_tmp, 0))
                run_pipe(qg, side=prep_side(8, 3))
                P.op("pool", lambda e, g=g: e.tensor_copy(out=QTs.t[:, g, :], in_=qs_tmp.t[:, 0, :]), [qs_tmp.b], [QTs.b])
                pend_b = None

                def issue_pv(Pt, bp, bc_, qcols, g=g):
                    pod = psOD_r.next()
                    odv = pod.t[:, 0:256].rearrange("p (o q) -> p o q", o=2)

                    def pv(e, Pt=Pt, odv=odv, bp=bp, bc_=bc_):
                        e.matmul(odv[:, 0, :], lhsT=Vg.t[:, bp, :], rhs=Pt.t[:, 0, :], start=True, stop=False)
                        e.matmul(odv[:, 0, :], lhsT=Vg.t[:, bc_, :], rhs=Pt.t[:, 1, :], start=False, stop=True)
                        e.matmul(odv[:, 1, :], lhsT=onesb.t[:], rhs=Pt.t[:, 0, :], start=True, stop=False)
                        return e.matmul(odv[:, 1, :], lhsT=onesb.t[:], rhs=Pt.t[:, 1, :], start=False, stop=True)
                    P.op("pe", pv, [Pt.b, Vg.b, onesb.b], [pod.b])
                    if g == 0:
                        P.op("dve", lambda e, odv=odv, qcols=qcols: e.tensor_copy(out=acc.t[:, :, qcols], in_=odv), [pod.b], [acc.b])
                    else:
                        P.op("dve", lambda e, odv=odv, qcols=qcols: e.tensor_tensor(out=acc.t[:, :, qcols], in0=acc.t[:, :, qcols], in1=odv, op=ALU.add),
                             [pod.b, acc.b], [acc.b])
                for r in range(d):
                    for nb in range(nbs):
                        qcols = slice(128 * d * nb + r, 128 * d * (nb + 1), d)
                        ccols = slice(CH + 128 * d * nb + r, CH + 128 * d * (nb + 1), d)
                        if nb == 0:
                            pcols = slice(CH - 128 * d + r, CH, d)
                        else:
                            pcols = slice(CH + 128 * d * (nb - 1) + r, CH + 128 * d * nb, d)
                        psS = psr.next()
                        psSv = psS.t[:, 0:256].rearrange("p (c q) -> p c q", c=2)

                        def smm(e, psSv=psSv, qcols=qcols, ccols=ccols, pcols=pcols):
                            e.matmul(psSv[:, 0, :], lhsT=KT.t[:, 0, pcols], rhs=QT.t[:, 0, qcols], start=True, stop=True)
                            return e.matmul(psSv[:, 1, :], lhsT=KT.t[:, 0, ccols], rhs=QT.t[:, 0, qcols], start=True, stop=True)
                        P.op("pe", smm, [KT.b, QT.b], [psS.b])
                        if pend_b is not None and PIPE_BLOCKS:
                            issue_pv(*pend_b)
                        Pt = Pt_r.next()
                        P.op("act", lambda e, Pt=Pt, psSv=psSv: e.activation(out=Pt.t[:], in_=psSv, func=AF.Exp, scale=SCALE), [psS.b], [Pt.b])
                        mk_ = maskH if nb == 0 else maskA
                        P.op("dve", lambda e, Pt=Pt, mk_=mk_: e.tensor_tensor(out=Pt.t[:], in0=Pt.t[:], in1=mk_.t[:], op=ALU.mult),
                             [Pt.b, mk_.b], [Pt.b])
                        pend_b = (Pt, vblk(r, nb), vblk(r, nb + 1), qcols)
                        blk_ctr[0] += 1
                        if blk_ctr[0] % 5 == 0:
                            prep_step(1)
                        if not PIPE_BLOCKS:
                            issue_pv(*pend_b)
                if PIPE_BLOCKS:
                    issue_pv(*pend_b)
            P.op("dve", lambda e: e.reciprocal(out=acc.t[:, 1], in_=acc.t[:, 1]), [acc.b], [acc.b])
            P.op("dve", lambda e, hp=hp: e.tensor_tensor(out=AT.t[:, hp, :], in0=acc.t[:, 0], in1=acc.t[:, 1], op=ALU.mult), [acc.b], [AT.b])
            for i in range(NS):
                odv = psOD.t[:, 0:2]
                vts = []
                for g, d in enumerate(DILS):
                    kcs = kcs_all[i][g]; vcs = vcs_all[i][g]
                    pt = psbr.next()
                    P.op("pe", lambda e, pt=pt, kcs=kcs: e.transpose(out=pt.t[:, 0:128], in_=kcs.t[:], identity=identb.t[:]), [kcs.b, identb.b], [pt.b])
                    kcT = kcT_r.next()
                    P.op("act", lambda e, pt=pt, kcT=kcT: e.copy(out=kcT.t[:], in_=pt.t[:, 0:128]), [pt.b], [kcT.b])

                    def ssm(e, kcT=kcT, g=g, i=i):
                        e.matmul(psSm.t[:, 2 * g:2 * g + 1], lhsT=kcT.t[:], rhs=QTs.t[:, g, i:i + 1], start=True, stop=True)
                        return e.matmul(psSm.t[0:1, 2 * g + 1:2 * g + 2], lhsT=KTs.t[:, 0, i:i + 1], rhs=QTs.t[:, g, i:i + 1], start=True, stop=True)
                    P.op("pe", ssm, [kcT.b, QTs.b, KTs.b], [psSm.b])
                    vts.append(vcs)

                def sexp(e):
                    e.activation(out=pS.t[:, :, 0], in_=psSm.t[:, 0:6:2], func=AF.Exp, scale=SCALE)
                    return e.activation(out=pS.t[0:1, :, 1], in_=psSm.t[0:1, 1:6:2], func=AF.Exp, scale=SCALE)
                P.op("act", sexp, [psSm.b], [pS.b])

                def spv(e, vts=vts, i=i, odv=odv):
                    for o in range(2):
                        for g in range(3):
                            lc = vts[g].t[:] if o == 0 else onesb.t[:]
                            ls = vrow.t[0:1, i, :] if o == 0 else onesb.t[0:1, :]
                            e.matmul(odv[:, o:o + 1], lhsT=lc, rhs=pS.t[:, g, 0:1], start=(g == 0), stop=False)
                            ins = e.matmul(odv[:, o:o + 1], lhsT=ls, rhs=pS.t[0:1, g, 1:2], start=False, stop=(g == 2))
                    return ins
                P.op("pe", spv, [pS.b, vrow.b, onesb.b] + [v.b for v in vts], [psOD.b])
                rs = ss_r.next()
                P.op("dve", lambda e, rs=rs, odv=odv: e.reciprocal(out=rs.t[:, 0:1], in_=odv[:, 1:2]), [psOD.b], [rs.b])
                P.op("dve", lambda e, rs=rs, odv=odv, hp=hp, i=i: e.tensor_tensor(out=AsT.t[:, hp, i:i + 1], in0=odv[:, 0:1], in1=rs.t[:, 0:1], op=ALU.mult),
                     [psOD.b, rs.b], [AsT.b])
        dbg_out("AT", AT, AT.t[:, 0, 0:256], [128, 256], BF16)
        prep_step(128)
        FREE(hTh, KT, KTs, QT, QTs, qs_tmp, Vg, vrow, acc, Pt_r, kcT_r, pS, wvb)
        for i_ in range(NS):
            FREE(*kcs_all[i_], *vcs_all[i_])
        FREE(ub_r, vb_r, ut_r)
        FREE(H["wb"])
        H["wb"] = Rot([SB(f"wb{i}", [128, 8, 512], BF16) for i in range(2)])

        phase_end(3)
        bT = SB("bT", [128, 4, CH], BF16); bsT = SB("bsT", [128, 4, NS], BF16)
        gluT = SB("gluT", [128, 4, 30 + CH], BF16)
        gluL = SB("gluL", [128, 4, 30])
        Dg = SB("Dg", [128, 4, 31, 128], BF16)
        gluS = SB("gluS", [128, 4, NS]); convS = SB("convS", [128, 4, NS])
        wu = load_wu()
        P.op("dve", lambda e: e.tensor_scalar(out=gluT.t[:, :, 0:30], in0=gluH.t[:, :, 98:128], scalar1=hb.t[:, 0:1], scalar2=None, op0=ALU.mult),
             [gluH.b, hb.b], [gluT.b])
        for st in range(4):
            glu_cols(wu, hTm, slice(st * 512, (st + 1) * 512), 512, lambda j, st=st: (gluT, gluT.t[:, j, 30 + st * 512:30 + (st + 1) * 512]),
                     (lambda j: (gluL, gluL.t[:, j, :])) if st == 3 else None)
        glu_cols(wu, hTs, slice(0, NS), NS, lambda j: (gluS, gluS.t[:, j, :]))
        gst = SB("gst", [NS, 512])
        psa_ = psr.next()
        mm_tm(psa_, NS, hTs, slice(0, NS), wu, 512, 512)
        sgs_ = sg_r.next()
        P.op("act", lambda e: e.activation(out=sgs_.t[:NS, :], in_=psa_.t[:NS, :], func=AF.Sigmoid), [psa_.b], [sgs_.b])
        psb_ = psr.next()
        mm_tm(psb_, NS, hTs, slice(0, NS), wu, 0, 512)
        P.op("dve", lambda e: e.tensor_tensor(out=gst.t[:], in0=psb_.t[:NS, :], in1=sgs_.t[:NS, :], op=ALU.mult), [psb_.b, sgs_.b], [gst.b])
        P.dma("sp", lambda e: e.dma_start(out=cso_o.t.ap()[:, 29, :], in_=gst.t[:]), [gst.b], [cso_o.b])
        FREE(wu, gluH)
        pc = psr.next()

        def trc(e):
            for j in range(4):
                ins = e.transpose(out=pc.t[0:30, j * 128:(j + 1) * 128], in_=gluL.t[:, j, :], identity=identf.t[:])
            return ins
        P.op("pe", trc, [gluL.b, identf.b], [pc.b])
        cvt = SB("cvt", [30, 512])
        P.op("act", lambda e: e.copy(out=cvt.t[:], in_=pc.t[0:30, :]), [pc.b], [cvt.b])
        P.dma("sp", lambda e: e.dma_start(out=cvo_o.t.ap(), in_=cvt.t[:]), [cvt.b], [cvo_o.b])
        st_tm = SB("st_tm", [30, NS, 512]); stT = SB("stT", [128, 4, NS, 31]); stw = SB("stw", [128, 4, NS, 31])
        P.dma("sp", lambda e: e.dma_start(out=st_tm.t[:], in_=stc.t.ap().rearrange("i t c -> t i c")), [stc.b], [st_tm.b])
        for i in range(NS):
            P.dma("sp", lambda e, i=i: e.dma_start(out=cso_o.t.ap()[i, 0:29, :], in_=st_tm.t[1:30, i, :]), [st_tm.b], [cso_o.b])
            pc = psr.next()

            def trs(e, pc=pc, i=i):
                for j in range(4):
                    ins = e.transpose(out=pc.t[:, j * 32:j * 32 + 30], in_=st_tm.t[0:30, i, j * 128:(j + 1) * 128], identity=identf.t[0:30, 0:30])
                return ins
            P.op("pe", trs, [st_tm.b, identf.b], [pc.b])
            pcv = pc.t[:, 0:128].rearrange("p (j t) -> p j t", j=4)
            P.op("act", lambda e, pcv=pcv, i=i: e.copy(out=stT.t[:, :, i, 0:30], in_=pcv[:, :, 0:30]), [pc.b], [stT.b])
        P.op("dve", lambda e: e.tensor_copy(out=stT.t[:, :, :, 30], in_=gluS.t[:]), [gluS.b, stT.b], [stT.b])
        P.op("dve", lambda e: e.tensor_tensor(out=stw.t[:], in0=stT.t[:], in1=wdwT.t[:].unsqueeze(2).to_broadcast([128, 4, NS, 31]), op=ALU.mult),
             [stT.b, wdwT.b], [stw.b])
        P.op("dve", lambda e: e.reduce_sum(out=convS.t[:], in_=stw.t[:], axis=AX.X), [stw.b], [convS.b])
        P.op("dve", lambda e: e.tensor_tensor(out=convS.t[:], in0=convS.t[:], in1=bdwT.t[:].unsqueeze(2).to_broadcast([128, 4, NS]), op=ALU.add),
             [convS.b, bdwT.b], [convS.b])
        sq_t = SB("sq_t", [128, 4, 512]); ln_a = SB("ln_a", [128, 512]); ln_b = SB("ln_b", [128, 512]); ln_c = SB("ln_c", [128, 512])
        convT = SB("convT", [128, 4, 512])

        def ln_swish(src, cols, N, dst, dcols):
            pm = psr.next(); pq = psr.next()
            P.op("act", lambda e: e.activation(out=sq_t.t[:, :, 0:N], in_=src.t[:, :, cols], func=AF.Square), [src.b], [sq_t.b])

            def st_(e):
                for j in range(4):
                    e.matmul(pm.t[:, 0:N], lhsT=onesf.t[:], rhs=src.t[:, j, cols], start=(j == 0), stop=(j == 3))
                for j in range(4):
                    ins = e.matmul(pq.t[:, 0:N], lhsT=onesf.t[:], rhs=sq_t.t[:, j, 0:N], start=(j == 0), stop=(j == 3))
                return ins
            P.op("pe", st_, [src.b, sq_t.b, onesf.b], [pm.b, pq.b])
            P.op("act", lambda e: e.activation(out=ln_a.t[:, 0:N], in_=pm.t[:, 0:N], func=AF.Copy, scale=1.0 / 512), [pm.b], [ln_a.b])
            P.op("dve", lambda e: e.tensor_tensor(out=ln_b.t[:, 0:N], in0=ln_a.t[:, 0:N], in1=ln_a.t[:, 0:N], op=ALU.mult), [ln_a.b], [ln_b.b])
            P.op("dve", lambda e: e.scalar_tensor_tensor(out=ln_b.t[:, 0:N], in0=pq.t[:, 0:N], scalar=1.0 / 512, in1=ln_b.t[:, 0:N], op0=ALU.mult, op1=ALU.subtract),
                 [pq.b, ln_b.b], [ln_b.b])
            P.op("act", lambda e: e.activation(out=ln_b.t[:, 0:N], in_=ln_b.t[:, 0:N], func=AF.Sqrt, bias=EPS, scale=1.0), [ln_b.b], [ln_b.b])
            P.op("dve", lambda e: e.reciprocal(out=ln_b.t[:, 0:N], in_=ln_b.t[:, 0:N]), [ln_b.b], [ln_b.b])
            for j in range(4):
                P.op("dve", lambda e, j=j: e.tensor_tensor(out=ln_c.t[:, 0:N], in0=src.t[:, j, cols], in1=ln_a.t[:, 0:N], op=ALU.subtract), [src.b, ln_a.b], [ln_c.b])
                P.op("dve", lambda e, j=j: e.tensor_tensor(out=ln_c.t[:, 0:N], in0=ln_c.t[:, 0:N], in1=ln_b.t[:, 0:N], op=ALU.mult), [ln_c.b, ln_b.b], [ln_c.b])
                P.op("dve", lambda e, j=j: e.tensor_scalar(out=ln_c.t[:, 0:N], in0=ln_c.t[:, 0:N], scalar1=lngT.t[:, j:j + 1], scalar2=lnbT.t[:, j:j + 1],
                                                           op0=ALU.mult, op1=ALU.add), [ln_c.b, lngT.b, lnbT.b], [ln_c.b])
                P.op("act", lambda e, j=j: e.activation(out=dst.t[:, j, dcols], in_=ln_c.t[:, 0:N], func=AF.Silu), [ln_c.b], [dst.b])
        for j in range(4):
            P.op("dve", lambda e, j=j: e.tensor_tensor(out=Dg.t[:, j], in0=identb.t[:].unsqueeze(1).to_broadcast([128, 31, 128]),
                                                       in1=wdwT.t[:, j, :].unsqueeze(2).to_broadcast([128, 31, 128]), op=ALU.mult),
                 [identb.b, wdwT.b], [Dg.b])
        for st in range(4):
            for j in range(4):
                base = st * 512
                pcv_ = psr.next()

                def cmm(e, pcv_=pcv_, j=j, base=base):
                    for tap in range(31):
                        ins = e.matmul(pcv_.t[:, :], lhsT=Dg.t[:, j, tap, :], rhs=gluT.t[:, j, base + tap:base + tap + 512], start=(tap == 0), stop=(tap == 30))
                    return ins
                P.op("pe", cmm, [Dg.b, gluT.b], [pcv_.b])
                P.op("act", lambda e, pcv_=pcv_, j=j: e.activation(out=convT.t[:, j, :], in_=pcv_.t[:, :], func=AF.Identity, bias=bdwT.t[:, j:j + 1], scale=1.0),
                     [pcv_.b, bdwT.b], [convT.b])
            ln_swish(convT, slice(0, 512), 512, bT, slice(st * 512, (st + 1) * 512))
        ln_swish(convS, slice(0, NS), NS, bsT, slice(0, NS))
        dbg_out("bT", bT, bT.t[:, 0, 0:256], [128, 256], BF16)
        FREE(gluT, gluL, Dg, gluS, convS, cvt, st_tm, stT, stw, gst, sq_t, ln_a, ln_b, ln_c, convT)

        phase_end(4)
        wab = SB("wab", [128, 4, 1024], BF16); wbb = SB("wbb", [128, 4, 1024], BF16); wmb = SB("wmb", [128, 4, 1024], BF16)
        for dst, src in ((wab, w_a), (wbb, w_b), (wmb, w_m)):
            load_w(src, 0, 4, 0, 512, dst, 0); load_w(src, 0, 4, 512, 512, dst, 512)
        MT = SB("MT", [128, 4, CH], BF16); MsT = SB("MsT", [128, 4, NS], BF16)
        qmT = SB("qmT", [128, 4, CH], BF16); qmTs = SB("qmTs", [128, 4, NS], BF16)
        wqm = load_w(w_in, 0, 8, 3584, 512)
        mg_ = [qk_gen(hTm, slice(t * 128, (t + 1) * 128), wqm, 0, 128, 4, qnm, None, 0, None, 0, 0, qmT, t * 128) for t in range(16)]
        mg_.append(qk_gen(hTs, slice(0, NS), wqm, 0, NS, 4, qnm, None, 0, None, 0, 0, qmTs, 0))
        run_pipe(mg_)
        Pm_r = Rot([SB(f"Pm{i}", [128, 2, 512], BF16) for i in range(2)])
        rd_r = Rot([SB(f"rd{i}", [128, 512]) for i in range(2)])

        def mem_attn(keyT_fn, val_fn, qT, cols, N, dst, h, dcols):
            Pm = Pm_r.next()
            pss = [psr.next(), psr.next()]
            for mb in range(2):
                P.op("pe", lambda e, mb=mb: e.matmul(pss[mb].t[:, 0:N], lhsT=keyT_fn(mb)[1], rhs=qT.t[:, h, cols], start=True, stop=True),
                     [keyT_fn(mb)[0].b, qT.b], [pss[mb].b])
                P.op("act", lambda e, mb=mb: e.activation(out=Pm.t[:, mb, 0:N], in_=pss[mb].t[:, 0:N], func=AF.Exp, scale=SCALE), [pss[mb].b], [Pm.b])
            po = psr.next(); pd = psr.next()

            def f(e):
                for mb in range(2):
                    e.matmul(po.t[:, 0:N], lhsT=val_fn(mb)[1], rhs=Pm.t[:, mb, 0:N], start=(mb == 0), stop=(mb == 1))
                for mb in range(2):
                    ins = e.matmul(pd.t[:, 0:N], lhsT=onesb.t[:], rhs=Pm.t[:, mb, 0:N], start=(mb == 0), stop=(mb == 1))
                return ins
            P.op("pe", f, [Pm.b, val_fn(0)[0].b, val_fn(1)[0].b, onesb.b], [po.b, pd.b])
            rd = rd_r.next()
            P.op("dve", lambda e: e.reciprocal(out=rd.t[:, 0:N], in_=pd.t[:, 0:N]), [pd.b], [rd.b])
            P.op("dve", lambda e: e.tensor_tensor(out=dst.t[:, h, dcols], in0=po.t[:, 0:N], in1=rd.t[:, 0:N], op=ALU.mult), [po.b, rd.b], [dst.b])
        for st in range(4):
            for h in range(4):
                cols = slice(st * 512, (st + 1) * 512)
                mem_attn(lambda mb, h=h: (mkT, mkT.t[:, h, mb * 128:(mb + 1) * 128]),
                         lambda mb, h=h: (mvb, mvb.t[:, mb, h * 128:(h + 1) * 128]), qmT, cols, 512, MT, h, cols)
        mks_r = Rot([SB(f"mks{i}", [128, 2, 512], BF16) for i in range(NS)])
        mvs_r = Rot([SB(f"mvs{i}", [128, 2, 512], BF16) for i in range(NS)])
        mksT_r = Rot([SB(f"mksT{i}", [128, 2, 128], BF16) for i in range(2)])
        for i in range(NS):
            mks = mks_r.next(); mvs = mvs_r.next()
            P.dma("pool", lambda e, mks=mks, i=i: e.dma_start(out=mks.t[:], in_=mkc.t.ap()[i].rearrange("(mb p) c -> p mb c", p=128)), [mkc.b], [mks.b])
            P.dma("pool", lambda e, mvs=mvs, i=i: e.dma_start(out=mvs.t[:], in_=mvc.t.ap()[i].rearrange("(mb p) c -> p mb c", p=128)), [mvc.b], [mvs.b])
            for h in range(4):
                pt = psbr.next()

                def trm(e, pt=pt, mks=mks, h=h):
                    for mb in range(2):
                        ins = e.transpose(out=pt.t[:, mb * 128:(mb + 1) * 128], in_=mks.t[:, mb, h * 128:(h + 1) * 128], identity=identb.t[:])
                    return ins
                P.op("pe", trm, [mks.b, identb.b], [pt.b])
                mksT = mksT_r.next()
                P.op("act", lambda e, pt=pt, mksT=mksT: e.copy(out=mksT.t[:], in_=pt.t[:, 0:256].rearrange("p (m k) -> p m k", m=2)), [pt.b], [mksT.b])
                mem_attn(lambda mb, mksT=mksT: (mksT, mksT.t[:, mb, :]),
                         lambda mb, mvs=mvs, h=h: (mvs, mvs.t[:, mb, h * 128:(h + 1) * 128]), qmTs, slice(i, i + 1), 1, MsT, h, slice(i, i + 1))
        dbg_out("MT", MT, MT.t[:, 0, 0:256], [128, 256], BF16)
        FREE(qmT, qmTs, Pm_r, rd_r, mks_r, mvs_r, mksT_r, mkT, mvb)

        phase_end(5)
        mergedT = SB("mergedT", [128, 8, CH], BF16); mergedS = SB("mergedS", [128, 8, NS], BF16)
        mg_r = Rot([SB(f"mg{i}", [128, 512]) for i in range(2)])
        for fc in range(8):
            wg = H["wb"].next()
            for br in range(3):
                load_w(w_in, 0, 8, 4096 + br * 1024 + fc * 128, 128, wg, br * 128)
            for (hT, cols, N, XS, dst) in [(hTm, slice(st * 512, (st + 1) * 512), 512, (AT, bT, MT), mergedT) for st in range(4)] + \
                                          [(hTs, slice(0, NS), NS, (AsT, bsT, MsT), mergedS)]:
                mg = mg_r.next()
                for br in range(3):
                    pg = psr.next()
                    mm_fm(pg, N, wg, br * 128, hT, cols)
                    sg = sg_r.next()
                    P.op("act", lambda e, pg=pg, sg=sg, br=br, N=N, fc=fc: e.activation(out=sg.t[:, 0:N], in_=pg.t[:, 0:N], func=AF.Sigmoid,
                                                                                       bias=bgT.t[:, br * 8 + fc:br * 8 + fc + 1], scale=1.0),
                         [pg.b, bgT.b], [sg.b])
                    py = psr.next()
                    mm_fm(py, N, (wab, wbb, wmb)[br], fc * 128, XS[br], cols, nk=4)
                    if br == 0:
                        P.op("dve", lambda e, py=py, sg=sg, mg=mg, N=N: e.tensor_tensor(out=mg.t[:, 0:N], in0=py.t[:, 0:N], in1=sg.t[:, 0:N], op=ALU.mult),
                             [py.b, sg.b], [mg.b])
                    else:
                        P.op("dve", lambda e, py=py, sg=sg, N=N: e.tensor_tensor(out=sg.t[:, 0:N], in0=py.t[:, 0:N], in1=sg.t[:, 0:N], op=ALU.mult),
                             [py.b, sg.b], [sg.b])
                        if br == 1:
                            P.op("dve", lambda e, sg=sg, mg=mg, N=N: e.tensor_tensor(out=mg.t[:, 0:N], in0=mg.t[:, 0:N], in1=sg.t[:, 0:N], op=ALU.add),
                                 [sg.b, mg.b], [mg.b])
                        else:
                            P.op("dve", lambda e, sg=sg, mg=mg, N=N, dst=dst, cols=cols, fc=fc: e.tensor_tensor(out=dst.t[:, fc, cols], in0=mg.t[:, 0:N], in1=sg.t[:, 0:N], op=ALU.add),
                                 [sg.b, mg.b], [dst.b])
        FREE(hTm, hTs, AT, AsT, bT, bsT, MT, MsT, wab, wbb, wmb, mg_r, H["wb"], sg_r, kf_r, kb_r, cs_r, rA_r, rB_r, vf_r)

        phase_end(6)
        alloc_xbufs()
        wob = SB("wob", [128, 8, 1024], BF16)
        load_w(w_o, 0, 8, 0, 512, wob, 0); load_w(w_o, 0, 8, 512, 512, wob, 512)
        wpq = SB("wpq", [128, 8, 2048], BF16)
        for c in range(4):
            load_w(w_pq, 0, 8, c * 512, 512, wpq, c * 512)
        x1_r = Rot([SB(f"x1_{i}", [128, 1024]) for i in range(2)])
        h2T = SB("h2T", [128, 8, CH + NS], BF16)
        for t in range(17):
            n = 128 if t < 16 else NS
            mT, cols, src_d, r0 = (mergedT, slice(t * 128, (t + 1) * 128), xm, t * 128) if t < 16 else (mergedS, slice(0, NS), xs, 0)
            xt = H["xt"].next()
            P.dma("sp", lambda e, xt=xt, src_d=src_d, r0=r0, n=n: e.dma_start(out=xt.t[:n], in_=src_d.t.ap()[r0:r0 + n, :]), [src_d.b], [xt.b])
            x1 = x1_r.next()
            for half in range(2):
                ps = psr.next()
                mm_tm(ps, n, mT, cols, wob, half * 512, 512)
                P.op("dve", lambda e, ps=ps, xt=xt, x1=x1, half=half, n=n: e.tensor_tensor(out=x1.t[:n, half * 512:(half + 1) * 512], in0=ps.t[:n, :],
                                                                                         in1=xt.t[:n, half * 512:(half + 1) * 512], op=ALU.add),
                     [ps.b, xt.b], [x1.b])
            P.dma("sp", lambda e, x1=x1, t=t, n=n: e.dma_start(out=x1s.t.ap()[t * 128:t * 128 + n, :], in_=x1.t[:n]), [x1.b], [x1s.b])
            if t < 16:
                norm_T(x1, 128, g2T, h2T, t * 128)
            else:
                norm_T(x1, NS, g2T, h2T, CH)
        dbg_out("h2T", h2T, h2T.t[:, 0, 0:256], [128, 256], BF16)
        FREE(mergedT, mergedS, wob)

        phase_end(7)

        phase_end(8)
        NT = CH + NS
        selTall = SB("selTall", [128, 3, NT])
        qpT = SB("qpT", [128, 16, 256], BF16)

        class RS:
            pass
        rsets = []
        for k_ in range(2):
            S = RS()
            S.sc = SB(f"sc{k_}", [128, 16, 128]); S.m16 = SB(f"m16{k_}", [128, 16, 16]); S.i16 = SB(f"i16{k_}", [128, 16, 16], U32)
            S.i16f = SB(f"i16f{k_}", [128, 16, 16]); S.cand = SB(f"cand{k_}", [128, 8, 16, 16]); S.cm = SB(f"cm{k_}", [128, 8, 16])
            S.ci = SB(f"ci{k_}", [128, 8, 16], U32); S.cia = SB(f"cia{k_}", [128, 8, 16], U32); S.cib = SB(f"cib{k_}", [128, 8, 16], U32)
            S.ciaf = SB(f"ciaf{k_}", [128, 8, 16]); S.cibf = SB(f"cibf{k_}", [128, 8, 16]); S.eq = SB(f"eq{k_}", [128, 8, 16, 16])
            S.sel = SB(f"sel{k_}", [128, 3, 128]); S.gsum = SB(f"gsum{k_}", [128, 8])
            rsets.append(S)

        def route_tile(S, t0, n, g0):
            sc, m16, i16, i16f, cand, cm, ci = S.sc, S.m16, S.i16, S.i16f, S.cand, S.cm, S.ci
            cia, cib, ciaf, cibf, eq, sel, gsum = S.cia, S.cib, S.ciaf, S.cibf, S.eq, S.sel, S.gsum
            for q4 in range(4):
                ps = psr.next()

                def smm(e, ps=ps, q4=q4):
                    for c in range(4):
                        c16 = q4 * 4 + c
                        ins = e.matmul(ps.t[:n, c * 128:(c + 1) * 128], lhsT=qpT.t[:, c16, t0:t0 + n], rhs=skT.t[:, c16 % 2, :], start=True, stop=True)
                    return ins
                P.op("pe", smm, [qpT.b, skT.b], [ps.b])
                P.op("act", lambda e, ps=ps, q4=q4: e.copy(out=sc.t[:n, q4 * 4:(q4 + 1) * 4, :], in_=ps.t[:n, :].rearrange("p (c k) -> p c k", c=4)),
                     [ps.b], [sc.b])
            yield

            def tk_a(e):
                for c16 in range(16):
                    ins = e.max(out=m16.t[:n, c16, 0:8], in_=sc.t[:n, c16, :])
                return ins
            P.op("dve", tk_a, [sc.b], [m16.b])
            yield

            def tk_b(e):
                for c16 in range(16):
                    ins = e.max_index(out=i16.t[:n, c16, 0:8], in_max=m16.t[:n, c16, 0:8], in_values=sc.t[:n, c16, :])
                return ins
            P.op("dve", tk_b, [sc.b, m16.b], [i16.b])
            yield

            def tk_b2(e):
                for c16 in range(16):
                    ins = e.match_replace(out=sc.t[:n, c16, :], in_to_replace=m16.t[:n, c16, 0:8], in_values=sc.t[:n, c16, :], imm_value=-1e30)
                return ins
            P.op("dve", tk_b2, [sc.b, m16.b], [sc.b])
            yield

            def tk_c(e):
                for c16 in range(16):
                    ins = e.max(out=m16.t[:n, c16, 8:16], in_=sc.t[:n, c16, :])
                return ins
            P.op("dve", tk_c, [sc.b, m16.b], [m16.b])
            yield

            def tk_d(e):
                for c16 in range(16):
                    ins = e.max_index(out=i16.t[:n, c16, 8:16], in_max=m16.t[:n, c16, 8:16], in_values=sc.t[:n, c16, :])
                return ins
            P.op("dve", tk_d, [sc.b, m16.b, i16.b], [i16.b])
            yield
            m16v = m16.t[:n].rearrange("p (h c) k -> p h c k", c=2)
            i16fv = i16f.t[:n].rearrange("p (h c) k -> p h c k", c=2)

            def cf_a(e):
                e.tensor_copy(out=i16f.t[:n], in_=i16.t[:n])
                return e.tensor_tensor(out=cand.t[:n], in0=m16v[:, :, 0, :].unsqueeze(3).to_broadcast([n, 8, 16, 16]),
                                       in1=m16v[:, :, 1, :].unsqueeze(2).to_broadcast([n, 8, 16, 16]), op=ALU.add)
            P.op("dve", cf_a, [m16.b, i16.b], [i16f.b, cand.b])
            yield
            cvs = [cand.t[:n, h].rearrange("p a b -> p (a b)") for h in range(8)]

            def cf_b(e):
                for h in range(8):
                    ins = e.max(out=cm.t[:n, h, 0:8], in_=cvs[h])
                return ins
            P.op("dve", cf_b, [cand.b], [cm.b])
            yield

            def cf_c(e):
                for h in range(8):
                    ins = e.max_index(out=ci.t[:n, h, 0:8], in_max=cm.t[:n, h, 0:8], in_values=cvs[h])
                return ins
            P.op("dve", cf_c, [cand.b, cm.b], [ci.b])
            yield

            def cf_c2(e):
                for h in range(8):
                    ins = e.match_replace(out=cvs[h], in_to_replace=cm.t[:n, h, 0:8], in_values=cvs[h], imm_value=-1e30)
                return ins
            P.op("dve", cf_c2, [cand.b, cm.b], [cand.b])
            yield

            def cf_d(e):
                for h in range(8):
                    ins = e.max(out=cm.t[:n, h, 8:16], in_=cvs[h])
                return ins
            P.op("dve", cf_d, [cand.b, cm.b], [cm.b])
            yield

            def cf_e(e):
                for h in range(8):
                    ins = e.max_index(out=ci.t[:n, h, 8:16], in_max=cm.t[:n, h, 8:16], in_values=cvs[h])
                return ins
            P.op("dve", cf_e, [cand.b, cm.b, ci.b], [ci.b])
            yield

            def cf_f(e):
                e.tensor_single_scalar(out=cia.t[:n], in_=ci.t[:n], scalar=4, op=ALU.logical_shift_right)
                return e.tensor_single_scalar(out=cib.t[:n], in_=ci.t[:n], scalar=15, op=ALU.bitwise_and)
            P.op("dve", cf_f, [ci.b], [cia.b, cib.b])
            yield

            def cf_g(e):
                e.tensor_copy(out=ciaf.t[:n], in_=cia.t[:n])
                return e.tensor_copy(out=cibf.t[:n], in_=cib.t[:n])
            P.op("dve", cf_g, [cia.b, cib.b], [ciaf.b, cibf.b])
            yield
            io16 = iota.t[:n, 0:16].unsqueeze(1).unsqueeze(1).to_broadcast([n, 8, 16, 16])
            for w, cf in ((0, ciaf), (1, cibf)):
                P.op("dve", lambda e, cf=cf: e.tensor_tensor(out=eq.t[:n], in0=cf.t[:n].unsqueeze(3).to_broadcast([n, 8, 16, 16]), in1=io16, op=ALU.is_equal),
                     [cf.b, iota.b], [eq.b])
                yield
                P.op("dve", lambda e, w=w: e.tensor_tensor(out=eq.t[:n], in0=eq.t[:n], in1=i16fv[:, :, w, :].unsqueeze(2).to_broadcast([n, 8, 16, 16]), op=ALU.mult),
                     [eq.b, i16f.b], [eq.b])
                yield
                P.op("dve", lambda e, w=w: e.reduce_sum(out=sel.t[:n, w, :].rearrange("p (h k) -> p h k", h=8), in_=eq.t[:n], axis=AX.X),
                     [eq.b], [sel.b])
                yield
            gv = sel.t[:n, 2, :].rearrange("p (h k) -> p h k", h=8)
            P.op("dve", lambda e: e.tensor_tensor(out=gv, in0=cm.t[:n], in1=cm.t[:n, :, 0:1].to_broadcast([n, 8, 16]), op=ALU.subtract),
                 [cm.b, sel.b], [sel.b])
            yield
            P.op("act", lambda e: e.activation(out=gv, in_=gv, func=AF.Exp), [sel.b], [sel.b])
            yield
            P.op("dve", lambda e: e.reduce_sum(out=gsum.t[:n], in_=gv, axis=AX.X), [sel.b], [gsum.b])
            yield
            P.op("dve", lambda e: e.reciprocal(out=gsum.t[:n], in_=gsum.t[:n]), [gsum.b], [gsum.b])
            yield
            P.op("dve", lambda e: e.tensor_tensor(out=gv, in0=gv, in1=gsum.t[:n].unsqueeze(2).to_broadcast([n, 8, 16]), op=ALU.mult),
                 [sel.b, gsum.b], [sel.b])
            yield
            pT = psr.next()

            def trsel(e):
                for w in range(3):
                    ins = e.transpose(out=pT.t[:, w * 128:w * 128 + n], in_=sel.t[:n, w, :], identity=identf.t[:n, :n])
                return ins
            P.op("pe", trsel, [sel.b, identf.b], [pT.b])
            P.op("act", lambda e: e.copy(out=selTall.t[:, :, g0 + t0:g0 + t0 + n],
                                         in_=pT.t[:, 0:384].rearrange("p (w t) -> p w t", w=3)[:, :, 0:n]), [pT.b], [selTall.b])

        def route_block(hT, c0, ntok, tiles, g0):
            cols = slice(c0, c0 + ntok)
            for c16 in range(16):
                ps = psr.next()
                mm_fm(ps, ntok, wpq, c16 * 128, hT, cols)
                if c16 % 2 == 0:
                    P.op("act", lambda e, ps=ps, c16=c16: e.copy(out=qpT.t[:, c16, 0:ntok], in_=ps.t[:, 0:ntok]), [ps.b], [qpT.b])
                else:
                    P.op("dve", lambda e, ps=ps, c16=c16: e.tensor_copy(out=qpT.t[:, c16, 0:ntok], in_=ps.t[:, 0:ntok]), [ps.b], [qpT.b])
            run_pipe([route_tile(rsets[k_], t0, n, g0) for k_, (t0, n) in enumerate(tiles)], depth=2)

        for sp_ in range(8):
            route_block(h2T, sp_ * 256, 256, [(0, 128), (128, 128)], sp_ * 256)
        route_block(h2T, CH, NS, [(0, NS)], CH)
        dbg_out("selT", selTall, selTall.t[:, :, 0:128], [128, 3, 128])
        FREE(wpq, qpT)
        for S in rsets:
            FREE(S.sc, S.m16, S.i16, S.i16f, S.cand, S.cm, S.ci, S.cia, S.cib, S.ciaf, S.cibf, S.eq, S.sel, S.gsum)

        phase_end(9)
        ut_r = Rot([SB(f"utx{i}", [128, 1024], BF16) for i in range(6)])
        TwH = [SB("TwA", [128, 256 + NS, 64], BF16), SB("TwB", [128, 256 + NS, 64], BF16)]
        selTb = SB("selTb", [128, 3, NT], BF16)
        P.op("dve", lambda e: e.tensor_copy(out=selTb.t[:], in_=selTall.t[:]), [selTall.b], [selTb.b])
        iotab = SB("iotab", [128, 128], BF16)
        P.op("dve", lambda e: e.tensor_copy(out=iotab.t[:], in_=iota.t[:]), [iota.b], [iotab.b])
        FREE(selTall)
        NOH = 8
        ohJ_r = Rot([SB(f"ohJ{i}", [128, 4, 128], BF16) for i in range(NOH)])
        ohI_r = Rot([SB(f"ohI{i}", [128, 4, 64], BF16) for i in range(NOH)])
        vt_r = Rot([SB(f"vt{i}", [128, 1024], BF16) for i in range(6)])
        ge_r = Rot([SB(f"ge{i}", [128, 256 + NS]) for i in range(3)])
        hg_r = Rot([SB(f"hg{i}", [128, 256 + NS], BF16) for i in range(3)])

        class FV:
            def __init__(self, tn):
                self.b = tn.b
                self.v = tn.t[:].bitcast(F32)

        def oap(o, n):
            return o.v[:n, :] if isinstance(o, FV) else o.t[:n, :]
        fvb = [FV(PSB[0]), FV(PSB[1])]
        outs_all = [(PS[0], PS[1]), (PS[2], PS[3]), (fvb[0], fvb[1])]
        psWf_r = Rot(fvb)
        psA_r = Rot([PS[4], PS[5]])
        BDELAY = 3

        def build_gen(blk, half):
            hT, c0, ntok, tiles, g0, dests = blk
            Tw = TwH[half]
            pendq = []

            def emit_pe(ohJ, ohI, tb, nb_):
                psW = psWf_r.next()
                pwv = psW.v

                def wmm(e):
                    for u in range(nb_):
                        ins = e.matmul(pwv[:, u * 64:(u + 1) * 64], lhsT=ohJ.t[:, u, :], rhs=ohI.t[:, u, :], start=True, stop=True)
                    return ins
                P.op("pe", wmm, [ohJ.b, ohI.b], [psW.b])
                P.op("act", lambda e: e.copy(out=Tw.t[:, tb:tb + nb_, :], in_=pwv[:, 0:nb_ * 64].rearrange("p (u i) -> p u i", u=nb_)), [psW.b], [Tw.b])
            for tb in range(0, ntok, 4):
                nb_ = min(4, ntok - tb)
                ohJ = ohJ_r.next(); ohI = ohI_r.next()
                tk0 = g0 + tb
                P.op("dve", lambda e, ohI=ohI, tk0=tk0, nb_=nb_: e.tensor_tensor(
                    out=ohI.t[:, 0:nb_, :], in0=iotab.t[:, half * 64:(half + 1) * 64].unsqueeze(1).to_broadcast([128, nb_, 64]),
                    in1=selTb.t[:, 0, tk0:tk0 + nb_].unsqueeze(2).to_broadcast([128, nb_, 64]), op=ALU.is_equal),
                    [iotab.b, selTb.b], [ohI.b])
                P.op("dve", lambda e, ohJ=ohJ, tk0=tk0, nb_=nb_: e.tensor_tensor(
                    out=ohJ.t[:, 0:nb_, :], in0=iotab.t[:].unsqueeze(1).to_broadcast([128, nb_, 128]),
                    in1=selTb.t[:, 1, tk0:tk0 + nb_].unsqueeze(2).to_broadcast([128, nb_, 128]), op=ALU.is_equal),
                    [iotab.b, selTb.b], [ohJ.b])
                P.op("dve", lambda e, ohI=ohI, tk0=tk0, nb_=nb_: e.tensor_tensor(
                    out=ohI.t[:, 0:nb_, :], in0=ohI.t[:, 0:nb_, :], in1=selTb.t[:, 2, tk0:tk0 + nb_].unsqueeze(2).to_broadcast([128, nb_, 64]), op=ALU.mult),
                    [ohI.b, selTb.b], [ohI.b])
                pendq.append((ohJ, ohI, tb, nb_))
                if len(pendq) > BDELAY:
                    emit_pe(*pendq.pop(0))
                yield
            while pendq:
                emit_pe(*pendq.pop(0))
            yield

        def loop_gen(blk):
            hT, c0, ntok, tiles, g0, dests = blk
            cols = slice(c0, c0 + ntok)
            outs = outs_all
            pend = None

            def issue_omm(i, hg, vt):
                def omm(e, hg=hg, vt=vt, i=i):
                    for ti, (t0, n) in enumerate(tiles):
                        for half in range(2):
                            ins = e.matmul(oap(outs[ti][half], n), lhsT=hg.t[:, t0:t0 + n], rhs=vt.t[:, half * 512:(half + 1) * 512],
                                           start=(i == 0), stop=(i == 127))
                    return ins
                P.op("pe", omm, [hg.b, vt.b], [outs[ti][half].b for ti in range(len(tiles)) for half in range(2)])
            for i in range(128):
                ut = ut_r.next(); vt = vt_r.next()
                P.dma("sp", lambda e, ut=ut, i=i: e.dma_start(out=ut.t[:], in_=uTs.t.ap()[i]), [uTs.b], [ut.b])
                P.dma("sp", lambda e, vt=vt, i=i: e.dma_start(out=vt.t[:], in_=vbs.t.ap()[i]), [vbs.b], [vt.b])
                psA = psA_r.next()

                def amm(e, ut=ut, psA=psA):
                    for kc in range(8):
                        ins = e.matmul(psA.t[:, 0:ntok], lhsT=ut.t[:, kc * 128:(kc + 1) * 128], rhs=hT.t[:, kc, cols], start=(kc == 0), stop=(kc == 7))
                    return ins
                P.op("pe", amm, [ut.b, hT.b], [psA.b])
                if pend is not None:
                    issue_omm(*pend)
                ge = ge_r.next(); hg = hg_r.next()
                Tw = TwH[i // 64]
                P.op("act", lambda e, ge=ge, psA=psA: e.activation(out=ge.t[:, 0:ntok], in_=psA.t[:, 0:ntok], func=AF.Gelu_apprx_tanh), [psA.b], [ge.b])
                P.op("dve", lambda e, ge=ge, hg=hg, i=i, Tw=Tw: e.tensor_tensor(out=hg.t[:, 0:ntok], in0=ge.t[:, 0:ntok], in1=Tw.t[:, 0:ntok, i % 64], op=ALU.mult),
                     [ge.b, Tw.b], [hg.b])
                pend = (i, hg, vt)
                yield
            issue_omm(*pend)
            for ti, (t0, n) in enumerate(tiles):
                xt = H["xt"].next(); x1 = x1_r.next()
                P.dma("sp", lambda e, xt=xt, t0=t0, n=n: e.dma_start(out=xt.t[:n], in_=x1s.t.ap()[g0 + t0:g0 + t0 + n, :]), [x1s.b], [xt.b])
                for half in range(2):
                    P.op("dve", lambda e, xt=xt, x1=x1, ti=ti, half=half, n=n: e.tensor_tensor(out=x1.t[:n, half * 512:(half + 1) * 512], in0=oap(outs[ti][half], n),
                                                                                             in1=xt.t[:n, half * 512:(half + 1) * 512], op=ALU.add),
                         [outs[ti][half].b, xt.b], [x1.b])
                out_d, orow = dests[ti]
                P.dma("sp", lambda e, x1=x1, n=n, out_d=out_d, orow=orow: e.dma_start(out=out_d.t.ap()[orow:orow + n, :], in_=x1.t[:n]), [x1.b], [out_d.b])
            yield

        def drain(g_):
            if g_ is None:
                return
            for _ in g_:
                pass

        def step(g_):
            if g_ is None:
                return
            try:
                next(g_)
            except StopIteration:
                pass
        blocks = [(h2T, sp_ * 256, 256, [(0, 128), (128, 128)], sp_ * 256, [(y_o, sp_ * 256), (y_o, sp_ * 256 + 128)]) for sp_ in range(7)]
        blocks.append((h2T, 7 * 256, 256 + NS, [(0, 128), (128, 128), (256, NS)], 7 * 256, [(y_o, 7 * 256), (y_o, 7 * 256 + 128), (ys_o, 0)]))
        drain(build_gen(blocks[0], 0))
        for bi, blk in enumerate(blocks):
            last = (bi == len(blocks) - 1)
            bB = build_gen(blk, 1)
            if last:
                drain(bB)
                bB = None
            lg = loop_gen(blk)
            bA = build_gen(blocks[bi + 1], 0) if not last else None
            for i in range(128):
                if i == 64:
                    drain(bB)
                next(lg)
                step(bB if i < 64 else bA)
            drain(lg)
            drain(bA)


    try:
        body()
    except _Stop:
        pass
    P.finish()
    return nc, list(dram_out.keys())


_CACHE = {}


def _consts():
    half = 16
    inv = (np.float32(500000.0) ** (-np.arange(half, dtype=np.float32) / np.float32(half))).astype(np.float32)

    def cs(pos):
        ang = (pos.astype(np.float32)[:, None] * inv[None, :]).astype(np.float32)
        c = np.cos(ang).astype(np.float32); s = np.sin(ang).astype(np.float32)
        return np.ascontiguousarray(np.concatenate([c, c, s, s], axis=1))
    ki = np.arange(128)[:, None]; qi = np.arange(128)[None, :]
    return dict(cs=cs, mprev=(ki >= qi).astype(np.float32), mcur=(ki <= qi).astype(np.float32),
                ident=np.eye(128, dtype=np.float32),
                iota=np.ascontiguousarray(np.broadcast_to(np.arange(128, dtype=np.float32), (128, 128))))


def _colT(v, n):
    return np.ascontiguousarray(np.asarray(v, np.float32).reshape(n, 128).T)


def make_in_maps(inp):
    C = _consts()
    f = lambda a: np.ascontiguousarray(np.asarray(a, dtype=np.float32))
    shared = dict(
        w_in=f(inp["w_in"][0]), w_mem=f(inp["w_mem_kv"][0]), w_a=f(inp["w_a_proj"][0]), w_b=f(inp["w_b_proj"][0]),
        w_m=f(inp["w_m_proj"][0]), w_o=f(inp["w_o"][0]), w_pq=f(inp["w_pq"][0]), u_tab=f(inp["u_tab"][0]), v_tab=f(inp["v_tab"][0]),
        skT=np.ascontiguousarray(np.asarray(inp["sub_keys"][0], np.float32).transpose(2, 0, 1)),
        g1T=_colT(inp["g_norm1"][0], 8), g2T=_colT(inp["g_norm2"][0], 8), gmT=_colT(inp["g_mem"][0], 8),
        bgT=_colT(inp["b_gate"][0], 24), bdwT=_colT(inp["b_dw"][0], 4), lngT=_colT(inp["ln_g"][0], 4), lnbT=_colT(inp["ln_b"][0], 4),
        wdwT=np.ascontiguousarray(np.asarray(inp["w_dw"][0], np.float32).T.reshape(4, 128, 31).transpose(1, 0, 2)),
        qna_bc=np.ascontiguousarray(np.broadcast_to(np.asarray(inp["qn_a"][0], np.float32), (128, 128))),
        kna_bc=np.ascontiguousarray(np.broadcast_to(np.asarray(inp["kn_a"][0], np.float32), (128, 128))),
        qnm_bc=np.ascontiguousarray(np.broadcast_to(np.asarray(inp["qn_m"][0], np.float32), (128, 128))),
        knm_bc=np.ascontiguousarray(np.broadcast_to(np.asarray(inp["kn_m"][0], np.float32), (128, 128))),
        ident=C["ident"], iota=C["iota"], mprev=C["mprev"], mcur=C["mcur"],
        css=C["cs"](np.full((NS,), 16384, dtype=np.int64)),
    )
    xp = np.asarray(inp["x_prompt"], np.float32)
    maps = []
    for c in range(NCORES):
        b, ch = c // 4, c % 4
        c0 = ch * CH
        m = dict(shared)
        m["xm"] = np.ascontiguousarray(xp[b, c0:c0 + CH])
        m["xh"] = np.ascontiguousarray(xp[b, c0 - CH:c0]) if ch > 0 else np.zeros((CH, 1024), np.float32)
        m["xs"] = np.ascontiguousarray(np.asarray(inp["x_sample"], np.float32)[c * NS:(c + 1) * NS, 0])
        m["memx"] = np.ascontiguousarray(np.asarray(inp["mem_prompt"], np.float32)[b])
        m["wkc"] = np.ascontiguousarray(np.asarray(inp["cache_win_k"], np.float32)[0, c * NS:(c + 1) * NS].reshape(NS, 2048, 512))
        m["wvc"] = np.ascontiguousarray(np.asarray(inp["cache_win_v"], np.float32)[0, c * NS:(c + 1) * NS].reshape(NS, 2048, 512))
        m["mkc"] = np.ascontiguousarray(np.asarray(inp["cache_mem_k"], np.float32)[0, c * NS:(c + 1) * NS].reshape(NS, 256, 512))
        m["mvc"] = np.ascontiguousarray(np.asarray(inp["cache_mem_v"], np.float32)[0, c * NS:(c + 1) * NS].reshape(NS, 256, 512))
        m["stc"] = np.ascontiguousarray(np.asarray(inp["state_conv"], np.float32)[0, c * NS:(c + 1) * NS])
        m["hb"] = np.full((128, 1), 1.0 if ch > 0 else 0.0, np.float32)
        m["csm"] = C["cs"](np.arange(c0, c0 + CH))
        m["csh"] = C["cs"](np.maximum(np.arange(c0 - CH, c0), 0))
        maps.append(m)
    return maps


def assemble(res):
    y = np.zeros((2, 8192, 1024), np.float32); ys = np.zeros((32, 1, 1024), np.float32)
    wk = np.zeros((1, 2, 2048, 4, 128), np.float32); wv = np.zeros_like(wk)
    mk = np.zeros((1, 2, 256, 4, 128), np.float32); mv = np.zeros_like(mk)
    cv = np.zeros((1, 2, 30, 512), np.float32)
    ks = np.zeros((1, 32, 1, 4, 128), np.float32); vs = np.zeros_like(ks); cs = np.zeros((1, 32, 30, 512), np.float32)
    for c in range(NCORES):
        r = res[c]
        b, ch = c // 4, c % 4
        y[b, ch * CH:(ch + 1) * CH] = r["y"]
        ys[c * NS:(c + 1) * NS, 0] = r["ys"]
        if ch == 3:
            wk[0, b] = r["ko"].reshape(2048, 4, 128); wv[0, b] = r["vo"].reshape(2048, 4, 128)
            cv[0, b] = r["cvo"]
        if ch == 0:
            mk[0, b] = r["mko"].reshape(256, 4, 128); mv[0, b] = r["mvo"].reshape(256, 4, 128)
        ks[0, c * NS:(c + 1) * NS, 0] = r["kso"].reshape(NS, 4, 128)
        vs[0, c * NS:(c + 1) * NS, 0] = r["vso"].reshape(NS, 4, 128)
        cs[0, c * NS:(c + 1) * NS] = r["cso"]
    return (y, ys, wk, wv, mk, mv, cv, ks, vs, cs)


def kernel(**inputs):
    if "nc" not in _CACHE:
        _CACHE["nc"] = build_program()[0]
    nc = _CACHE["nc"]
    maps = make_in_maps(inputs)
    res = run_bass_kernel_spmd(nc, maps, core_ids=list(range(NCORES)))
    return assemble(res.results)
```
